# Optimizing a Trainium2 kernel written in Bass

```python
import math
import jax, jax.numpy as jnp
from jax import lax
import numpy as np

D_MODEL = 1024
BATCH = 16
SEQ = 4096
DEPTH = 4

D_MIX = D_MODEL
ATT_HEADS = 8
ATT_HEAD_DIM = 64
ATT_WIDTH = ATT_HEADS * ATT_HEAD_DIM
KV_LATENT = 128
IDX_HEADS = 8
IDX_DIM = 32
TOPK_MAX = 256
Q_BLOCK = 128
SSM_WIDTH = D_MIX - ATT_WIDTH
GROUP_CH = 16
N_GROUPS = SSM_WIDTH // GROUP_CH
STATE = 64
DT_MIN = 0.001
DT_MAX = 0.1
LN_EPS = 1e-5
RMS_EPS = 1e-6
DEEPNORM_ALPHA = (2 * DEPTH) ** 0.25
DEEPNORM_BETA = (8 * DEPTH) ** -0.25
SPLITS = (ATT_WIDTH, KV_LATENT, IDX_HEADS * IDX_DIM, IDX_DIM, IDX_HEADS, ATT_WIDTH, SSM_WIDTH, SSM_WIDTH)
D_IN = sum(SPLITS)

kernel_name = 'hymba_dsa_s5_deepnorm_trunk'


def _split_points():
    pts, acc = [], 0
    for s in SPLITS[:-1]:
        acc += s
        pts.append(acc)
    return pts


def layer_norm(x, g, b):
    xf = x.astype(jnp.float32)
    mu = jnp.mean(xf, axis=-1, keepdims=True)
    var = jnp.mean(jnp.square(xf - mu), axis=-1, keepdims=True)
    y = (xf - mu) * lax.rsqrt(var + LN_EPS) * g.astype(jnp.float32) + b.astype(jnp.float32)
    return y.astype(x.dtype)


def rms_norm(x, g):
    xf = x.astype(jnp.float32)
    y = xf * lax.rsqrt(jnp.mean(jnp.square(xf), axis=-1, keepdims=True) + RMS_EPS)
    return y * g.astype(jnp.float32)


def dsa_attention(q, c_kv, q_idx, k_idx, w_idx, kv_g, w_uk, w_uv):
    bsz, seq, _ = q.shape
    n_sel = min(TOPK_MAX, seq // 4)
    n_blk = seq // Q_BLOCK
    c = rms_norm(c_kv, kv_g)
    qh = q.reshape(bsz, seq, ATT_HEADS, ATT_HEAD_DIM)
    q_lat = jnp.einsum('bshd,hcd->bshc', qh, w_uk).astype(jnp.float32) * (ATT_HEAD_DIM ** -0.5)
    qi = q_idx.reshape(bsz, seq, IDX_HEADS, IDX_DIM).astype(jnp.float32) * (IDX_DIM ** -0.5)
    ki = k_idx.astype(jnp.float32)
    wi = w_idx.astype(jnp.float32) * (IDX_HEADS ** -0.5)
    key_pos = jnp.arange(seq, dtype=jnp.int32)
    neg = jnp.finfo(jnp.float32).min

    def to_blocks(a):
        return jnp.moveaxis(a.reshape(bsz, n_blk, Q_BLOCK, *a.shape[2:]), 1, 0)

    def block(args):
        blk, ql_b, qi_b, wi_b = args
        q_pos = blk * Q_BLOCK + jnp.arange(Q_BLOCK, dtype=jnp.int32)
        causal = key_pos[None, :] <= q_pos[:, None]
        idx_logits = jnp.einsum('bthd,bsd->bths', qi_b, ki)
        index_score = jnp.einsum('bths,bth->bts', jax.nn.relu(idx_logits), wi_b)
        index_score = jnp.where(causal[None], index_score, neg)
        _, sel = lax.top_k(index_score, n_sel)
        valid = sel <= q_pos[None, :, None]
        c_sel = jax.vmap(lambda cb, ib: cb[ib])(c, sel)
        s = jnp.einsum('bthc,btkc->bthk', ql_b, c_sel)
        s = jnp.where(valid[:, :, None, :], s, neg)
        p = jax.nn.softmax(s, axis=-1)
        return jnp.einsum('bthk,btkc->bthc', p, c_sel)

    o_lat = lax.map(block, (jnp.arange(n_blk, dtype=jnp.int32), to_blocks(q_lat), to_blocks(qi), to_blocks(wi)))
    o_lat = jnp.moveaxis(o_lat, 0, 1).reshape(bsz, seq, ATT_HEADS, KV_LATENT)
    out = jnp.einsum('bshc,hcd->bshd', o_lat.astype(w_uv.dtype), w_uv)
    return out.reshape(bsz, seq, ATT_WIDTH).astype(q.dtype)


def _ssm_combine(e_i, e_j):
    a_i, b_i = e_i
    a_j, b_j = e_j
    return a_j * a_i, a_j * b_i + b_j


def s5_branch(u, log_dt, a_re, a_im, b_re, b_im, c_re, c_im, d_skip, w_glu, b_glu):
    bsz, seq, _ = u.shape
    f32 = jnp.float32
    u32 = u.astype(f32).reshape(bsz, seq, N_GROUPS, GROUP_CH)
    lam = lax.complex(a_re.astype(f32), a_im.astype(f32))
    dt = jnp.exp(log_dt.astype(f32))[:, None]
    a_bar = jnp.exp(lam * dt)
    b_bar = ((a_bar - 1.0) / lam)[..., None] * lax.complex(b_re.astype(f32), b_im.astype(f32))
    c_mat = lax.complex(c_re.astype(f32), c_im.astype(f32))
    bu = jnp.einsum('bsgc,gpc->bsgp', u32.astype(jnp.complex64), b_bar)
    a_seq = jnp.broadcast_to(a_bar[None, None], (1, seq, N_GROUPS, STATE))
    _, states = lax.associative_scan(_ssm_combine, (a_seq, bu), axis=1)
    y = jnp.einsum('bsgp,gcp->bsgc', states, c_mat).real + d_skip.astype(f32) * u32
    y = jax.nn.gelu(y.reshape(bsz, seq, SSM_WIDTH))
    h = jnp.einsum('bsc,ce->bse', y, w_glu.astype(f32)) + b_glu.astype(f32)
    val, gate = jnp.split(h, 2, axis=-1)
    return (val * jax.nn.sigmoid(gate)).astype(u.dtype)


def hybrid_layer(x, w_in, kv_g, w_uk, w_uv, log_dt, a_re, a_im, b_re, b_im, c_re, c_im, d_skip, w_glu, b_glu, w_out, ln_g, ln_b):
    proj = jnp.einsum('bsd,de->bse', x, w_in)
    q, c_kv, q_idx, k_idx, w_idx, gate_a, u, gate_s = jnp.split(proj, _split_points(), axis=-1)
    att = dsa_attention(q, c_kv, q_idx, k_idx, w_idx, kv_g, w_uk, w_uv)
    ssm = s5_branch(u, log_dt, a_re, a_im, b_re, b_im, c_re, c_im, d_skip, w_glu, b_glu)
    mixed = jnp.concatenate([att * jax.nn.silu(gate_a), ssm * jax.nn.silu(gate_s)], axis=-1)
    y = jnp.einsum('bse,ed->bsd', mixed, w_out)
    return layer_norm(DEEPNORM_ALPHA * x + y, ln_g, ln_b)


def setup_inputs(seed: int = 0) -> dict:
    key = jax.random.key(seed)
    ks = jax.random.split(key, 19)
    f = jnp.float32
    L = DEPTH
    nrm = jax.random.normal
    x = nrm(ks[0], (BATCH, SEQ, D_MODEL), f)
    w_in = nrm(ks[1], (L, D_MODEL, D_IN), f) * (D_MODEL ** -0.5)
    kv_norm_g = 1.0 + 0.1 * nrm(ks[2], (L, KV_LATENT), f)
    w_uk = nrm(ks[3], (L, ATT_HEADS, KV_LATENT, ATT_HEAD_DIM), f) * (KV_LATENT ** -0.5)
    w_uv = nrm(ks[4], (L, ATT_HEADS, KV_LATENT, ATT_HEAD_DIM), f) * (KV_LATENT ** -0.5) * DEEPNORM_BETA
    log_dt = jax.random.uniform(ks[5], (L, N_GROUPS), f, minval=math.log(DT_MIN), maxval=math.log(DT_MAX))
    n = jnp.arange(STATE, dtype=f)
    a_re = -0.5 + 0.01 * nrm(ks[6], (L, N_GROUPS, STATE), f)
    a_im = jnp.pi * n + 0.01 * nrm(ks[7], (L, N_GROUPS, STATE), f)
    b_re = nrm(ks[8], (L, N_GROUPS, STATE, GROUP_CH), f) * ((2 * GROUP_CH) ** -0.5)
    b_im = nrm(ks[9], (L, N_GROUPS, STATE, GROUP_CH), f) * ((2 * GROUP_CH) ** -0.5)
    c_re = nrm(ks[10], (L, N_GROUPS, GROUP_CH, STATE), f) * ((2 * STATE) ** -0.5)
    c_im = nrm(ks[11], (L, N_GROUPS, GROUP_CH, STATE), f) * ((2 * STATE) ** -0.5)
    d_skip = nrm(ks[12], (L, N_GROUPS, GROUP_CH), f)
    w_glu = nrm(ks[13], (L, SSM_WIDTH, 2 * SSM_WIDTH), f) * (SSM_WIDTH ** -0.5)
    b_glu = 0.01 * nrm(ks[14], (L, 2 * SSM_WIDTH), f)
    w_out = nrm(ks[15], (L, D_MIX, D_MODEL), f) * (D_MIX ** -0.5) * DEEPNORM_BETA
    ln_g = 1.0 + 0.1 * nrm(ks[16], (L, D_MODEL), f)
    ln_b = 0.01 * nrm(ks[17], (L, D_MODEL), f)
    return {'x': x, 'w_in': w_in, 'kv_norm_g': kv_norm_g, 'w_uk': w_uk, 'w_uv': w_uv,
            'log_dt': log_dt, 'a_re': a_re, 'a_im': a_im, 'b_re': b_re, 'b_im': b_im,
            'c_re': c_re, 'c_im': c_im, 'd_skip': d_skip, 'w_glu': w_glu, 'b_glu': b_glu,
            'w_out': w_out, 'ln_g': ln_g, 'ln_b': ln_b}


def reference(x, w_in, kv_norm_g, w_uk, w_uv, log_dt, a_re, a_im, b_re, b_im, c_re, c_im, d_skip, w_glu, b_glu, w_out, ln_g, ln_b):
    h = x
    for l in range(DEPTH):
        h = hybrid_layer(h, w_in[l], kv_norm_g[l], w_uk[l], w_uv[l], log_dt[l], a_re[l], a_im[l],
                         b_re[l], b_im[l], c_re[l], c_im[l], d_skip[l], w_glu[l], b_glu[l],
                         w_out[l], ln_g[l], ln_b[l])
    return h
```

```python
import math
import os
from contextlib import ExitStack

import numpy as np
import concourse.bass as bass
import concourse.mybir as mybir
from concourse.bass_utils import run_bass_kernel_spmd

F32 = mybir.dt.float32
BF16 = mybir.dt.bfloat16
F16 = mybir.dt.float16
I32 = mybir.dt.int32
U8 = mybir.dt.uint8
AF = mybir.ActivationFunctionType
ALU = mybir.AluOpType
AX = mybir.AxisListType

DEPTH = 4
NCORES = 8
NSEQ = 2
S = 4096
T = NSEQ * S
D = 1024
ALPHA = (2 * DEPTH) ** 0.25
LN_EPS = 1e-5
RMS_EPS = 1e-6
NEG = -60000.0
RB = 16.0
NIT = 15
PI = math.pi
DSZ = {F32: 4, BF16: 2, F16: 2, I32: 4, U8: 1}


def merge(*ds):
    out = {}
    for d in ds:
        if not d:
            continue
        for k, v in d.items():
            if out.get(k, 0) < v:
                out[k] = v
    return out


class Buf:
    def __init__(self):
        self.prev = {}
        self.ws = {}
        self.rs = {}

    def new(self):
        self.prev = merge(self.prev, self.rs, self.ws)
        self.ws = {}
        self.rs = {}


class Sched:
    ENG = ("pe", "act", "dve", "pool", "sp")

    def __init__(self, nc, es):
        self.nc = nc
        self.sem = {k: es.enter_context(nc.semaphore("s_" + k)) for k in self.ENG}
        self.cnt = {k: 0 for k in self.ENG}
        self.thunks = {k: [] for k in self.ENG}
        self.seen = {k: {} for k in self.ENG}
        self.pending = {k: {} for k in self.ENG}
        self.NDMA = 16
        for i in range(self.NDMA):
            self.sem["d%d" % i] = es.enter_context(nc.semaphore("s_d%d" % i))
        self.dcnt = [0] * self.NDMA
        self.dnext = 0
        self.bufs = {}
        self.ninst = 0

    def buf(self, name):
        b = self.bufs.get(name)
        if b is None:
            b = self.bufs[name] = Buf()
        return b

    def _emit(self, eng, fn, deps, own_inc):
        seen = self.seen[eng]
        deps = merge(deps, self.pending[eng])
        self.pending[eng] = {}
        waits = []
        for k, v in deps.items():
            if seen.get(k, 0) >= v:
                continue
            seen[k] = v
            waits.append((self.sem[k], v))
        sems = self.sem
        self.ninst += 1 + len(waits)

        def thunk(e, waits=waits, fn=fn, own_inc=own_inc):
            for s_, v_ in waits:
                e.wait_ge(s_, v_)
            ins = fn(e)
            ins.then_inc(own_inc[0], own_inc[1])

        self.thunks[eng].append(thunk)

    def op(self, eng, method, *args, R=(), W=(), N=(), deps=None, **kw):
        dr = merge(*[self.buf(n).ws for n in R])
        for n in N:
            self.buf(n).new()
        d = merge(deps, dr, *[self.buf(n).prev for n in W])
        self.cnt[eng] += 1
        tok = {eng: self.cnt[eng]}

        def fn(e, method=method, args=args, kw=kw):
            return getattr(e, method)(*args, **kw)

        self._emit(eng, fn, d, (self.sem[eng], 1))
        for n in R:
            b = self.buf(n)
            b.rs = merge(b.rs, tok)
        for n in W:
            b = self.buf(n)
            b.ws = merge(b.ws, tok)
        return tok

    def dma(self, out, in_, R=(), W=(), N=(), deps=None, q="sp"):
        dr = merge(*[self.buf(n).ws for n in R])
        for n in N:
            self.buf(n).new()
        s = self.dnext
        self.dnext = (s + 1) % self.NDMA
        key = "d%d" % s
        d = merge(deps, dr, *[self.buf(n).prev for n in W],
                  {key: 16 * self.dcnt[s]} if self.dcnt[s] else None)
        self.dcnt[s] += 1
        tok = {key: 16 * self.dcnt[s]}

        def fn(e, out=out, in_=in_):
            return e.dma_start(out=out, in_=in_)

        self._emit(q, fn, d, (self.sem[key], 16))
        for n in R:
            b = self.buf(n)
            b.rs = merge(b.rs, tok)
        for n in W:
            b = self.buf(n)
            b.ws = merge(b.ws, tok)
        return tok

    def all_tokens(self):
        t = {k: self.cnt[k] for k in self.ENG if self.cnt[k]}
        for i in range(self.NDMA):
            if self.dcnt[i]:
                t["d%d" % i] = 16 * self.dcnt[i]
        return t

    def barrier(self):
        t = self.all_tokens()
        for k in self.ENG:
            self.pending[k] = merge(self.pending[k], t)
        self.bufs = {}


class Arena:
    def __init__(self, t, size):
        self.t = t
        self.size = size
        self.off = 0

    def alloc(self, shape, dt):
        n = int(np.prod(shape)) * DSZ[dt]
        off = (self.off + 63) // 64 * 64
        assert off + n <= self.size, ("SBUF arena overflow", off, n, self.size)
        self.off = off + n
        ap = self.t[:, off:off + n].bitcast(dt)
        if len(shape) == 1:
            return ap
        names = " ".join("a%d" % i for i in range(len(shape)))
        kw = {"a%d" % i: int(shape[i]) for i in range(len(shape))}
        return ap.rearrange("p (%s) -> p %s" % (names, names), **kw)

    def mark(self):
        return self.off

    def release(self, m):
        self.off = m


def build_model(NL=DEPTH, debug=False, phases="ABCD"):
    nc = bass.Bass("TRN2", target_bir_lowering=False)

    def din(name, shape, dt=F32):
        return nc.dram_tensor(name, list(shape), dt, kind="ExternalInput").ap()

    class C:
        pass
    x_in = din("x", [T, D])
    out_final = nc.dram_tensor("out", [T, D], F32, kind="ExternalOutput").ap()
    xbufs = [nc.dram_tensor("xbuf%d" % i, [T, D], F32).ap() for i in range(2)]
    LAYERED = dict(wf=[128, 8, 2816], wt=[128, 8, 392], wuk=[128, 8, 128], wuv=[128, 8, 64], kvg_bc=[128, 128],
                   kvg_col=[128, 1], ldt=[128, 16], are=[128, 16], aim=[128, 16], bre=[128, 16, 16], bim=[128, 16, 16],
                   cre=[128, 16, 16], cim=[128, 16, 16], dsk=[128, 32], wglu=[128, 4, 1024], bglu=[128, 8],
                   wout=[128, 8, 1024], lng=[128, 1024], lnb=[128, 1024])
    LAY = {k: din(k, [NL] + v) for k, v in LAYERED.items()}

    def set_layer(l):
        C.x_d = x_in if l == 0 else xbufs[(l - 1) % 2]
        C.out_d = out_final if l == NL - 1 else xbufs[l % 2]
        for k in LAYERED:
            setattr(C, k.replace("_", "") + "_d", LAY[k][l])
    ident_d = din("ident", [128, 128])
    caus_d = din("caus", [128, 128])
    tri_d = din("tri", [128, 128])
    dg_d = din("dg", [128, 8, 352])
    mv_d = din("mvals", [128, 24])
    kk_d = din("kk", [128, 512])

    def scr(name, shape, dt):
        if debug and name in debug:
            return nc.dram_tensor(name, list(shape), dt, kind="ExternalOutput").ap()
        return nc.dram_tensor(name, list(shape), dt).ap()

    qT_d = scr("qT_s", [4, 128, T], BF16)
    idx_d = scr("idx_s", [6, 128, T], BF16)
    ga_d = scr("ga_s", [4, 128, T], BF16)
    u_d = scr("u_s", [4, 128, T], BF16)
    gs_d = scr("gs_s", [4, 128, T], BF16)
    chat_d = scr("chat_s", [T, 128], BF16)
    chatT_d = scr("chatT_s", [128, T], BF16)
    wn_d = scr("wn_s", [T, 8], F32)
    mix_d = scr("mix_s", [8, 128, T], BF16)
    dbgC = bool(debug) and "ygT_s" in debug
    if dbgC:
        ygT_dbg = scr("ygT_s", [4, 128, S], BF16)
        W_dbg = scr("W_s", [128, 32, 128], BF16)
        Vre_dbg = scr("Vre_s", [128, 16, 128], BF16)
        Vim_dbg = scr("Vim_s", [128, 16, 128], BF16)
        Wcre_dbg = scr("Wcre_s", [128, 16, 128], BF16)
        Wcim_dbg = scr("Wcim_s", [128, 16, 128], BF16)
        RT_dbg = scr("RT_s", [128, 32], F32)

    es = ExitStack()
    with es:
        SBSZ = 200 * 1024
        arena_t = es.enter_context(nc.sbuf_tensor("arena", [128, SBSZ], U8))
        AR = Arena(arena_t, SBSZ)
        psum = es.enter_context(nc.psum_tensor("ps", [128, 8, 512], F32))
        Sd = Sched(nc, es)

        def PSB(b, w=512):
            return psum[:, b, 0:w]

        def PE(m, *a, **k):
            return Sd.op("pe", m, *a, **k)

        def ACT(m, *a, **k):
            return Sd.op("act", m, *a, **k)

        def DVE(m, *a, **k):
            return Sd.op("dve", m, *a, **k)

        def POOL(m, *a, **k):
            return Sd.op("pool", m, *a, **k)

        DMA = Sd.dma

        identf = AR.alloc([128], F32)
        identb = AR.alloc([128], BF16)
        onesb = AR.alloc([128], BF16)
        causf = AR.alloc([128], F32)
        trif = AR.alloc([128], F32)
        dgb = AR.alloc([8, 352], BF16)
        wukp = AR.alloc([8, 128], BF16)
        wuvp = AR.alloc([8, 64], BF16)
        kvgbc = AR.alloc([128], F32)
        kvgcol = AR.alloc([1], F32)
        W_sb = AR.alloc([32, 128], BF16)
        Wc_re = AR.alloc([16, 128], BF16)
        Wc_im = AR.alloc([16, 128], BF16)
        V_re = AR.alloc([16, 128], BF16)
        V_im = AR.alloc([16, 128], BF16)
        Rsc = AR.alloc([16], F32)
        Thr = AR.alloc([16], F32)
        kkf = AR.alloc([512], F32)
        bglu = AR.alloc([8], F32)
        pm = AR.mark()

        stgA = AR.alloc([8, 352], F32)
        DMA(identf, ident_d, W=["identf"], N=["identf"])
        DMA(causf, caus_d, W=["causf"], N=["causf"])
        DMA(trif, tri_d, W=["trif"], N=["trif"])
        DMA(kkf, kk_d, W=["kkf"], N=["kkf"])
        DMA(stgA, dg_d, W=["stgA"], N=["stgA"])
        POOL("tensor_copy", identb, identf, R=["identf"], W=["identb"], N=["identb"])
        POOL("memset", onesb, 1.0, W=["onesb"], N=["onesb"])
        POOL("tensor_copy", dgb, stgA, R=["stgA"], W=["dgb"], N=["dgb"])
        Sd.barrier()
        AR.release(pm)

        def layer_consts():
            m0 = AR.mark()
            stgB = AR.alloc([8, 64], F32)
            stgC = AR.alloc([8, 128], F32)
            DMA(kvgbc, C.kvgbc_d, W=["kvgbc"], N=["kvgbc"])
            DMA(kvgcol, C.kvgcol_d, W=["kvgcol"], N=["kvgcol"])
            DMA(bglu, C.bglu_d, W=["bglu"], N=["bglu"])
            DMA(stgB, C.wuv_d, W=["stgB"], N=["stgB"])
            DMA(stgC, C.wuk_d, W=["stgC"], N=["stgC"])
            DVE("tensor_scalar", wuvp, stgB, kvgcol[:, 0:1], None, ALU.mult, R=["stgB", "kvgcol"], W=["wuvp"], N=["wuvp"])
            DVE("scalar_tensor_tensor", wukp, stgC, 0.125, kvgbc.unsqueeze(1).to_broadcast([128, 8, 128]),
                ALU.mult, ALU.mult, R=["stgC", "kvgbc"], W=["wukp"], N=["wukp"])
            Sd.barrier()
            AR.release(m0)

        def sincos(ang, F, out_cos, out_sin, tmp_f, tmp_i, tmp_g, tag):
            bn = lambda s: tag + s
            DVE("tensor_scalar", tmp_f, ang, 1.0 / (2 * PI), None, ALU.mult, R=[bn("ang")], W=[bn("tf")], N=[bn("tf")])
            DVE("tensor_copy", tmp_i, tmp_f, R=[bn("tf")], W=[bn("ti")], N=[bn("ti")])
            DVE("tensor_copy", tmp_f, tmp_i, R=[bn("ti")], W=[bn("tf")], N=[bn("tf")])
            DVE("scalar_tensor_tensor", out_sin, tmp_f, -2 * PI, ang, ALU.mult, ALU.add,
                R=[bn("tf"), bn("ang")], W=[bn("r")], N=[bn("r")])
            for (thr, cmp, adj) in ((PI, ALU.is_gt, -2 * PI), (-PI, ALU.is_lt, 2 * PI)):
                DVE("tensor_scalar", tmp_g, out_sin, thr, adj, cmp, ALU.mult, R=[bn("r")], W=[bn("tg")], N=[bn("tg")])
                DVE("tensor_tensor", out_sin, out_sin, tmp_g, ALU.add, R=[bn("tg")], W=[bn("r")], N=[bn("r")])
            DVE("tensor_scalar", out_cos, out_sin, PI / 2, None, ALU.add, R=[bn("r")], W=[bn("rc")], N=[bn("rc")])
            DVE("tensor_scalar", tmp_g, out_cos, PI, -2 * PI, ALU.is_gt, ALU.mult, R=[bn("rc")], W=[bn("tg")], N=[bn("tg")])
            DVE("tensor_tensor", out_cos, out_cos, tmp_g, ALU.add, R=[bn("tg")], W=[bn("rc")], N=[bn("rc")])
            DVE("tensor_scalar", out_sin, out_sin, PI, -PI, ALU.min, ALU.max, R=[bn("r")], W=[bn("r")], N=[bn("r")])
            DVE("tensor_scalar", out_cos, out_cos, PI, -PI, ALU.min, ALU.max, R=[bn("rc")], W=[bn("rc")], N=[bn("rc")])
            ACT("activation", out_sin, out_sin, AF.Sin, R=[bn("r")], W=[bn("sin")], N=[bn("sin"), bn("r")])
            ACT("activation", out_cos, out_cos, AF.Sin, R=[bn("rc")], W=[bn("cos")], N=[bn("cos"), bn("rc")])

        def s5_setup():
            m0 = AR.mark()
            ldt = AR.alloc([16], F32)
            are = AR.alloc([16], F32)
            aim = AR.alloc([16], F32)
            bre = AR.alloc([16, 16], F32)
            bim = AR.alloc([16, 16], F32)
            cre = AR.alloc([16, 16], F32)
            cim = AR.alloc([16, 16], F32)
            dsk = AR.alloc([32], F32)
            mv = AR.alloc([24], F32)
            for ap, d_, n in ((ldt, C.ldt_d, "ldt"), (are, C.are_d, "are"), (aim, C.aim_d, "aim"), (bre, C.bre_d, "bre"),
                              (bim, C.bim_d, "bim"), (cre, C.cre_d, "cre"), (cim, C.cim_d, "cim"), (dsk, C.dsk_d, "dsk"),
                              (mv, mv_d, "mv")):
                DMA(ap, d_, W=[n], N=[n])
            dt = AR.alloc([16], F32)
            dre = AR.alloc([16], F32)
            dim = AR.alloc([16], F32)
            ACT("activation", dt, ldt, AF.Exp, R=["ldt"], W=["dt"], N=["dt"])
            DVE("tensor_tensor", dre, are, dt, ALU.mult, R=["are", "dt"], W=["dre"], N=["dre"])
            DVE("tensor_tensor", dim, aim, dt, ALU.mult, R=["aim", "dt"], W=["dim"], N=["dim"])
            ACT("activation", Rsc, dre, AF.Exp, scale=8.0, R=["dre"], W=["Rsc"], N=["Rsc"])
            th8 = AR.alloc([16], F32)
            tq = AR.alloc([16], F32)
            tqi = AR.alloc([16], I32)
            tg = AR.alloc([16], F32)
            DVE("tensor_scalar", th8, dim, 8.0, None, ALU.mult, R=["dim"], W=["th8"], N=["th8"])
            DVE("tensor_scalar", tq, th8, 1.0 / (2 * PI), None, ALU.mult, R=["th8"], W=["tq"], N=["tq"])
            DVE("tensor_copy", tqi, tq, R=["tq"], W=["tqi"], N=["tqi"])
            DVE("tensor_copy", tq, tqi, R=["tqi"], W=["tq"], N=["tq"])
            DVE("scalar_tensor_tensor", Thr, tq, -2 * PI, th8, ALU.mult, ALU.add, R=["tq", "th8"], W=["Thr"], N=["Thr"])
            for (thr, cmp, adj) in ((PI, ALU.is_gt, -2 * PI), (-PI, ALU.is_lt, 2 * PI)):
                DVE("tensor_scalar", tg, Thr, thr, adj, cmp, ALU.mult, R=["Thr"], W=["tg8"], N=["tg8"])
                DVE("tensor_tensor", Thr, Thr, tg, ALU.add, R=["tg8"], W=["Thr"], N=["Thr"])
            F = 16 * 24
            mre = AR.alloc([16, 24], F32)
            ang = AR.alloc([16, 24], F32)
            Ere = AR.alloc([16, 24], F32)
            Eim = AR.alloc([16, 24], F32)
            tf_ = AR.alloc([16, 24], F32)
            tg_ = AR.alloc([16, 24], F32)
            ti_ = AR.alloc([16, 24], I32)
            mvb = mv.unsqueeze(1).to_broadcast([128, 16, 24])
            DVE("tensor_tensor", mre, dre.unsqueeze(2).to_broadcast([128, 16, 24]), mvb, ALU.mult,
                R=["dre", "mv"], W=["mre"], N=["mre"])
            DVE("tensor_tensor", ang, dim.unsqueeze(2).to_broadcast([128, 16, 24]), mvb, ALU.mult,
                R=["dim", "mv"], W=["Eang"], N=["Eang"])
            ACT("activation", mre, mre, AF.Exp, R=["mre"], W=["mre"], N=["mre"])
            sincos(ang, F, Ere, Eim, tf_, ti_, tg_, "E")
            DVE("tensor_tensor", Ere, Ere, mre, ALU.mult, R=["Ecos", "mre"], W=["Ere"], N=["Ere", "Ecos"])
            DVE("tensor_tensor", Eim, Eim, mre, ALU.mult, R=["Esin", "mre"], W=["Eim"], N=["Eim", "Esin"])
            nr = AR.alloc([16], F32)
            den = AR.alloc([16], F32)
            t1 = AR.alloc([16], F32)
            t2 = AR.alloc([16], F32)
            fre = AR.alloc([16], F32)
            fim = AR.alloc([16], F32)
            e1r = Ere[:, :, 8]
            e1i = Eim[:, :, 8]
            DVE("tensor_scalar", nr, e1r, -1.0, None, ALU.add, R=["Ere"], W=["nr"], N=["nr"])
            DVE("tensor_tensor", t1, are, are, ALU.mult, R=["are"], W=["t1"], N=["t1"])
            DVE("tensor_tensor", t2, aim, aim, ALU.mult, R=["aim"], W=["t2"], N=["t2"])
            DVE("tensor_tensor", den, t1, t2, ALU.add, R=["t1", "t2"], W=["den"], N=["den"])
            DVE("reciprocal", den, den, R=["den"], W=["den"], N=["den"])
            DVE("tensor_tensor", t1, nr, are, ALU.mult, R=["nr", "are"], W=["t1"], N=["t1"])
            DVE("tensor_tensor", t2, e1i, aim, ALU.mult, R=["Eim", "aim"], W=["t2"], N=["t2"])
            DVE("tensor_tensor", fre, t1, t2, ALU.add, R=["t1", "t2"], W=["fre"], N=["fre"])
            DVE("tensor_tensor", fre, fre, den, ALU.mult, R=["den"], W=["fre"], N=["fre"])
            DVE("tensor_tensor", t1, e1i, are, ALU.mult, R=["Eim", "are"], W=["t1"], N=["t1"])
            DVE("tensor_tensor", t2, nr, aim, ALU.mult, R=["nr", "aim"], W=["t2"], N=["t2"])
            DVE("tensor_tensor", fim, t1, t2, ALU.subtract, R=["t1", "t2"], W=["fim"], N=["fim"])
            DVE("tensor_tensor", fim, fim, den, ALU.mult, R=["den"], W=["fim"], N=["fim"])
            Bre = AR.alloc([16, 16], F32)
            Bim = AR.alloc([16, 16], F32)
            u1 = AR.alloc([16, 16], F32)
            freb = fre.unsqueeze(2).to_broadcast([128, 16, 16])
            fimb = fim.unsqueeze(2).to_broadcast([128, 16, 16])
            DVE("tensor_tensor", Bre, bre, freb, ALU.mult, R=["bre", "fre"], W=["Bre"], N=["Bre"])
            DVE("tensor_tensor", u1, bim, fimb, ALU.mult, R=["bim", "fim"], W=["u1"], N=["u1"])
            DVE("tensor_tensor", Bre, Bre, u1, ALU.subtract, R=["u1"], W=["Bre"], N=["Bre"])
            DVE("tensor_tensor", Bim, bim, freb, ALU.mult, R=["bim", "fre"], W=["Bim"], N=["Bim"])
            DVE("tensor_tensor", u1, bre, fimb, ALU.mult, R=["bre", "fim"], W=["u1"], N=["u1"])
            DVE("tensor_tensor", Bim, Bim, u1, ALU.add, R=["u1"], W=["Bim"], N=["Bim"])

            big = [16, 8, 16]
            tA = AR.alloc(big, F32)
            tB = AR.alloc(big, F32)

            def cprod(o_re, o_im, sl, yre, yim, yn_re, yn_im, neg_im, tag):
                er = Ere[:, :, sl].unsqueeze(3).to_broadcast([128, 16, 8, 16])
                ei = Eim[:, :, sl].unsqueeze(3).to_broadcast([128, 16, 8, 16])
                yr = yre.unsqueeze(2).to_broadcast([128, 16, 8, 16])
                yi = yim.unsqueeze(2).to_broadcast([128, 16, 8, 16])
                DVE("tensor_tensor", tA, er, yr, ALU.mult, R=["Ere", yn_re], W=["tA"], N=["tA"])
                DVE("tensor_tensor", tB, ei, yi, ALU.mult, R=["Eim", yn_im], W=["tB"], N=["tB"])
                DVE("tensor_tensor", o_re, tA, tB, ALU.subtract, R=["tA", "tB"], W=[tag + "re"], N=[tag + "re"])
                DVE("tensor_tensor", tA, er, yi, ALU.mult, R=["Ere", yn_im], W=["tA"], N=["tA"])
                DVE("tensor_tensor", tB, ei, yr, ALU.mult, R=["Eim", yn_re], W=["tB"], N=["tB"])
                if neg_im:
                    DVE("scalar_tensor_tensor", o_im, tA, -1.0, tB, ALU.mult, ALU.subtract,
                        R=["tA", "tB"], W=[tag + "im"], N=[tag + "im"])
                else:
                    DVE("tensor_tensor", o_im, tA, tB, ALU.add, R=["tA", "tB"], W=[tag + "im"], N=[tag + "im"])

            Bq_re = AR.alloc(big, F32)
            Bq_im = AR.alloc(big, F32)
            Cq_re = AR.alloc(big, F32)
            Cq_im = AR.alloc(big, F32)
            cprod(Bq_re, Bq_im, slice(0, 8), Bre, Bim, "Bre", "Bim", False, "Bq")
            cprod(Cq_re, Cq_im, slice(8, 16), cre, cim, "cre", "cim", True, "Cq")
            POOL("tensor_copy", Wc_re, Cq_re.rearrange("p i a b -> p i (a b)"), R=["Cqre"], W=["Wc_re"], N=["Wc_re"])
            POOL("tensor_copy", Wc_im, Cq_im.rearrange("p i a b -> p i (a b)"), R=["Cqim"], W=["Wc_im"], N=["Wc_im"])
            Bqm_re = [AR.alloc([16, 128], BF16) for _ in range(2)]
            Bqm_im = [AR.alloc([16, 128], BF16) for _ in range(2)]
            Sd.buf("Bqreb").new()
            Sd.buf("Bqimb").new()
            for e_ in range(2):
                o_ = slice(64 * (1 - e_), 64 * (1 - e_) + 64)
                k_ = slice(64 * e_, 64 * e_ + 64)
                POOL("memset", Bqm_re[e_][o_], 0.0, W=["Bqreb"])
                POOL("memset", Bqm_im[e_][o_], 0.0, W=["Bqimb"])
                POOL("tensor_copy", Bqm_re[e_][k_], Bq_re[k_].rearrange("p i a b -> p i (a b)"), R=["Bqre"], W=["Bqreb"])
                POOL("tensor_copy", Bqm_im[e_][k_], Bq_im[k_].rearrange("p i a b -> p i (a b)"), R=["Bqim"], W=["Bqimb"])
            tW = AR.alloc([128], F32)
            for g in range(32):
                i, e = g // 2, g % 2
                pb = 4 + (g % 2)
                sl = slice(64 * e, 64 * e + 64)
                PE("matmul", PSB(pb, 128), Bqm_re[e][:, i, :], Wc_re[:, i, :], start=True, stop=False,
                   R=["Bqreb", "Wc_re"], W=["psW%d" % pb], N=["psW%d" % pb])
                PE("matmul", PSB(pb, 128), Bqm_im[e][:, i, :], Wc_im[:, i, :], start=False, stop=True,
                   R=["Bqimb", "Wc_im"], W=["psW%d" % pb])
                DVE("tensor_tensor", tW, PSB(pb, 128), trif, ALU.mult, R=["psW%d" % pb, "trif"], W=["tW"], N=["tW"])
                DVE("scalar_tensor_tensor", W_sb[:, g, :], identf, dsk[:, g:g + 1], tW, ALU.mult, ALU.add,
                    R=["tW", "identf", "dsk"], W=["W_sb"])
            cprod(Bq_re, Bq_im, slice(16, 24), Bre, Bim, "Bre", "Bim", False, "Bq")
            for i in range(16):
                for (src, dst, sn, dn) in ((Bq_re, V_re, "Bqre", "V_re"), (Bq_im, V_im, "Bqim", "V_im")):
                    pb = 6 + (i % 2)
                    PE("transpose", PSB(pb, 128), src[:, i, :, :].rearrange("p a b -> p (a b)"), identf,
                       R=[sn, "identf"], W=["psV%d" % pb], N=["psV%d" % pb])
                    ACT("activation", dst[:, i, :], PSB(pb, 128), AF.Copy, R=["psV%d" % pb], W=[dn])
            if dbgC:
                DMA(W_dbg, W_sb, R=["W_sb"])
                DMA(Vre_dbg, V_re, R=["V_re"])
                DMA(Vim_dbg, V_im, R=["V_im"])
                DMA(Wcre_dbg, Wc_re, R=["Wc_re"])
                DMA(Wcim_dbg, Wc_im, R=["Wc_im"])
                DMA(RT_dbg[:, 0:16], Rsc, R=["Rsc"])
                DMA(RT_dbg[:, 16:32], Thr, R=["Thr"])
            Sd.barrier()
            AR.release(m0)

        def phase_A(seq):
            m0 = AR.mark()
            wf = AR.alloc([8, 2816], BF16)
            wt = AR.alloc([8, 392], BF16)
            stg = [AR.alloc([2816], F32) for _ in range(2)]
            for k in range(8):
                b = k % 2
                DMA(stg[b], C.wf_d[:, k, :], W=["stg%d" % b], N=["stg%d" % b])
                POOL("tensor_copy", wf[:, k, :], stg[b], R=["stg%d" % b], W=["wf"])
            wts = AR.alloc([8, 392], F32)
            DMA(wts, C.wt_d, W=["wts"], N=["wts"])
            POOL("tensor_copy", wt, wts, R=["wts"], W=["wt"], N=["wt"])
            xtok = [AR.alloc([1024], F32) for _ in range(2)]
            xbf = [AR.alloc([1024], BF16) for _ in range(2)]
            xT = [AR.alloc([8, 512], BF16) for _ in range(2)]
            fm = AR.alloc([22, 512], BF16)
            sqj = AR.alloc([128], BF16)
            ss = AR.alloc([1], F32)
            vv = AR.alloc([1], F32)
            sv = AR.alloc([1], F32)
            rstd = AR.alloc([1], F32)
            chat_c = AR.alloc([4, 128], BF16)
            chatT_c = AR.alloc([512], BF16)
            wq = AR.alloc([264], F32)
            sq = AR.alloc([256], F32)
            qn = AR.alloc([8], F32)
            aw = AR.alloc([8], F32)
            s1 = AR.alloc([1], F32)
            wn_c = AR.alloc([4, 8], F32)
            for ch in range(8):
                t0c = seq * S + ch * 512
                xb = xT[ch % 2]
                xn = "xT%d" % (ch % 2)
                Sd.buf(xn).new()
                Sd.buf("chat_c").new()
                Sd.buf("chatT_c").new()
                Sd.buf("wn_c").new()
                for tt in range(4):
                    t0 = t0c + tt * 128
                    j = (ch * 4 + tt) % 2
                    DMA(xtok[j], C.x_d[t0:t0 + 128, :], W=["xtok%d" % j], N=["xtok%d" % j])
                    POOL("tensor_copy", xbf[j], xtok[j], R=["xtok%d" % j], W=["xbf%d" % j], N=["xbf%d" % j])
                    tp = PSB(6 + j).bitcast(BF16).rearrange("p (a b) -> p a b", a=8)
                    Sd.buf("pstp%d" % j).new()
                    for k in range(8):
                        PE("transpose", tp[:, k, :], xbf[j][:, k * 128:(k + 1) * 128], identb,
                           R=["xbf%d" % j, "identb"], W=["pstp%d" % j])
                    ACT("activation", xb[:, :, tt * 128:(tt + 1) * 128], tp, AF.Copy, R=["pstp%d" % j], W=[xn])
                    tokps = PSB(5, 392)
                    Sd.buf("tokps").new()
                    for k in range(8):
                        PE("matmul", tokps, xb[:, k, tt * 128:(tt + 1) * 128], wt[:, k, :], start=(k == 0), stop=(k == 7),
                           R=[xn, "wt"], W=["tokps"])
                    ACT("activation", sqj, tokps[:, 0:128], AF.Square, accum_out=ss[:, 0:1], R=["tokps"], W=["ss", "sqj"], N=["ss", "sqj"])
                    DVE("tensor_scalar", vv, ss, 1.0 / 128, RMS_EPS, ALU.mult, ALU.add, R=["ss"], W=["vv"], N=["vv"])
                    ACT("activation", sv, vv, AF.Sqrt, R=["vv"], W=["sv"], N=["sv"])
                    DVE("reciprocal", rstd, sv, R=["sv"], W=["rstd"], N=["rstd"])
                    ACT("activation", chat_c[:, tt, :], tokps[:, 0:128], AF.Copy, scale=rstd[:, 0:1],
                        R=["tokps", "rstd"], W=["chat_c"])
                    tp2 = PSB(4).bitcast(BF16)[:, 0:128]
                    PE("transpose", tp2, chat_c[:, tt, :], identb, R=["chat_c", "identb"], W=["pstp2"], N=["pstp2"])
                    DVE("tensor_copy", chatT_c[:, tt * 128:(tt + 1) * 128], tp2, R=["pstp2"], W=["chatT_c"])
                    ACT("activation", wq, tokps[:, 128:392], AF.Copy, R=["tokps"], W=["wq"], N=["wq"])
                    DVE("tensor_tensor", sq, wq[:, 8:264], wq[:, 8:264], ALU.mult, R=["wq"], W=["sq"], N=["sq"])
                    DVE("tensor_reduce", qn, sq.rearrange("p (h d) -> p h d", h=8), AX.X, ALU.add, R=["sq"], W=["qn"], N=["qn"])
                    ACT("activation", qn, qn, AF.Sqrt, R=["qn"], W=["qn"], N=["qn"])
                    DVE("scalar_tensor_tensor", aw, wq[:, 0:8], -1.0, wq[:, 0:8], ALU.mult, ALU.max, R=["wq"], W=["aw"], N=["aw"])
                    DVE("tensor_tensor", aw, aw, qn, ALU.mult, R=["qn"], W=["aw"], N=["aw"])
                    DVE("tensor_reduce", s1, aw, AX.X, ALU.add, R=["aw"], W=["s1"], N=["s1"])
                    DVE("tensor_scalar", s1, s1, 1e-30, None, ALU.add, R=["s1"], W=["s1"], N=["s1"])
                    DVE("reciprocal", s1, s1, R=["s1"], W=["s1"], N=["s1"])
                    DVE("tensor_scalar", wn_c[:, tt, :], wq[:, 0:8], s1[:, 0:1], None, ALU.mult, R=["wq", "s1"], W=["wn_c"])
                DMA(chat_d[t0c:t0c + 512, :].rearrange("(a p) c -> p a c", p=128), chat_c, R=["chat_c"])
                DMA(chatT_d[:, t0c:t0c + 512], chatT_c, R=["chatT_c"])
                DMA(wn_d[t0c:t0c + 512, :].rearrange("(a p) h -> p a h", p=128), wn_c, R=["wn_c"])
                Sd.buf("fm").new()
                for cb in range(22):
                    pb = cb % 4
                    pn = "psfm%d" % pb
                    Sd.buf(pn).new()
                    for k in range(8):
                        PE("matmul", PSB(pb), wf[:, k, cb * 128:(cb + 1) * 128], xb[:, k, :], start=(k == 0), stop=(k == 7),
                           R=["wf", xn], W=[pn])
                    silu = (10 <= cb < 14) or cb >= 18
                    if silu:
                        ACT("activation", fm[:, cb, :], PSB(pb), AF.Silu, R=[pn], W=["fm"])
                    elif cb % 2 == 0:
                        ACT("activation", fm[:, cb, :], PSB(pb), AF.Copy, R=[pn], W=["fm"])
                    else:
                        DVE("tensor_copy", fm[:, cb, :], PSB(pb), R=[pn], W=["fm"])
                for (dd, c0_, c1_) in ((qT_d, 0, 4), (idx_d, 4, 10), (ga_d, 10, 14), (u_d, 14, 18), (gs_d, 18, 22)):
                    DMA(dd[:, :, t0c:t0c + 512].rearrange("j p t -> p j t"), fm[:, c0_:c1_, :], R=["fm"])
            Sd.barrier()
            AR.release(m0)

        def phase_B(seq):
            m0 = AR.mark()
            s0 = seq * S
            kiT = AR.alloc([3, S], BF16)
            chatT = AR.alloc([S], BF16)
            chtok = AR.alloc([32, 128], BF16)
            DMA(kiT, idx_d[3:6, :, s0:s0 + S].rearrange("j p t -> p j t"), W=["kiT"], N=["kiT"])
            DMA(chatT, chatT_d[:, s0:s0 + S], W=["chatT"], N=["chatT"])
            DMA(chtok, chat_d[s0:s0 + S, :].rearrange("(a p) c -> p a c", p=128), W=["chtok"], N=["chtok"])
            I_sb = AR.alloc([S], F16)
            mask = AR.alloc([S], BF16)
            maskT = AR.alloc([32, 128], BF16)
            Rsb = [AR.alloc([512], BF16) for _ in range(3)]
            diagw = AR.alloc([8, 128], BF16)
            qlat = AR.alloc([8, 128], BF16)
            sqs = AR.alloc([8, 128], BF16)
            esb = [AR.alloc([512], BF16) for _ in range(2)]
            psb_ = [AR.alloc([512], BF16) for _ in range(2)]
            rl = AR.alloc([512], F32)
            on = AR.alloc([512], BF16)
            qTb = AR.alloc([4, 128], BF16)
            idxq = AR.alloc([3, 128], BF16)
            wnb = AR.alloc([8], F32)
            gab = AR.alloc([4, 128], BF16)
            mixo = AR.alloc([4, 128], BF16)
            m2 = AR.alloc([1], F32)
            negB = AR.alloc([1], F32)
            mid = [AR.alloc([1], F32) for _ in range(2)]
            cnt = AR.alloc([1], F32)
            mm_ = AR.alloc([1], F32)
            BST = int(os.environ.get('BST', '9'))
            for qb in range(int(os.environ.get('NQB', '32'))):
                t0 = s0 + qb * 128
                N = (qb + 1) * 128
                DMA(qTb, qT_d[:, :, t0:t0 + 128].rearrange("j p t -> p j t"), W=["qTb"], N=["qTb"])
                DMA(idxq, idx_d[0:3, :, t0:t0 + 128].rearrange("j p t -> p j t"), W=["idxq"], N=["idxq"])
                DMA(wnb, wn_d[t0:t0 + 128, :], W=["wnb"], N=["wnb"])
                DMA(gab, ga_d[:, :, t0:t0 + 128].rearrange("j p t -> p j t"), W=["gab"], N=["gab"])
                qlps = psum[:, 0:2, :].rearrange("p b (h t) -> p (b h) t", h=4)
                Sd.buf("ps01").new()
                for h in range(8):
                    j, e = h // 2, h % 2
                    sl = slice(64 * e, 64 * e + 64)
                    PE("matmul", qlps[:, h, :], wukp[:, h, :], qTb[:, j, :], start=True, stop=True,
                       R=["qTb", "wukp"], W=["ps01"])
                ACT("activation", qlat, qlps, AF.Copy, R=["ps01"], W=["qlat"], N=["qlat"])
                ACT("activation", sqs, qlps, AF.Square, R=["ps01"], W=["sqs"], N=["sqs"])
                Sd.buf("ps01").new()
                for b in range(2):
                    PE("matmul", PSB(b), onesb, sqs[:, 4 * b:4 * b + 4, :].rearrange("p h t -> p (h t)"),
                       start=True, stop=True, R=["sqs", "onesb"], W=["ps01"])
                DVE("tensor_reduce", m2, psum[:, 0:2, :].rearrange("p b f -> p (b f)"), AX.X, ALU.max,
                    R=["ps01"], W=["m2"], N=["m2"])
                ACT("activation", negB, m2, AF.Sqrt, scale=128.0 * 1.03, R=["m2"], W=["negB"], N=["negB"])
                DVE("tensor_scalar", negB, negB, -1.0, None, ALU.mult, R=["negB"], W=["negB"], N=["negB"])
                if BST < 2:
                    continue
                DVE("tensor_tensor", diagw, identf.unsqueeze(1).to_broadcast([128, 8, 128]),
                    wnb.unsqueeze(2).to_broadcast([128, 8, 128]), ALU.mult, R=["identf", "wnb"], W=["diagw"], N=["diagw"])
                nkc = (N + 511) // 512
                Sd.buf("I_sb").new()
                ridx = 0
                for kc in range(nkc):
                    w = min(512, N - kc * 512)
                    k0 = kc * 512
                    Sd.buf("psI").new()
                    for h in range(8):
                        tl, r = h // 3, h % 3
                        sl = slice(32 * r, 32 * r + 32)
                        pb = 2 + (h % 2)
                        pn = "psL%d" % pb
                        PE("matmul", PSB(pb, w), idxq[:, tl, :], kiT[:, r, k0:k0 + w], start=True, stop=True,
                           R=["idxq", "kiT"], W=[pn], N=[pn])
                        rb = ridx % 3
                        ridx += 1
                        rn = "Rsb%d" % rb
                        if h % 2 == 0:
                            ACT("activation", Rsb[rb][:, 0:w], PSB(pb, w), AF.Relu, R=[pn], W=[rn], N=[rn])
                        else:
                            DVE("tensor_scalar", Rsb[rb][:, 0:w], PSB(pb, w), 0.0, None, ALU.max, R=[pn], W=[rn], N=[rn])
                        PE("matmul", PSB(4, w), diagw[:, h, :], Rsb[rb][:, 0:w], start=(h == 0), stop=(h == 7),
                           R=[rn, "diagw"], W=["psI"])
                    if kc == nkc - 1:
                        wd = w - 128
                        if wd > 0:
                            DVE("tensor_copy", I_sb[:, k0:k0 + wd], PSB(4, wd), R=["psI"], W=["I_sb"])
                        DVE("tensor_tensor", I_sb[:, k0 + wd:k0 + w], psum[:, 4, wd:w], causf, ALU.add,
                            R=["psI", "causf"], W=["I_sb"])
                    else:
                        DVE("tensor_copy", I_sb[:, k0:k0 + w], PSB(4, w), R=["psI"], W=["I_sb"])
                if BST < 3:
                    continue
                DVE("memset", mid[0], 0.0, W=["mid0"], N=["mid0"])
                h_k = RB
                for it in range(NIT):
                    a, b = it % 2, (it + 1) % 2
                    DVE("tensor_scalar", mask[:, 0:N], I_sb[:, 0:N], mid[a][:, 0:1], 0.0, ALU.is_ge, ALU.add,
                        accum_out=cnt[:, 0:1], R=["I_sb", "mid%d" % a], W=["mask", "cnt"], N=["mask", "cnt"])
                    DVE("tensor_scalar", mm_, cnt, 255.5, h_k, ALU.is_ge, ALU.mult, R=["cnt"], W=["mm"], N=["mm"])
                    DVE("scalar_tensor_tensor", mid[b], mm_, -h_k / 2, mid[a], ALU.add, ALU.add,
                        R=["mm", "mid%d" % a], W=["mid%d" % b], N=["mid%d" % b])
                    h_k = h_k / 2
                fin = NIT % 2
                DVE("tensor_scalar", mask[:, 0:N], I_sb[:, 0:N], mid[fin][:, 0:1], -h_k, ALU.subtract, ALU.is_ge,
                    R=["I_sb", "mid%d" % fin], W=["mask"], N=["mask"])
                if BST < 4:
                    continue
                Sd.buf("maskT").new()
                for g4 in range((qb + 4) // 4):
                    nb = min(4, qb + 1 - 4 * g4)
                    tpm = PSB(5).bitcast(BF16).rearrange("p (a b) -> p a b", a=8)
                    Sd.buf("psT").new()
                    for u_ in range(nb):
                        kb = 4 * g4 + u_
                        PE("transpose", tpm[:, u_, :], mask[:, kb * 128:(kb + 1) * 128], identb,
                           R=["mask", "identb"], W=["psT"])
                    DVE("tensor_copy", maskT[:, 4 * g4:4 * g4 + nb, :], tpm[:, 0:nb, :], R=["psT"], W=["maskT"])
                if BST < 5:
                    continue
                attps = psum[:, 0, :].rearrange("p (j t) -> p j t", j=4)
                Sd.buf("ps01").new()
                for hh in range(2):
                    rhs_q = qlat[:, 4 * hh:4 * hh + 4, :].rearrange("p h t -> p (h t)")
                    Sd.buf("pso").new()
                    Sd.buf("psl").new()
                    for kb in range(qb + 1):
                        pb = 2 + (kb % 2)
                        pn = "psL%d" % pb
                        eb = kb % 2
                        PE("matmul", PSB(pb), chatT[:, kb * 128:(kb + 1) * 128], rhs_q, start=True, stop=True,
                           R=["chatT", "qlat"], W=[pn], N=[pn])
                        ACT("activation", esb[eb], PSB(pb), AF.Exp, bias=negB[:, 0:1], R=[pn, "negB"],
                            W=["esb%d" % eb], N=["esb%d" % eb])
                        POOL("tensor_tensor", psb_[eb].rearrange("p (h t) -> p h t", h=4),
                             esb[eb].rearrange("p (h t) -> p h t", h=4),
                             maskT[:, kb, :].unsqueeze(1).to_broadcast([128, 4, 128]), ALU.mult,
                             R=["esb%d" % eb, "maskT"], W=["psb%d" % eb], N=["psb%d" % eb])
                        PE("matmul", PSB(6), chtok[:, kb, :], psb_[eb], start=(kb == 0), stop=(kb == qb),
                           R=["chtok", "psb%d" % eb], W=["pso"])
                        PE("matmul", PSB(7), onesb, psb_[eb], start=(kb == 0), stop=(kb == qb),
                           R=["onesb", "psb%d" % eb], W=["psl"])
                    DVE("reciprocal", rl, PSB(7), R=["psl"], W=["rl"], N=["rl"])
                    DVE("tensor_tensor", on, PSB(6), rl, ALU.mult, R=["pso", "rl"], W=["on"], N=["on"])
                    for hl in range(4):
                        h = 4 * hh + hl
                        j, e = h // 2, h % 2
                        PE("matmul", attps[64 * e:64 * e + 64, j, :], wuvp[:, h, :], on[:, hl * 128:(hl + 1) * 128],
                           start=True, stop=True, R=["on", "wuvp"], W=["ps01"])
                DVE("tensor_tensor", mixo, attps, gab, ALU.mult, R=["ps01", "gab"], W=["mixo"], N=["mixo"])
                DMA(mix_d[0:4, :, t0:t0 + 128].rearrange("j p t -> p j t"), mixo, R=["mixo"])
            Sd.barrier()
            AR.release(m0)

        def phase_C(seq):
            m0 = AR.mark()
            s0 = seq * S
            wglu = AR.alloc([4, 1024], BF16)
            stg = AR.alloc([4, 1024], F32)
            DMA(stg, C.wglu_d, W=["stgg"], N=["stgg"])
            POOL("tensor_copy", wglu, stg, R=["stgg"], W=["wglu"], N=["wglu"])
            uT = [AR.alloc([S], BF16) for _ in range(2)]
            Ub = [AR.alloc([512], BF16) for _ in range(2)]
            angt = AR.alloc([512], F32)
            C1 = AR.alloc([512], F32)
            S1 = AR.alloc([512], F32)
            tf_ = AR.alloc([512], F32)
            tg_ = AR.alloc([512], F32)
            ti_ = AR.alloc([512], I32)
            ta = AR.alloc([512], F32)
            tb = AR.alloc([512], F32)
            Stre = AR.alloc([512], F32)
            Stim = AR.alloc([512], F32)
            Zre = AR.alloc([512], F32)
            Zim = AR.alloc([512], F32)
            Xre = [AR.alloc([512], BF16) for _ in range(2)]
            Xim = [AR.alloc([512], BF16) for _ in range(2)]
            Yg = AR.alloc([8, 512], BF16)
            ygT = AR.alloc([4, S], BF16)
            sig = AR.alloc([512], F32)
            ssm = AR.alloc([512], F32)
            gsb = AR.alloc([512], BF16)
            mxo = AR.alloc([512], BF16)
            for e in range(2):
                POOL("memset", Xre[e], 0.0, W=["Xre%d" % e], N=["Xre%d" % e])
                POOL("memset", Xim[e], 0.0, W=["Xim%d" % e], N=["Xim%d" % e])
            for q in range(4):
                ub = uT[q % 2]
                un = "uT%d" % (q % 2)
                DMA(ub, u_d[q][:, s0:s0 + S], W=[un], N=[un])
                Sd.buf("Yg").new()
                for ip in range(4):
                    i = 4 * q + ip
                    for e in range(2):
                        gl = 2 * ip + e
                        pn = "psU%d" % e
                        Sd.buf(pn).new()
                        for j in range(8):
                            x0 = 112 + 16 * (gl - j)
                            PE("matmul", PSB(e), dgb[:, gl, x0:x0 + 128], ub[:, j:S:8], start=(j == 0), stop=(j == 7),
                               R=[un, "dgb"], W=[pn])
                        ACT("activation", Ub[e], PSB(e), AF.Copy, R=[pn], W=["Ub%d" % e], N=["Ub%d" % e])
                    Sd.buf("psSre").new()
                    Sd.buf("psSim").new()
                    for e in range(2):
                        sl = slice(64 * e, 64 * e + 64)
                        PE("matmul", psum[sl, 2, :], V_re[:, i, sl], Ub[e], start=True, stop=True,
                           R=["V_re", "Ub%d" % e], W=["psSre"])
                        PE("matmul", psum[sl, 3, :], V_im[:, i, sl], Ub[e], start=True, stop=True,
                           R=["V_im", "Ub%d" % e], W=["psSim"])
                    DVE("tensor_scalar", angt, kkf, Thr[:, i:i + 1], None, ALU.mult, R=["kkf", "Thr"], W=["Tang"], N=["Tang"])
                    sincos(angt, 512, C1, S1, tf_, ti_, tg_, "T")
                    DVE("tensor_tensor", ta, PSB(2), C1, ALU.mult, R=["psSre", "Tcos"], W=["ta"], N=["ta"])
                    DVE("tensor_tensor", tb, PSB(3), S1, ALU.mult, R=["psSim", "Tsin"], W=["tb"], N=["tb"])
                    DVE("tensor_tensor", Stre, ta, tb, ALU.add, R=["ta", "tb"], W=["Stre"], N=["Stre"])
                    DVE("tensor_tensor", ta, PSB(3), C1, ALU.mult, R=["psSim", "Tcos"], W=["ta"], N=["ta"])
                    DVE("tensor_tensor", tb, PSB(2), S1, ALU.mult, R=["psSre", "Tsin"], W=["tb"], N=["tb"])
                    DVE("tensor_tensor", Stim, ta, tb, ALU.subtract, R=["ta", "tb"], W=["Stim"], N=["Stim"])
                    Rbc = Rsc[:, i:i + 1].to_broadcast([128, 512])
                    DVE("tensor_tensor_scan", Zre, Rbc, Stre, 0.0, ALU.mult, ALU.add, R=["Stre", "Rsc"], W=["Zre"], N=["Zre"])
                    DVE("tensor_tensor_scan", Zim, Rbc, Stim, 0.0, ALU.mult, ALU.add, R=["Stim", "Rsc"], W=["Zim"], N=["Zim"])
                    DVE("tensor_tensor", ta, Zre, C1, ALU.mult, R=["Zre", "Tcos"], W=["ta"], N=["ta"])
                    DVE("tensor_tensor", tb, Zim, S1, ALU.mult, R=["Zim", "Tsin"], W=["tb"], N=["tb"])
                    for e in range(2):
                        sl = slice(64 * e, 64 * e + 64)
                        DVE("tensor_tensor", Xre[e][sl, 1:512], ta[sl, 0:511], tb[sl, 0:511], ALU.subtract,
                            R=["ta", "tb"], W=["Xre%d" % e], N=["Xre%d" % e])
                    DVE("tensor_tensor", ta, Zim, C1, ALU.mult, R=["Zim", "Tcos"], W=["ta"], N=["ta"])
                    DVE("tensor_tensor", tb, Zre, S1, ALU.mult, R=["Zre", "Tsin"], W=["tb"], N=["tb"])
                    for e in range(2):
                        sl = slice(64 * e, 64 * e + 64)
                        DVE("tensor_tensor", Xim[e][sl, 1:512], ta[sl, 0:511], tb[sl, 0:511], ALU.add,
                            R=["ta", "tb"], W=["Xim%d" % e], N=["Xim%d" % e])
                    for e in range(2):
                        g = 2 * i + e
                        gl = 2 * ip + e
                        pb = 4 + e
                        pn = "psY%d" % e
                        PE("matmul", PSB(pb), W_sb[:, g, :], Ub[e], start=True, stop=False,
                           R=["W_sb", "Ub%d" % e], W=[pn], N=[pn])
                        PE("matmul", PSB(pb), Wc_re[:, i, :], Xre[e], start=False, stop=False,
                           R=["Wc_re", "Xre%d" % e], W=[pn])
                        PE("matmul", PSB(pb), Wc_im[:, i, :], Xim[e], start=False, stop=True,
                           R=["Wc_im", "Xim%d" % e], W=[pn])
                        ACT("activation", Yg[:, gl, :], PSB(pb), AF.Gelu_apprx_tanh, R=[pn], W=["Yg"])
                Sd.buf("ygT").new() if q == 0 else None
                ygv = ygT[:, q, :].rearrange("p (k t) -> p k t", t=8)
                for tau in range(8):
                    pb = 6 + (tau % 2)
                    pn = "psC%d" % pb
                    Sd.buf(pn).new()
                    for gl in range(8):
                        x0 = 112 + 16 * (tau - gl)
                        PE("matmul", PSB(pb), dgb[:, tau, x0:x0 + 128], Yg[:, gl, :], start=(gl == 0), stop=(gl == 7),
                           R=["Yg", "dgb"], W=[pn])
                    if tau % 2 == 0:
                        DVE("tensor_copy", ygv[:, :, tau], PSB(pb), R=[pn], W=["ygT"])
                    else:
                        ACT("activation", ygv[:, :, tau], PSB(pb), AF.Copy, R=[pn], W=["ygT"])
            if dbgC and seq == 0:
                DMA(ygT_dbg.rearrange("q p t -> p q t"), ygT, R=["ygT"])
            for tc in range(8):
                c0 = tc * 512
                for v in range(4):
                    Sd.buf("psU0").new()
                    Sd.buf("psU1").new()
                    for q in range(4):
                        PE("matmul", PSB(0), wglu[:, q, v * 128:(v + 1) * 128], ygT[:, q, c0:c0 + 512],
                           start=(q == 0), stop=(q == 3), R=["wglu", "ygT"], W=["psU0"])
                    for q in range(4):
                        PE("matmul", PSB(1), wglu[:, q, 512 + v * 128:512 + (v + 1) * 128], ygT[:, q, c0:c0 + 512],
                           start=(q == 0), stop=(q == 3), R=["wglu", "ygT"], W=["psU1"])
                    DMA(gsb, gs_d[v][:, s0 + c0:s0 + c0 + 512], W=["gsb"], N=["gsb"])
                    ACT("activation", sig, PSB(1), AF.Sigmoid, bias=bglu[:, 4 + v:5 + v], R=["psU1", "bglu"], W=["sig"], N=["sig"])
                    DVE("scalar_tensor_tensor", ssm, PSB(0), bglu[:, v:v + 1], sig, ALU.add, ALU.mult,
                        R=["psU0", "sig", "bglu"], W=["ssm"], N=["ssm"])
                    DVE("tensor_tensor", mxo, ssm, gsb, ALU.mult, R=["ssm", "gsb"], W=["mxo"], N=["mxo"])
                    DMA(mix_d[4 + v][:, s0 + c0:s0 + c0 + 512], mxo, R=["mxo"])
            Sd.barrier()
            AR.release(m0)

        def phase_D(seq):
            m0 = AR.mark()
            s0 = seq * S
            wout = AR.alloc([8, 1024], BF16)
            stg = [AR.alloc([1024], F32) for _ in range(2)]
            for k in range(8):
                b = k % 2
                DMA(stg[b], C.wout_d[:, k, :], W=["stgo%d" % b], N=["stgo%d" % b])
                POOL("tensor_copy", wout[:, k, :], stg[b], R=["stgo%d" % b], W=["wout"])
            lng = AR.alloc([1024], F32)
            lnb = AR.alloc([1024], F32)
            DMA(lng, C.lng_d, W=["lng"], N=["lng"])
            DMA(lnb, C.lnb_d, W=["lnb"], N=["lnb"])
            mixT = [AR.alloc([8, 128], BF16) for _ in range(2)]
            xt = [AR.alloc([1024], F32) for _ in range(2)]
            rr = [AR.alloc([1024], F32) for _ in range(2)]
            st = AR.alloc([2, 6], F32)
            mvv = AR.alloc([2], F32)
            rs_ = AR.alloc([1], F32)
            for tt in range(32):
                t0 = s0 + tt * 128
                j = tt % 2
                DMA(mixT[j], mix_d[:, :, t0:t0 + 128].rearrange("j p t -> p j t"), W=["mixT%d" % j], N=["mixT%d" % j])
                DMA(xt[j], C.x_d[t0:t0 + 128, :], W=["xt%d" % j], N=["xt%d" % j])
                for half in range(2):
                    pb = 2 * j + half
                    pn = "psD%d" % pb
                    Sd.buf(pn).new()
                    for e in range(8):
                        PE("matmul", PSB(pb), mixT[j][:, e, :], wout[:, e, half * 512:(half + 1) * 512],
                           start=(e == 0), stop=(e == 7), R=["mixT%d" % j, "wout"], W=[pn])
                rn = "rr%d" % j
                Sd.buf(rn).new()
                for half in range(2):
                    pb = 2 * j + half
                    DVE("scalar_tensor_tensor", rr[j][:, half * 512:(half + 1) * 512], xt[j][:, half * 512:(half + 1) * 512],
                        ALPHA, PSB(pb), ALU.mult, ALU.add, R=["xt%d" % j, "psD%d" % pb], W=[rn])
                Sd.buf("st").new()
                for half in range(2):
                    DVE("bn_stats", st[:, half, :], rr[j][:, half * 512:(half + 1) * 512], R=[rn], W=["st"])
                DVE("bn_aggr", mvv, st.rearrange("p a b -> p (a b)"), R=["st"], W=["mvv"], N=["mvv"])
                DVE("tensor_scalar", rs_, mvv[:, 1:2], LN_EPS, None, ALU.add, R=["mvv"], W=["rs"], N=["rs"])
                ACT("activation", rs_, rs_, AF.Sqrt, R=["rs"], W=["rs"], N=["rs"])
                DVE("reciprocal", rs_, rs_, R=["rs"], W=["rs"], N=["rs"])
                DVE("tensor_scalar", rr[j], rr[j], mvv[:, 0:1], rs_[:, 0:1], ALU.subtract, ALU.mult,
                    R=["mvv", "rs", rn], W=[rn], N=[rn])
                POOL("tensor_tensor", rr[j], rr[j], lng, ALU.mult, R=[rn, "lng"], W=[rn], N=[rn])
                POOL("tensor_tensor", rr[j], rr[j], lnb, ALU.add, R=[rn, "lnb"], W=[rn], N=[rn])
                DMA(C.out_d[t0:t0 + 128, :], rr[j], R=[rn])
            Sd.barrier()
            AR.release(m0)

        for l in range(NL):
            set_layer(l)
            layer_consts()
            if "C" in phases:
                s5_setup()
            for seq in range(NSEQ):
                if "A" in phases:
                    phase_A(seq)
                if "B" in phases:
                    phase_B(seq)
                if "C" in phases:
                    phase_C(seq)
                if "D" in phases:
                    phase_D(seq)
        Sd.barrier()
        Sd.cnt["sp"] += 1

        def fin(e):
            return e.nop()
        Sd._emit("sp", fin, {}, (Sd.sem["sp"], 1))

        with nc.Block() as block:
            @block.tensor
            def _(e):
                for th in Sd.thunks["pe"]:
                    th(e)

            @block.scalar
            def _(e):
                for th in Sd.thunks["act"]:
                    th(e)

            @block.vector
            def _(e):
                for th in Sd.thunks["dve"]:
                    th(e)

            @block.gpsimd
            def _(e):
                for th in Sd.thunks["pool"]:
                    th(e)

            @block.sync
            def _(e):
                for th in Sd.thunks["sp"]:
                    th(e)
    return nc


def _consts():
    ident = np.eye(128, dtype=np.float32)
    t = np.arange(128)
    caus = np.where(t[None, :] <= t[:, None], 0.0, NEG).astype(np.float32)
    jj = t // 16
    tri = (jj[None, :] >= jj[:, None]).astype(np.float32)
    dg = np.zeros((128, 8, 352), np.float32)
    for g in range(8):
        for r in range(16 * g, 16 * g + 16):
            dg[r, g, 112 + r] = 1.0
    powers = [-(j + 1) for j in range(8)] + [tau + 1 for tau in range(8)] + [7 - j for j in range(8)]
    mvals = np.tile(np.asarray(powers, np.float32)[None, :], (128, 1))
    kk = np.tile(np.arange(1, 513, dtype=np.float32)[None, :], (128, 1))
    return dict(ident=ident, caus=caus, tri=tri, dg=dg, mvals=mvals, kk=kk)


def _ptile(a):
    rest = a.shape[2:]
    a = a.reshape((16, 2, 64) + rest)
    perm = (1, 2, 0) + tuple(range(3, 3 + len(rest)))
    return np.ascontiguousarray(a.transpose(perm).reshape((128, 16) + rest))


def prep_layer(l, inp):
    f = np.float32
    w_in = np.asarray(inp["w_in"][l], f)
    zeros32 = np.zeros((D, 32), f)
    q = w_in[:, 0:512]
    ckv = w_in[:, 512:640]
    qidx = w_in[:, 640:896]
    kidx = w_in[:, 896:928]
    widx = w_in[:, 928:936]
    ga = w_in[:, 936:1448]
    u = w_in[:, 1448:1960]
    gs = w_in[:, 1960:2472]
    qh = [qidx[:, 32 * h:32 * h + 32] for h in range(8)]
    idxA = np.concatenate([qh[0], qh[1], qh[2], zeros32], 1)
    idxB = np.concatenate([qh[3], qh[4], qh[5], zeros32], 1)
    idxC = np.concatenate([qh[6], qh[7], zeros32, zeros32], 1)
    k0_ = np.concatenate([kidx, zeros32, zeros32, zeros32], 1)
    k1_ = np.concatenate([zeros32, kidx, zeros32, zeros32], 1)
    k2_ = np.concatenate([zeros32, zeros32, kidx, zeros32], 1)
    wfm = np.concatenate([q, idxA, idxB, idxC, k0_, k1_, k2_, ga, u, gs], 1)
    wtm = np.concatenate([ckv, widx, qidx], 1)
    tile_k = lambda m: np.ascontiguousarray(m.reshape(8, 128, m.shape[1]).transpose(1, 0, 2))
    d = {}
    d["wf"] = tile_k(wfm)
    d["wt"] = tile_k(wtm)
    w_uk = np.asarray(inp["w_uk"][l], f)
    w_uv = np.asarray(inp["w_uv"][l], f)
    wukz = np.zeros((128, 8, 128), f)
    for h_ in range(8):
        e_ = h_ % 2
        wukz[64 * e_:64 * e_ + 64, h_, :] = w_uk[h_].T
    d["wuk"] = wukz
    d["wuv"] = np.ascontiguousarray(w_uv.transpose(1, 0, 2))
    g = np.asarray(inp["kv_norm_g"][l], f)
    d["kvg_bc"] = np.ascontiguousarray(np.tile(g[None, :], (128, 1)))
    d["kvg_col"] = np.ascontiguousarray(g[:, None])
    d["ldt"] = _ptile(np.tile(np.asarray(inp["log_dt"][l], f)[:, None], (1, 64)))
    d["are"] = _ptile(np.asarray(inp["a_re"][l], f))
    d["aim"] = _ptile(np.asarray(inp["a_im"][l], f))
    d["bre"] = _ptile(np.asarray(inp["b_re"][l], f))
    d["bim"] = _ptile(np.asarray(inp["b_im"][l], f))
    d["cre"] = _ptile(np.asarray(inp["c_re"][l], f).transpose(0, 2, 1))
    d["cim"] = _ptile(np.asarray(inp["c_im"][l], f).transpose(0, 2, 1))
    dsk = np.asarray(inp["d_skip"][l], f)
    d["dsk"] = np.ascontiguousarray(np.tile(dsk.T, (8, 1)))
    d["wglu"] = np.ascontiguousarray(np.asarray(inp["w_glu"][l], f).reshape(4, 128, 1024).transpose(1, 0, 2))
    d["bglu"] = np.ascontiguousarray(np.asarray(inp["b_glu"][l], f).reshape(8, 128).T)
    d["wout"] = np.ascontiguousarray(np.asarray(inp["w_out"][l], f).reshape(8, 128, 1024).transpose(1, 0, 2))
    d["lng"] = np.ascontiguousarray(np.tile(np.asarray(inp["ln_g"][l], f)[None, :], (128, 1)))
    d["lnb"] = np.ascontiguousarray(np.tile(np.asarray(inp["ln_b"][l], f)[None, :], (128, 1)))
    d.update(_consts())
    return d


def kernel(**inputs):
    x = np.ascontiguousarray(np.asarray(inputs["x"], np.float32))
    B = x.shape[0]
    h = x.reshape(NCORES, T, D)
    per = [prep_layer(l, inputs) for l in range(DEPTH)]
    cn = _consts()
    w = {k: np.ascontiguousarray(np.stack([p[k] for p in per])) for k in per[0] if k not in cn}
    w.update(cn)
    in_maps = [dict(w, x=np.ascontiguousarray(h[c])) for c in range(NCORES)]
    nc = build_model()
    res = run_bass_kernel_spmd(nc, in_maps, core_ids=list(range(NCORES)))
    out = np.stack([np.asarray(r["out"], np.float32) for r in res.results])
    return out.reshape(B, S, D).astype(np.float32)
```

```python
import math
import os
from contextlib import ExitStack

import numpy as np
import concourse.bass as bass
import concourse.mybir as mybir
from concourse.bass_utils import run_bass_kernel_spmd

F32 = mybir.dt.float32
BF16 = mybir.dt.bfloat16
F16 = mybir.dt.float16
I32 = mybir.dt.int32
U8 = mybir.dt.uint8
AF = mybir.ActivationFunctionType
ALU = mybir.AluOpType
AX = mybir.AxisListType

DEPTH = 4
NCORES = 8
NSEQ = 2
S = 4096
T = NSEQ * S
D = 1024
ALPHA = (2 * DEPTH) ** 0.25
LN_EPS = 1e-5
RMS_EPS = 1e-6
NEG = -60000.0
RB = 16.0
NIT = 15
PI = math.pi
DSZ = {F32: 4, BF16: 2, F16: 2, I32: 4, U8: 1}


def merge(*ds):
    out = {}
    for d in ds:
        if not d:
            continue
        for k, v in d.items():
            if out.get(k, 0) < v:
                out[k] = v
    return out


class Buf:
    def __init__(self):
        self.prev = {}
        self.ws = {}
        self.rs = {}

    def new(self):
        self.prev = merge(self.prev, self.rs, self.ws)
        self.ws = {}
        self.rs = {}


class Sched:
    ENG = ("pe", "act", "dve", "pool", "sp")

    def __init__(self, nc, es):
        self.nc = nc
        self.sem = {k: es.enter_context(nc.semaphore("s_" + k)) for k in self.ENG}
        self.cnt = {k: 0 for k in self.ENG}
        self.thunks = {k: [] for k in self.ENG}
        self.seen = {k: {} for k in self.ENG}
        self.pending = {k: {} for k in self.ENG}
        self.NDMA = 16
        for i in range(self.NDMA):
            self.sem["d%d" % i] = es.enter_context(nc.semaphore("s_d%d" % i))
        self.dcnt = [0] * self.NDMA
        self.dnext = 0
        self.bufs = {}
        self.ninst = 0

    def buf(self, name):
        b = self.bufs.get(name)
        if b is None:
            b = self.bufs[name] = Buf()
        return b

    def _emit(self, eng, fn, deps, own_inc):
        seen = self.seen[eng]
        deps = merge(deps, self.pending[eng])
        self.pending[eng] = {}
        waits = []
        for k, v in deps.items():
            if seen.get(k, 0) >= v:
                continue
            seen[k] = v
            waits.append((self.sem[k], v))
        sems = self.sem
        self.ninst += 1 + len(waits)

        def thunk(e, waits=waits, fn=fn, own_inc=own_inc):
            for s_, v_ in waits:
                e.wait_ge(s_, v_)
            ins = fn(e)
            ins.then_inc(own_inc[0], own_inc[1])

        self.thunks[eng].append(thunk)

    def op(self, eng, method, *args, R=(), W=(), N=(), deps=None, **kw):
        dr = merge(*[self.buf(n).ws for n in R])
        for n in N:
            self.buf(n).new()
        d = merge(deps, dr, *[self.buf(n).prev for n in W])
        self.cnt[eng] += 1
        tok = {eng: self.cnt[eng]}

        def fn(e, method=method, args=args, kw=kw):
            return getattr(e, method)(*args, **kw)

        self._emit(eng, fn, d, (self.sem[eng], 1))
        for n in R:
            b = self.buf(n)
            b.rs = merge(b.rs, tok)
        for n in W:
            b = self.buf(n)
            b.ws = merge(b.ws, tok)
        return tok

    def dma(self, out, in_, R=(), W=(), N=(), deps=None, q="sp"):
        dr = merge(*[self.buf(n).ws for n in R])
        for n in N:
            self.buf(n).new()
        s = self.dnext
        self.dnext = (s + 1) % self.NDMA
        key = "d%d" % s
        d = merge(deps, dr, *[self.buf(n).prev for n in W],
                  {key: 16 * self.dcnt[s]} if self.dcnt[s] else None)
        self.dcnt[s] += 1
        tok = {key: 16 * self.dcnt[s]}

        def fn(e, out=out, in_=in_):
            return e.dma_start(out=out, in_=in_)

        self._emit(q, fn, d, (self.sem[key], 16))
        for n in R:
            b = self.buf(n)
            b.rs = merge(b.rs, tok)
        for n in W:
            b = self.buf(n)
            b.ws = merge(b.ws, tok)
        return tok

    def all_tokens(self):
        t = {k: self.cnt[k] for k in self.ENG if self.cnt[k]}
        for i in range(self.NDMA):
            if self.dcnt[i]:
                t["d%d" % i] = 16 * self.dcnt[i]
        return t

    def barrier(self):
        t = self.all_tokens()
        for k in self.ENG:
            self.pending[k] = merge(self.pending[k], t)
        self.bufs = {}


class Arena:
    def __init__(self, t, size):
        self.t = t
        self.size = size
        self.off = 0

    def alloc(self, shape, dt):
        n = int(np.prod(shape)) * DSZ[dt]
        off = (self.off + 63) // 64 * 64
        assert off + n <= self.size, ("SBUF arena overflow", off, n, self.size)
        self.off = off + n
        ap = self.t[:, off:off + n].bitcast(dt)
        if len(shape) == 1:
            return ap
        names = " ".join("a%d" % i for i in range(len(shape)))
        kw = {"a%d" % i: int(shape[i]) for i in range(len(shape))}
        return ap.rearrange("p (%s) -> p %s" % (names, names), **kw)

    def mark(self):
        return self.off

    def release(self, m):
        self.off = m


def build_model(NL=DEPTH, debug=False, phases="ABCD"):
    nc = bass.Bass("TRN2", target_bir_lowering=False)

    def din(name, shape, dt=F32):
        return nc.dram_tensor(name, list(shape), dt, kind="ExternalInput").ap()

    class C:
        pass
    x_in = din("x", [T, D])
    out_final = nc.dram_tensor("out", [T, D], F32, kind="ExternalOutput").ap()
    xbufs = [nc.dram_tensor("xbuf%d" % i, [T, D], F32).ap() for i in range(2)]
    LAYERED = dict(wf=[128, 8, 2816], wt=[128, 8, 392], wuk=[128, 8, 128], wuv=[128, 8, 64], kvg_bc=[128, 128],
                   kvg_col=[128, 1], ldt=[128, 16], are=[128, 16], aim=[128, 16], bre=[128, 16, 16], bim=[128, 16, 16],
                   cre=[128, 16, 16], cim=[128, 16, 16], dsk=[128, 32], wglu=[128, 4, 1024], bglu=[128, 8],
                   wout=[128, 8, 1024], lng=[128, 1024], lnb=[128, 1024])
    LAY = {k: din(k, [NL] + v) for k, v in LAYERED.items()}

    def set_layer(l):
        C.x_d = x_in if l == 0 else xbufs[(l - 1) % 2]
        C.out_d = out_final if l == NL - 1 else xbufs[l % 2]
        for k in LAYERED:
            setattr(C, k.replace("_", "") + "_d", LAY[k][l])
    ident_d = din("ident", [128, 128])
    caus_d = din("caus", [128, 128])
    tri_d = din("tri", [128, 128])
    dg_d = din("dg", [128, 8, 352])
    mv_d = din("mvals", [128, 24])
    kk_d = din("kk", [128, 512])

    def scr(name, shape, dt):
        if debug and name in debug:
            return nc.dram_tensor(name, list(shape), dt, kind="ExternalOutput").ap()
        return nc.dram_tensor(name, list(shape), dt).ap()

    qT_d = scr("qT_s", [4, 128, T], BF16)
    idx_d = scr("idx_s", [6, 128, T], BF16)
    ga_d = scr("ga_s", [4, 128, T], BF16)
    u_d = scr("u_s", [4, 128, T], BF16)
    gs_d = scr("gs_s", [4, 128, T], BF16)
    chat_d = scr("chat_s", [T, 128], BF16)
    chatT_d = scr("chatT_s", [128, T], BF16)
    wn_d = scr("wn_s", [T, 8], F32)
    mix_d = scr("mix_s", [8, 128, T], BF16)
    dbgC = bool(debug) and "ygT_s" in debug
    if dbgC:
        ygT_dbg = scr("ygT_s", [4, 128, S], BF16)
        W_dbg = scr("W_s", [128, 32, 128], BF16)
        Vre_dbg = scr("Vre_s", [128, 16, 128], BF16)
        Vim_dbg = scr("Vim_s", [128, 16, 128], BF16)
        Wcre_dbg = scr("Wcre_s", [128, 16, 128], BF16)
        Wcim_dbg = scr("Wcim_s", [128, 16, 128], BF16)
        RT_dbg = scr("RT_s", [128, 32], F32)

    es = ExitStack()
    with es:
        SBSZ = 200 * 1024
        arena_t = es.enter_context(nc.sbuf_tensor("arena", [128, SBSZ], U8))
        AR = Arena(arena_t, SBSZ)
        psum = es.enter_context(nc.psum_tensor("ps", [128, 8, 512], F32))
        Sd = Sched(nc, es)

        def PSB(b, w=512):
            return psum[:, b, 0:w]

        def PE(m, *a, **k):
            return Sd.op("pe", m, *a, **k)

        def ACT(m, *a, **k):
            return Sd.op("act", m, *a, **k)

        def DVE(m, *a, **k):
            return Sd.op("dve", m, *a, **k)

        def POOL(m, *a, **k):
            return Sd.op("pool", m, *a, **k)

        DMA = Sd.dma

        identf = AR.alloc([128], F32)
        identb = AR.alloc([128], BF16)
        onesb = AR.alloc([128], BF16)
        causf = AR.alloc([128], F32)
        trif = AR.alloc([128], F32)
        dgb = AR.alloc([8, 352], BF16)
        wukp = AR.alloc([8, 128], BF16)
        wuvp = AR.alloc([8, 64], BF16)
        kvgbc = AR.alloc([128], F32)
        kvgcol = AR.alloc([1], F32)
        W_sb = AR.alloc([32, 128], BF16)
        Wc_re = AR.alloc([16, 128], BF16)
        Wc_im = AR.alloc([16, 128], BF16)
        V_re = AR.alloc([16, 128], BF16)
        V_im = AR.alloc([16, 128], BF16)
        Rsc = AR.alloc([16], F32)
        Thr = AR.alloc([16], F32)
        kkf = AR.alloc([512], F32)
        bglu = AR.alloc([8], F32)
        pm = AR.mark()

        stgA = AR.alloc([8, 352], F32)
        DMA(identf, ident_d, W=["identf"], N=["identf"])
        DMA(causf, caus_d, W=["causf"], N=["causf"])
        DMA(trif, tri_d, W=["trif"], N=["trif"])
        DMA(kkf, kk_d, W=["kkf"], N=["kkf"])
        DMA(stgA, dg_d, W=["stgA"], N=["stgA"])
        POOL("tensor_copy", identb, identf, R=["identf"], W=["identb"], N=["identb"])
        POOL("memset", onesb, 1.0, W=["onesb"], N=["onesb"])
        POOL("tensor_copy", dgb, stgA, R=["stgA"], W=["dgb"], N=["dgb"])
        Sd.barrier()
        AR.release(pm)

        def layer_consts():
            m0 = AR.mark()
            stgB = AR.alloc([8, 64], F32)
            stgC = AR.alloc([8, 128], F32)
            DMA(kvgbc, C.kvgbc_d, W=["kvgbc"], N=["kvgbc"])
            DMA(kvgcol, C.kvgcol_d, W=["kvgcol"], N=["kvgcol"])
            DMA(bglu, C.bglu_d, W=["bglu"], N=["bglu"])
            DMA(stgB, C.wuv_d, W=["stgB"], N=["stgB"])
            DMA(stgC, C.wuk_d, W=["stgC"], N=["stgC"])
            DVE("tensor_scalar", wuvp, stgB, kvgcol[:, 0:1], None, ALU.mult, R=["stgB", "kvgcol"], W=["wuvp"], N=["wuvp"])
            DVE("scalar_tensor_tensor", wukp, stgC, 0.125, kvgbc.unsqueeze(1).to_broadcast([128, 8, 128]),
                ALU.mult, ALU.mult, R=["stgC", "kvgbc"], W=["wukp"], N=["wukp"])
            Sd.barrier()
            AR.release(m0)

        def sincos(ang, F, out_cos, out_sin, tmp_f, tmp_i, tmp_g, tag):
            bn = lambda s: tag + s
            DVE("tensor_scalar", tmp_f, ang, 1.0 / (2 * PI), None, ALU.mult, R=[bn("ang")], W=[bn("tf")], N=[bn("tf")])
            DVE("tensor_copy", tmp_i, tmp_f, R=[bn("tf")], W=[bn("ti")], N=[bn("ti")])
            DVE("tensor_copy", tmp_f, tmp_i, R=[bn("ti")], W=[bn("tf")], N=[bn("tf")])
            DVE("scalar_tensor_tensor", out_sin, tmp_f, -2 * PI, ang, ALU.mult, ALU.add,
                R=[bn("tf"), bn("ang")], W=[bn("r")], N=[bn("r")])
            for (thr, cmp, adj) in ((PI, ALU.is_gt, -2 * PI), (-PI, ALU.is_lt, 2 * PI)):
                DVE("tensor_scalar", tmp_g, out_sin, thr, adj, cmp, ALU.mult, R=[bn("r")], W=[bn("tg")], N=[bn("tg")])
                DVE("tensor_tensor", out_sin, out_sin, tmp_g, ALU.add, R=[bn("tg")], W=[bn("r")], N=[bn("r")])
            DVE("tensor_scalar", out_cos, out_sin, PI / 2, None, ALU.add, R=[bn("r")], W=[bn("rc")], N=[bn("rc")])
            DVE("tensor_scalar", tmp_g, out_cos, PI, -2 * PI, ALU.is_gt, ALU.mult, R=[bn("rc")], W=[bn("tg")], N=[bn("tg")])
            DVE("tensor_tensor", out_cos, out_cos, tmp_g, ALU.add, R=[bn("tg")], W=[bn("rc")], N=[bn("rc")])
            DVE("tensor_scalar", out_sin, out_sin, PI, -PI, ALU.min, ALU.max, R=[bn("r")], W=[bn("r")], N=[bn("r")])
            DVE("tensor_scalar", out_cos, out_cos, PI, -PI, ALU.min, ALU.max, R=[bn("rc")], W=[bn("rc")], N=[bn("rc")])
            ACT("activation", out_sin, out_sin, AF.Sin, R=[bn("r")], W=[bn("sin")], N=[bn("sin"), bn("r")])
            ACT("activation", out_cos, out_cos, AF.Sin, R=[bn("rc")], W=[bn("cos")], N=[bn("cos"), bn("rc")])

        def s5_setup():
            m0 = AR.mark()
            ldt = AR.alloc([16], F32)
            are = AR.alloc([16], F32)
            aim = AR.alloc([16], F32)
            bre = AR.alloc([16, 16], F32)
            bim = AR.alloc([16, 16], F32)
            cre = AR.alloc([16, 16], F32)
            cim = AR.alloc([16, 16], F32)
            dsk = AR.alloc([32], F32)
            mv = AR.alloc([24], F32)
            for ap, d_, n in ((ldt, C.ldt_d, "ldt"), (are, C.are_d, "are"), (aim, C.aim_d, "aim"), (bre, C.bre_d, "bre"),
                              (bim, C.bim_d, "bim"), (cre, C.cre_d, "cre"), (cim, C.cim_d, "cim"), (dsk, C.dsk_d, "dsk"),
                              (mv, mv_d, "mv")):
                DMA(ap, d_, W=[n], N=[n])
            dt = AR.alloc([16], F32)
            dre = AR.alloc([16], F32)
            dim = AR.alloc([16], F32)
            ACT("activation", dt, ldt, AF.Exp, R=["ldt"], W=["dt"], N=["dt"])
            DVE("tensor_tensor", dre, are, dt, ALU.mult, R=["are", "dt"], W=["dre"], N=["dre"])
            DVE("tensor_tensor", dim, aim, dt, ALU.mult, R=["aim", "dt"], W=["dim"], N=["dim"])
            ACT("activation", Rsc, dre, AF.Exp, scale=8.0, R=["dre"], W=["Rsc"], N=["Rsc"])
            th8 = AR.alloc([16], F32)
            tq = AR.alloc([16], F32)
            tqi = AR.alloc([16], I32)
            tg = AR.alloc([16], F32)
            DVE("tensor_scalar", th8, dim, 8.0, None, ALU.mult, R=["dim"], W=["th8"], N=["th8"])
            DVE("tensor_scalar", tq, th8, 1.0 / (2 * PI), None, ALU.mult, R=["th8"], W=["tq"], N=["tq"])
            DVE("tensor_copy", tqi, tq, R=["tq"], W=["tqi"], N=["tqi"])
            DVE("tensor_copy", tq, tqi, R=["tqi"], W=["tq"], N=["tq"])
            DVE("scalar_tensor_tensor", Thr, tq, -2 * PI, th8, ALU.mult, ALU.add, R=["tq", "th8"], W=["Thr"], N=["Thr"])
            for (thr, cmp, adj) in ((PI, ALU.is_gt, -2 * PI), (-PI, ALU.is_lt, 2 * PI)):
                DVE("tensor_scalar", tg, Thr, thr, adj, cmp, ALU.mult, R=["Thr"], W=["tg8"], N=["tg8"])
                DVE("tensor_tensor", Thr, Thr, tg, ALU.add, R=["tg8"], W=["Thr"], N=["Thr"])
            F = 16 * 24
            mre = AR.alloc([16, 24], F32)
            ang = AR.alloc([16, 24], F32)
            Ere = AR.alloc([16, 24], F32)
            Eim = AR.alloc([16, 24], F32)
            tf_ = AR.alloc([16, 24], F32)
            tg_ = AR.alloc([16, 24], F32)
            ti_ = AR.alloc([16, 24], I32)
            mvb = mv.unsqueeze(1).to_broadcast([128, 16, 24])
            DVE("tensor_tensor", mre, dre.unsqueeze(2).to_broadcast([128, 16, 24]), mvb, ALU.mult,
                R=["dre", "mv"], W=["mre"], N=["mre"])
            DVE("tensor_tensor", ang, dim.unsqueeze(2).to_broadcast([128, 16, 24]), mvb, ALU.mult,
                R=["dim", "mv"], W=["Eang"], N=["Eang"])
            ACT("activation", mre, mre, AF.Exp, R=["mre"], W=["mre"], N=["mre"])
            sincos(ang, F, Ere, Eim, tf_, ti_, tg_, "E")
            DVE("tensor_tensor", Ere, Ere, mre, ALU.mult, R=["Ecos", "mre"], W=["Ere"], N=["Ere", "Ecos"])
            DVE("tensor_tensor", Eim, Eim, mre, ALU.mult, R=["Esin", "mre"], W=["Eim"], N=["Eim", "Esin"])
            nr = AR.alloc([16], F32)
            den = AR.alloc([16], F32)
            t1 = AR.alloc([16], F32)
            t2 = AR.alloc([16], F32)
            fre = AR.alloc([16], F32)
            fim = AR.alloc([16], F32)
            e1r = Ere[:, :, 8]
            e1i = Eim[:, :, 8]
            DVE("tensor_scalar", nr, e1r, -1.0, None, ALU.add, R=["Ere"], W=["nr"], N=["nr"])
            DVE("tensor_tensor", t1, are, are, ALU.mult, R=["are"], W=["t1"], N=["t1"])
            DVE("tensor_tensor", t2, aim, aim, ALU.mult, R=["aim"], W=["t2"], N=["t2"])
            DVE("tensor_tensor", den, t1, t2, ALU.add, R=["t1", "t2"], W=["den"], N=["den"])
            DVE("reciprocal", den, den, R=["den"], W=["den"], N=["den"])
            DVE("tensor_tensor", t1, nr, are, ALU.mult, R=["nr", "are"], W=["t1"], N=["t1"])
            DVE("tensor_tensor", t2, e1i, aim, ALU.mult, R=["Eim", "aim"], W=["t2"], N=["t2"])
            DVE("tensor_tensor", fre, t1, t2, ALU.add, R=["t1", "t2"], W=["fre"], N=["fre"])
            DVE("tensor_tensor", fre, fre, den, ALU.mult, R=["den"], W=["fre"], N=["fre"])
            DVE("tensor_tensor", t1, e1i, are, ALU.mult, R=["Eim", "are"], W=["t1"], N=["t1"])
            DVE("tensor_tensor", t2, nr, aim, ALU.mult, R=["nr", "aim"], W=["t2"], N=["t2"])
            DVE("tensor_tensor", fim, t1, t2, ALU.subtract, R=["t1", "t2"], W=["fim"], N=["fim"])
            DVE("tensor_tensor", fim, fim, den, ALU.mult, R=["den"], W=["fim"], N=["fim"])
            Bre = AR.alloc([16, 16], F32)
            Bim = AR.alloc([16, 16], F32)
            u1 = AR.alloc([16, 16], F32)
            freb = fre.unsqueeze(2).to_broadcast([128, 16, 16])
            fimb = fim.unsqueeze(2).to_broadcast([128, 16, 16])
            DVE("tensor_tensor", Bre, bre, freb, ALU.mult, R=["bre", "fre"], W=["Bre"], N=["Bre"])
            DVE("tensor_tensor", u1, bim, fimb, ALU.mult, R=["bim", "fim"], W=["u1"], N=["u1"])
            DVE("tensor_tensor", Bre, Bre, u1, ALU.subtract, R=["u1"], W=["Bre"], N=["Bre"])
            DVE("tensor_tensor", Bim, bim, freb, ALU.mult, R=["bim", "fre"], W=["Bim"], N=["Bim"])
            DVE("tensor_tensor", u1, bre, fimb, ALU.mult, R=["bre", "fim"], W=["u1"], N=["u1"])
            DVE("tensor_tensor", Bim, Bim, u1, ALU.add, R=["u1"], W=["Bim"], N=["Bim"])

            big = [16, 8, 16]
            tA = AR.alloc(big, F32)
            tB = AR.alloc(big, F32)

            def cprod(o_re, o_im, sl, yre, yim, yn_re, yn_im, neg_im, tag):
                er = Ere[:, :, sl].unsqueeze(3).to_broadcast([128, 16, 8, 16])
                ei = Eim[:, :, sl].unsqueeze(3).to_broadcast([128, 16, 8, 16])
                yr = yre.unsqueeze(2).to_broadcast([128, 16, 8, 16])
                yi = yim.unsqueeze(2).to_broadcast([128, 16, 8, 16])
                DVE("tensor_tensor", tA, er, yr, ALU.mult, R=["Ere", yn_re], W=["tA"], N=["tA"])
                DVE("tensor_tensor", tB, ei, yi, ALU.mult, R=["Eim", yn_im], W=["tB"], N=["tB"])
                DVE("tensor_tensor", o_re, tA, tB, ALU.subtract, R=["tA", "tB"], W=[tag + "re"], N=[tag + "re"])
                DVE("tensor_tensor", tA, er, yi, ALU.mult, R=["Ere", yn_im], W=["tA"], N=["tA"])
                DVE("tensor_tensor", tB, ei, yr, ALU.mult, R=["Eim", yn_re], W=["tB"], N=["tB"])
                if neg_im:
                    DVE("scalar_tensor_tensor", o_im, tA, -1.0, tB, ALU.mult, ALU.subtract,
                        R=["tA", "tB"], W=[tag + "im"], N=[tag + "im"])
                else:
                    DVE("tensor_tensor", o_im, tA, tB, ALU.add, R=["tA", "tB"], W=[tag + "im"], N=[tag + "im"])

            Bq_re = AR.alloc(big, F32)
            Bq_im = AR.alloc(big, F32)
            Cq_re = AR.alloc(big, F32)
            Cq_im = AR.alloc(big, F32)
            cprod(Bq_re, Bq_im, slice(0, 8), Bre, Bim, "Bre", "Bim", False, "Bq")
            cprod(Cq_re, Cq_im, slice(8, 16), cre, cim, "cre", "cim", True, "Cq")
            POOL("tensor_copy", Wc_re, Cq_re.rearrange("p i a b -> p i (a b)"), R=["Cqre"], W=["Wc_re"], N=["Wc_re"])
            POOL("tensor_copy", Wc_im, Cq_im.rearrange("p i a b -> p i (a b)"), R=["Cqim"], W=["Wc_im"], N=["Wc_im"])
            Bqm_re = [AR.alloc([16, 128], BF16) for _ in range(2)]
            Bqm_im = [AR.alloc([16, 128], BF16) for _ in range(2)]
            Sd.buf("Bqreb").new()
            Sd.buf("Bqimb").new()
            for e_ in range(2):
                o_ = slice(64 * (1 - e_), 64 * (1 - e_) + 64)
                k_ = slice(64 * e_, 64 * e_ + 64)
                POOL("memset", Bqm_re[e_][o_], 0.0, W=["Bqreb"])
                POOL("memset", Bqm_im[e_][o_], 0.0, W=["Bqimb"])
                POOL("tensor_copy", Bqm_re[e_][k_], Bq_re[k_].rearrange("p i a b -> p i (a b)"), R=["Bqre"], W=["Bqreb"])
                POOL("tensor_copy", Bqm_im[e_][k_], Bq_im[k_].rearrange("p i a b -> p i (a b)"), R=["Bqim"], W=["Bqimb"])
            tW = AR.alloc([128], F32)
            for g in range(32):
                i, e = g // 2, g % 2
                pb = 4 + (g % 2)
                sl = slice(64 * e, 64 * e + 64)
                PE("matmul", PSB(pb, 128), Bqm_re[e][:, i, :], Wc_re[:, i, :], start=True, stop=False,
                   R=["Bqreb", "Wc_re"], W=["psW%d" % pb], N=["psW%d" % pb])
                PE("matmul", PSB(pb, 128), Bqm_im[e][:, i, :], Wc_im[:, i, :], start=False, stop=True,
                   R=["Bqimb", "Wc_im"], W=["psW%d" % pb])
                DVE("tensor_tensor", tW, PSB(pb, 128), trif, ALU.mult, R=["psW%d" % pb, "trif"], W=["tW"], N=["tW"])
                DVE("scalar_tensor_tensor", W_sb[:, g, :], identf, dsk[:, g:g + 1], tW, ALU.mult, ALU.add,
                    R=["tW", "identf", "dsk"], W=["W_sb"])
            cprod(Bq_re, Bq_im, slice(16, 24), Bre, Bim, "Bre", "Bim", False, "Bq")
            for i in range(16):
                for (src, dst, sn, dn) in ((Bq_re, V_re, "Bqre", "V_re"), (Bq_im, V_im, "Bqim", "V_im")):
                    pb = 6 + (i % 2)
                    PE("transpose", PSB(pb, 128), src[:, i, :, :].rearrange("p a b -> p (a b)"), identf,
                       R=[sn, "identf"], W=["psV%d" % pb], N=["psV%d" % pb])
                    ACT("activation", dst[:, i, :], PSB(pb, 128), AF.Copy, R=["psV%d" % pb], W=[dn])
            if dbgC:
                DMA(W_dbg, W_sb, R=["W_sb"])
                DMA(Vre_dbg, V_re, R=["V_re"])
                DMA(Vim_dbg, V_im, R=["V_im"])
                DMA(Wcre_dbg, Wc_re, R=["Wc_re"])
                DMA(Wcim_dbg, Wc_im, R=["Wc_im"])
                DMA(RT_dbg[:, 0:16], Rsc, R=["Rsc"])
                DMA(RT_dbg[:, 16:32], Thr, R=["Thr"])
            Sd.barrier()
            AR.release(m0)

        def phase_A(seq):
            m0 = AR.mark()
            wf = AR.alloc([8, 2816], BF16)
            wt = AR.alloc([8, 392], BF16)
            stg = [AR.alloc([2816], F32) for _ in range(2)]
            for k in range(8):
                b = k % 2
                DMA(stg[b], C.wf_d[:, k, :], W=["stg%d" % b], N=["stg%d" % b])
                POOL("tensor_copy", wf[:, k, :], stg[b], R=["stg%d" % b], W=["wf"])
            wts = AR.alloc([8, 392], F32)
            DMA(wts, C.wt_d, W=["wts"], N=["wts"])
            POOL("tensor_copy", wt, wts, R=["wts"], W=["wt"], N=["wt"])
            xtok = [AR.alloc([1024], F32) for _ in range(2)]
            xbf = [AR.alloc([1024], BF16) for _ in range(2)]
            xT = [AR.alloc([8, 512], BF16) for _ in range(2)]
            fm = AR.alloc([22, 512], BF16)
            sqj = AR.alloc([128], BF16)
            ss = AR.alloc([1], F32)
            vv = AR.alloc([1], F32)
            sv = AR.alloc([1], F32)
            rstd = AR.alloc([1], F32)
            chat_c = AR.alloc([4, 128], BF16)
            chatT_c = AR.alloc([512], BF16)
            wq = AR.alloc([264], F32)
            sq = AR.alloc([256], F32)
            qn = AR.alloc([8], F32)
            aw = AR.alloc([8], F32)
            s1 = AR.alloc([1], F32)
            wn_c = AR.alloc([4, 8], F32)
            for ch in range(8):
                t0c = seq * S + ch * 512
                xb = xT[ch % 2]
                xn = "xT%d" % (ch % 2)
                Sd.buf(xn).new()
                Sd.buf("chat_c").new()
                Sd.buf("chatT_c").new()
                Sd.buf("wn_c").new()
                for tt in range(4):
                    t0 = t0c + tt * 128
                    j = (ch * 4 + tt) % 2
                    DMA(xtok[j], C.x_d[t0:t0 + 128, :], W=["xtok%d" % j], N=["xtok%d" % j])
                    POOL("tensor_copy", xbf[j], xtok[j], R=["xtok%d" % j], W=["xbf%d" % j], N=["xbf%d" % j])
                    tp = PSB(6 + j).bitcast(BF16).rearrange("p (a b) -> p a b", a=8)
                    Sd.buf("pstp%d" % j).new()
                    for k in range(8):
                        PE("transpose", tp[:, k, :], xbf[j][:, k * 128:(k + 1) * 128], identb,
                           R=["xbf%d" % j, "identb"], W=["pstp%d" % j])
                    ACT("activation", xb[:, :, tt * 128:(tt + 1) * 128], tp, AF.Copy, R=["pstp%d" % j], W=[xn])
                    tokps = PSB(5, 392)
                    Sd.buf("tokps").new()
                    for k in range(8):
                        PE("matmul", tokps, xb[:, k, tt * 128:(tt + 1) * 128], wt[:, k, :], start=(k == 0), stop=(k == 7),
                           R=[xn, "wt"], W=["tokps"])
                    ACT("activation", sqj, tokps[:, 0:128], AF.Square, accum_out=ss[:, 0:1], R=["tokps"], W=["ss", "sqj"], N=["ss", "sqj"])
                    DVE("tensor_scalar", vv, ss, 1.0 / 128, RMS_EPS, ALU.mult, ALU.add, R=["ss"], W=["vv"], N=["vv"])
                    ACT("activation", sv, vv, AF.Sqrt, R=["vv"], W=["sv"], N=["sv"])
                    DVE("reciprocal", rstd, sv, R=["sv"], W=["rstd"], N=["rstd"])
                    ACT("activation", chat_c[:, tt, :], tokps[:, 0:128], AF.Copy, scale=rstd[:, 0:1],
                        R=["tokps", "rstd"], W=["chat_c"])
                    tp2 = PSB(4).bitcast(BF16)[:, 0:128]
                    PE("transpose", tp2, chat_c[:, tt, :], identb, R=["chat_c", "identb"], W=["pstp2"], N=["pstp2"])
                    DVE("tensor_copy", chatT_c[:, tt * 128:(tt + 1) * 128], tp2, R=["pstp2"], W=["chatT_c"])
                    ACT("activation", wq, tokps[:, 128:392], AF.Copy, R=["tokps"], W=["wq"], N=["wq"])
                    DVE("tensor_tensor", sq, wq[:, 8:264], wq[:, 8:264], ALU.mult, R=["wq"], W=["sq"], N=["sq"])
                    DVE("tensor_reduce", qn, sq.rearrange("p (h d) -> p h d", h=8), AX.X, ALU.add, R=["sq"], W=["qn"], N=["qn"])
                    ACT("activation", qn, qn, AF.Sqrt, R=["qn"], W=["qn"], N=["qn"])
                    DVE("scalar_tensor_tensor", aw, wq[:, 0:8], -1.0, wq[:, 0:8], ALU.mult, ALU.max, R=["wq"], W=["aw"], N=["aw"])
                    DVE("tensor_tensor", aw, aw, qn, ALU.mult, R=["qn"], W=["aw"], N=["aw"])
                    DVE("tensor_reduce", s1, aw, AX.X, ALU.add, R=["aw"], W=["s1"], N=["s1"])
                    DVE("tensor_scalar", s1, s1, 1e-30, None, ALU.add, R=["s1"], W=["s1"], N=["s1"])
                    DVE("reciprocal", s1, s1, R=["s1"], W=["s1"], N=["s1"])
                    DVE("tensor_scalar", wn_c[:, tt, :], wq[:, 0:8], s1[:, 0:1], None, ALU.mult, R=["wq", "s1"], W=["wn_c"])
                DMA(chat_d[t0c:t0c + 512, :].rearrange("(a p) c -> p a c", p=128), chat_c, R=["chat_c"])
                DMA(chatT_d[:, t0c:t0c + 512], chatT_c, R=["chatT_c"])
                DMA(wn_d[t0c:t0c + 512, :].rearrange("(a p) h -> p a h", p=128), wn_c, R=["wn_c"])
                Sd.buf("fm").new()
                for cb in range(22):
                    pb = cb % 4
                    pn = "psfm%d" % pb
                    Sd.buf(pn).new()
                    for k in range(8):
                        PE("matmul", PSB(pb), wf[:, k, cb * 128:(cb + 1) * 128], xb[:, k, :], start=(k == 0), stop=(k == 7),
                           R=["wf", xn], W=[pn])
                    silu = (10 <= cb < 14) or cb >= 18
                    if silu:
                        ACT("activation", fm[:, cb, :], PSB(pb), AF.Silu, R=[pn], W=["fm"])
                    elif cb % 2 == 0:
                        ACT("activation", fm[:, cb, :], PSB(pb), AF.Copy, R=[pn], W=["fm"])
                    else:
                        DVE("tensor_copy", fm[:, cb, :], PSB(pb), R=[pn], W=["fm"])
                for (dd, c0_, c1_) in ((qT_d, 0, 4), (idx_d, 4, 10), (ga_d, 10, 14), (u_d, 14, 18), (gs_d, 18, 22)):
                    DMA(dd[:, :, t0c:t0c + 512].rearrange("j p t -> p j t"), fm[:, c0_:c1_, :], R=["fm"])
            Sd.barrier()
            AR.release(m0)

        def phase_B(seq):
            m0 = AR.mark()
            s0 = seq * S
            kiT = AR.alloc([3, S], BF16)
            chatT = AR.alloc([S], BF16)
            chtok = AR.alloc([32, 128], BF16)
            DMA(kiT, idx_d[3:6, :, s0:s0 + S].rearrange("j p t -> p j t"), W=["kiT"], N=["kiT"])
            DMA(chatT, chatT_d[:, s0:s0 + S], W=["chatT"], N=["chatT"])
            DMA(chtok, chat_d[s0:s0 + S, :].rearrange("(a p) c -> p a c", p=128), W=["chtok"], N=["chtok"])
            I_sb = AR.alloc([S], F16)
            mask = AR.alloc([S], BF16)
            maskT = AR.alloc([32, 128], BF16)
            Rsb = [AR.alloc([512], BF16) for _ in range(3)]
            diagw = AR.alloc([8, 128], BF16)
            qlat = AR.alloc([8, 128], BF16)
            sqs = AR.alloc([8, 128], BF16)
            esb = [AR.alloc([512], BF16) for _ in range(2)]
            psb_ = [AR.alloc([512], BF16) for _ in range(2)]
            rl = AR.alloc([512], F32)
            on = AR.alloc([512], BF16)
            qTb = AR.alloc([4, 128], BF16)
            idxq = AR.alloc([3, 128], BF16)
            wnb = AR.alloc([8], F32)
            gab = AR.alloc([4, 128], BF16)
            mixo = AR.alloc([4, 128], BF16)
            m2 = AR.alloc([1], F32)
            negB = AR.alloc([1], F32)
            mid = [AR.alloc([1], F32) for _ in range(2)]
            cnt = AR.alloc([1], F32)
            mm_ = AR.alloc([1], F32)
            BST = int(os.environ.get('BST', '9'))
            for qb in range(int(os.environ.get('NQB', '32'))):
                t0 = s0 + qb * 128
                N = (qb + 1) * 128
                DMA(qTb, qT_d[:, :, t0:t0 + 128].rearrange("j p t -> p j t"), W=["qTb"], N=["qTb"])
                DMA(idxq, idx_d[0:3, :, t0:t0 + 128].rearrange("j p t -> p j t"), W=["idxq"], N=["idxq"])
                DMA(wnb, wn_d[t0:t0 + 128, :], W=["wnb"], N=["wnb"])
                DMA(gab, ga_d[:, :, t0:t0 + 128].rearrange("j p t -> p j t"), W=["gab"], N=["gab"])
                qlps = psum[:, 0:2, :].rearrange("p b (h t) -> p (b h) t", h=4)
                Sd.buf("ps01").new()
                for h in range(8):
                    j, e = h // 2, h % 2
                    sl = slice(64 * e, 64 * e + 64)
                    PE("matmul", qlps[:, h, :], wukp[:, h, :], qTb[:, j, :], start=True, stop=True,
                       R=["qTb", "wukp"], W=["ps01"])
                ACT("activation", qlat, qlps, AF.Copy, R=["ps01"], W=["qlat"], N=["qlat"])
                ACT("activation", sqs, qlps, AF.Square, R=["ps01"], W=["sqs"], N=["sqs"])
                Sd.buf("ps01").new()
                for b in range(2):
                    PE("matmul", PSB(b), onesb, sqs[:, 4 * b:4 * b + 4, :].rearrange("p h t -> p (h t)"),
                       start=True, stop=True, R=["sqs", "onesb"], W=["ps01"])
                DVE("tensor_reduce", m2, psum[:, 0:2, :].rearrange("p b f -> p (b f)"), AX.X, ALU.max,
                    R=["ps01"], W=["m2"], N=["m2"])
                ACT("activation", negB, m2, AF.Sqrt, scale=128.0 * 1.03, R=["m2"], W=["negB"], N=["negB"])
                DVE("tensor_scalar", negB, negB, -1.0, None, ALU.mult, R=["negB"], W=["negB"], N=["negB"])
                if BST < 2:
                    continue
                DVE("tensor_tensor", diagw, identf.unsqueeze(1).to_broadcast([128, 8, 128]),
                    wnb.unsqueeze(2).to_broadcast([128, 8, 128]), ALU.mult, R=["identf", "wnb"], W=["diagw"], N=["diagw"])
                nkc = (N + 511) // 512
                Sd.buf("I_sb").new()
                ridx = 0
                for kc in range(nkc):
                    w = min(512, N - kc * 512)
                    k0 = kc * 512
                    Sd.buf("psI").new()
                    pend = None
                    for h in range(8):
                        tl, r = h // 3, h % 3
                        pb = 2 + (h % 2)
                        pn = "psL%d" % pb
                        PE("matmul", PSB(pb, w), idxq[:, tl, :], kiT[:, r, k0:k0 + w], start=True, stop=True,
                           R=["idxq", "kiT"], W=[pn], N=[pn])
                        if pend is not None:
                            ph, prb, prn = pend
                            PE("matmul", PSB(4, w), diagw[:, ph, :], Rsb[prb][:, 0:w], start=(ph == 0), stop=False,
                               R=[prn, "diagw"], W=["psI"])
                        rb = ridx % 3
                        ridx += 1
                        rn = "Rsb%d" % rb
                        if h % 2 == 0:
                            ACT("activation", Rsb[rb][:, 0:w], PSB(pb, w), AF.Relu, R=[pn], W=[rn], N=[rn])
                        else:
                            DVE("tensor_scalar", Rsb[rb][:, 0:w], PSB(pb, w), 0.0, None, ALU.max, R=[pn], W=[rn], N=[rn])
                        pend = (h, rb, rn)
                    ph, prb, prn = pend
                    PE("matmul", PSB(4, w), diagw[:, ph, :], Rsb[prb][:, 0:w], start=False, stop=True,
                       R=[prn, "diagw"], W=["psI"])
                    if kc == nkc - 1:
                        wd = w - 128
                        if wd > 0:
                            DVE("tensor_copy", I_sb[:, k0:k0 + wd], PSB(4, wd), R=["psI"], W=["I_sb"])
                        DVE("tensor_tensor", I_sb[:, k0 + wd:k0 + w], psum[:, 4, wd:w], causf, ALU.add,
                            R=["psI", "causf"], W=["I_sb"])
                    else:
                        DVE("tensor_copy", I_sb[:, k0:k0 + w], PSB(4, w), R=["psI"], W=["I_sb"])
                if BST < 3:
                    continue
                DVE("memset", mid[0], 0.0, W=["mid0"], N=["mid0"])
                h_k = RB
                for it in range(NIT):
                    a, b = it % 2, (it + 1) % 2
                    DVE("tensor_scalar", mask[:, 0:N], I_sb[:, 0:N], mid[a][:, 0:1], 0.0, ALU.is_ge, ALU.add,
                        accum_out=cnt[:, 0:1], R=["I_sb", "mid%d" % a], W=["mask", "cnt"], N=["mask", "cnt"])
                    DVE("tensor_scalar", mm_, cnt, 255.5, h_k, ALU.is_ge, ALU.mult, R=["cnt"], W=["mm"], N=["mm"])
                    DVE("scalar_tensor_tensor", mid[b], mm_, -h_k / 2, mid[a], ALU.add, ALU.add,
                        R=["mm", "mid%d" % a], W=["mid%d" % b], N=["mid%d" % b])
                    h_k = h_k / 2
                fin = NIT % 2
                DVE("tensor_scalar", mask[:, 0:N], I_sb[:, 0:N], mid[fin][:, 0:1], -h_k, ALU.subtract, ALU.is_ge,
                    R=["I_sb", "mid%d" % fin], W=["mask"], N=["mask"])
                if BST < 4:
                    continue
                Sd.buf("maskT").new()
                for g4 in range((qb + 4) // 4):
                    nb = min(4, qb + 1 - 4 * g4)
                    tpm = PSB(5).bitcast(BF16).rearrange("p (a b) -> p a b", a=8)
                    Sd.buf("psT").new()
                    for u_ in range(nb):
                        kb = 4 * g4 + u_
                        PE("transpose", tpm[:, u_, :], mask[:, kb * 128:(kb + 1) * 128], identb,
                           R=["mask", "identb"], W=["psT"])
                    DVE("tensor_copy", maskT[:, 4 * g4:4 * g4 + nb, :], tpm[:, 0:nb, :], R=["psT"], W=["maskT"])
                if BST < 5:
                    continue
                attps = psum[:, 0, :].rearrange("p (j t) -> p j t", j=4)
                Sd.buf("ps01").new()
                for hh in range(2):
                    rhs_q = qlat[:, 4 * hh:4 * hh + 4, :].rearrange("p h t -> p (h t)")
                    Sd.buf("pso").new()
                    Sd.buf("psl").new()
                    def score(kb_):
                        pb_ = 2 + (kb_ % 2)
                        PE("matmul", PSB(pb_), chatT[:, kb_ * 128:(kb_ + 1) * 128], rhs_q, start=True, stop=True,
                           R=["chatT", "qlat"], W=["psL%d" % pb_], N=["psL%d" % pb_])
                    score(0)
                    for kb in range(qb + 1):
                        pb = 2 + (kb % 2)
                        pn = "psL%d" % pb
                        eb = kb % 2
                        if kb + 1 <= qb:
                            score(kb + 1)
                        ACT("activation", esb[eb], PSB(pb), AF.Exp, bias=negB[:, 0:1], R=[pn, "negB"],
                            W=["esb%d" % eb], N=["esb%d" % eb])
                        DVE("tensor_tensor", psb_[eb].rearrange("p (h t) -> p h t", h=4),
                            esb[eb].rearrange("p (h t) -> p h t", h=4),
                            maskT[:, kb, :].unsqueeze(1).to_broadcast([128, 4, 128]), ALU.mult,
                            R=["esb%d" % eb, "maskT"], W=["psb%d" % eb], N=["psb%d" % eb])
                        PE("matmul", PSB(6), chtok[:, kb, :], psb_[eb], start=(kb == 0), stop=(kb == qb),
                           R=["chtok", "psb%d" % eb], W=["pso"])
                        PE("matmul", PSB(7), onesb, psb_[eb], start=(kb == 0), stop=(kb == qb),
                           R=["onesb", "psb%d" % eb], W=["psl"])
                    DVE("reciprocal", rl, PSB(7), R=["psl"], W=["rl"], N=["rl"])
                    DVE("tensor_tensor", on, PSB(6), rl, ALU.mult, R=["pso", "rl"], W=["on"], N=["on"])
                    for hl in range(4):
                        h = 4 * hh + hl
                        j, e = h // 2, h % 2
                        PE("matmul", attps[64 * e:64 * e + 64, j, :], wuvp[:, h, :], on[:, hl * 128:(hl + 1) * 128],
                           start=True, stop=True, R=["on", "wuvp"], W=["ps01"])
                DVE("tensor_tensor", mixo, attps, gab, ALU.mult, R=["ps01", "gab"], W=["mixo"], N=["mixo"])
                DMA(mix_d[0:4, :, t0:t0 + 128].rearrange("j p t -> p j t"), mixo, R=["mixo"])
            Sd.barrier()
            AR.release(m0)

        def phase_C(seq):
            m0 = AR.mark()
            s0 = seq * S
            wglu = AR.alloc([4, 1024], BF16)
            stg = AR.alloc([4, 1024], F32)
            DMA(stg, C.wglu_d, W=["stgg"], N=["stgg"])
            POOL("tensor_copy", wglu, stg, R=["stgg"], W=["wglu"], N=["wglu"])
            uT = [AR.alloc([S], BF16) for _ in range(2)]
            Ub = [AR.alloc([512], BF16) for _ in range(2)]
            angt = AR.alloc([512], F32)
            C1 = AR.alloc([512], F32)
            S1 = AR.alloc([512], F32)
            tf_ = AR.alloc([512], F32)
            tg_ = AR.alloc([512], F32)
            ti_ = AR.alloc([512], I32)
            ta = AR.alloc([512], F32)
            tb = AR.alloc([512], F32)
            Stre = AR.alloc([512], F32)
            Stim = AR.alloc([512], F32)
            Zre = AR.alloc([512], F32)
            Zim = AR.alloc([512], F32)
            Xre = [AR.alloc([512], BF16) for _ in range(2)]
            Xim = [AR.alloc([512], BF16) for _ in range(2)]
            Yg = AR.alloc([8, 512], BF16)
            ygT = AR.alloc([4, S], BF16)
            sig = AR.alloc([512], F32)
            ssm = AR.alloc([512], F32)
            gsb = AR.alloc([512], BF16)
            mxo = AR.alloc([512], BF16)
            for e in range(2):
                POOL("memset", Xre[e], 0.0, W=["Xre%d" % e], N=["Xre%d" % e])
                POOL("memset", Xim[e], 0.0, W=["Xim%d" % e], N=["Xim%d" % e])
            for q in range(4):
                ub = uT[q % 2]
                un = "uT%d" % (q % 2)
                DMA(ub, u_d[q][:, s0:s0 + S], W=[un], N=[un])
                Sd.buf("Yg").new()
                for ip in range(4):
                    i = 4 * q + ip
                    for e in range(2):
                        gl = 2 * ip + e
                        pn = "psU%d" % e
                        Sd.buf(pn).new()
                        for j in range(8):
                            x0 = 112 + 16 * (gl - j)
                            PE("matmul", PSB(e), dgb[:, gl, x0:x0 + 128], ub[:, j:S:8], start=(j == 0), stop=(j == 7),
                               R=[un, "dgb"], W=[pn])
                        ACT("activation", Ub[e], PSB(e), AF.Copy, R=[pn], W=["Ub%d" % e], N=["Ub%d" % e])
                    Sd.buf("psSre").new()
                    Sd.buf("psSim").new()
                    for e in range(2):
                        sl = slice(64 * e, 64 * e + 64)
                        PE("matmul", psum[sl, 2, :], V_re[:, i, sl], Ub[e], start=True, stop=True,
                           R=["V_re", "Ub%d" % e], W=["psSre"])
                        PE("matmul", psum[sl, 3, :], V_im[:, i, sl], Ub[e], start=True, stop=True,
                           R=["V_im", "Ub%d" % e], W=["psSim"])
                    DVE("tensor_scalar", angt, kkf, Thr[:, i:i + 1], None, ALU.mult, R=["kkf", "Thr"], W=["Tang"], N=["Tang"])
                    sincos(angt, 512, C1, S1, tf_, ti_, tg_, "T")
                    DVE("tensor_tensor", ta, PSB(2), C1, ALU.mult, R=["psSre", "Tcos"], W=["ta"], N=["ta"])
                    DVE("tensor_tensor", tb, PSB(3), S1, ALU.mult, R=["psSim", "Tsin"], W=["tb"], N=["tb"])
                    DVE("tensor_tensor", Stre, ta, tb, ALU.add, R=["ta", "tb"], W=["Stre"], N=["Stre"])
                    DVE("tensor_tensor", ta, PSB(3), C1, ALU.mult, R=["psSim", "Tcos"], W=["ta"], N=["ta"])
                    DVE("tensor_tensor", tb, PSB(2), S1, ALU.mult, R=["psSre", "Tsin"], W=["tb"], N=["tb"])
                    DVE("tensor_tensor", Stim, ta, tb, ALU.subtract, R=["ta", "tb"], W=["Stim"], N=["Stim"])
                    Rbc = Rsc[:, i:i + 1].to_broadcast([128, 512])
                    DVE("tensor_tensor_scan", Zre, Rbc, Stre, 0.0, ALU.mult, ALU.add, R=["Stre", "Rsc"], W=["Zre"], N=["Zre"])
                    DVE("tensor_tensor_scan", Zim, Rbc, Stim, 0.0, ALU.mult, ALU.add, R=["Stim", "Rsc"], W=["Zim"], N=["Zim"])
                    DVE("tensor_tensor", ta, Zre, C1, ALU.mult, R=["Zre", "Tcos"], W=["ta"], N=["ta"])
                    DVE("tensor_tensor", tb, Zim, S1, ALU.mult, R=["Zim", "Tsin"], W=["tb"], N=["tb"])
                    for e in range(2):
                        sl = slice(64 * e, 64 * e + 64)
                        DVE("tensor_tensor", Xre[e][sl, 1:512], ta[sl, 0:511], tb[sl, 0:511], ALU.subtract,
                            R=["ta", "tb"], W=["Xre%d" % e], N=["Xre%d" % e])
                    DVE("tensor_tensor", ta, Zim, C1, ALU.mult, R=["Zim", "Tcos"], W=["ta"], N=["ta"])
                    DVE("tensor_tensor", tb, Zre, S1, ALU.mult, R=["Zre", "Tsin"], W=["tb"], N=["tb"])
                    for e in range(2):
                        sl = slice(64 * e, 64 * e + 64)
                        DVE("tensor_tensor", Xim[e][sl, 1:512], ta[sl, 0:511], tb[sl, 0:511], ALU.add,
                            R=["ta", "tb"], W=["Xim%d" % e], N=["Xim%d" % e])
                    for e in range(2):
                        g = 2 * i + e
                        gl = 2 * ip + e
                        pb = 4 + e
                        pn = "psY%d" % e
                        PE("matmul", PSB(pb), W_sb[:, g, :], Ub[e], start=True, stop=False,
                           R=["W_sb", "Ub%d" % e], W=[pn], N=[pn])
                        PE("matmul", PSB(pb), Wc_re[:, i, :], Xre[e], start=False, stop=False,
                           R=["Wc_re", "Xre%d" % e], W=[pn])
                        PE("matmul", PSB(pb), Wc_im[:, i, :], Xim[e], start=False, stop=True,
                           R=["Wc_im", "Xim%d" % e], W=[pn])
                        ACT("activation", Yg[:, gl, :], PSB(pb), AF.Gelu_apprx_tanh, R=[pn], W=["Yg"])
                Sd.buf("ygT").new() if q == 0 else None
                ygv = ygT[:, q, :].rearrange("p (k t) -> p k t", t=8)
                for tau in range(8):
                    pb = 6 + (tau % 2)
                    pn = "psC%d" % pb
                    Sd.buf(pn).new()
                    for gl in range(8):
                        x0 = 112 + 16 * (tau - gl)
                        PE("matmul", PSB(pb), dgb[:, tau, x0:x0 + 128], Yg[:, gl, :], start=(gl == 0), stop=(gl == 7),
                           R=["Yg", "dgb"], W=[pn])
                    if tau % 2 == 0:
                        DVE("tensor_copy", ygv[:, :, tau], PSB(pb), R=[pn], W=["ygT"])
                    else:
                        ACT("activation", ygv[:, :, tau], PSB(pb), AF.Copy, R=[pn], W=["ygT"])
            if dbgC and seq == 0:
                DMA(ygT_dbg.rearrange("q p t -> p q t"), ygT, R=["ygT"])
            for tc in range(8):
                c0 = tc * 512
                for v in range(4):
                    Sd.buf("psU0").new()
                    Sd.buf("psU1").new()
                    for q in range(4):
                        PE("matmul", PSB(0), wglu[:, q, v * 128:(v + 1) * 128], ygT[:, q, c0:c0 + 512],
                           start=(q == 0), stop=(q == 3), R=["wglu", "ygT"], W=["psU0"])
                    for q in range(4):
                        PE("matmul", PSB(1), wglu[:, q, 512 + v * 128:512 + (v + 1) * 128], ygT[:, q, c0:c0 + 512],
                           start=(q == 0), stop=(q == 3), R=["wglu", "ygT"], W=["psU1"])
                    DMA(gsb, gs_d[v][:, s0 + c0:s0 + c0 + 512], W=["gsb"], N=["gsb"])
                    ACT("activation", sig, PSB(1), AF.Sigmoid, bias=bglu[:, 4 + v:5 + v], R=["psU1", "bglu"], W=["sig"], N=["sig"])
                    DVE("scalar_tensor_tensor", ssm, PSB(0), bglu[:, v:v + 1], sig, ALU.add, ALU.mult,
                        R=["psU0", "sig", "bglu"], W=["ssm"], N=["ssm"])
                    DVE("tensor_tensor", mxo, ssm, gsb, ALU.mult, R=["ssm", "gsb"], W=["mxo"], N=["mxo"])
                    DMA(mix_d[4 + v][:, s0 + c0:s0 + c0 + 512], mxo, R=["mxo"])
            Sd.barrier()
            AR.release(m0)

        def phase_D(seq):
            m0 = AR.mark()
            s0 = seq * S
            wout = AR.alloc([8, 1024], BF16)
            stg = [AR.alloc([1024], F32) for _ in range(2)]
            for k in range(8):
                b = k % 2
                DMA(stg[b], C.wout_d[:, k, :], W=["stgo%d" % b], N=["stgo%d" % b])
                POOL("tensor_copy", wout[:, k, :], stg[b], R=["stgo%d" % b], W=["wout"])
            lng = AR.alloc([1024], F32)
            lnb = AR.alloc([1024], F32)
            DMA(lng, C.lng_d, W=["lng"], N=["lng"])
            DMA(lnb, C.lnb_d, W=["lnb"], N=["lnb"])
            mixT = [AR.alloc([8, 128], BF16) for _ in range(2)]
            xt = [AR.alloc([1024], F32) for _ in range(2)]
            rr = [AR.alloc([1024], F32) for _ in range(2)]
            st = AR.alloc([2, 6], F32)
            mvv = AR.alloc([2], F32)
            rs_ = AR.alloc([1], F32)
            for tt in range(32):
                t0 = s0 + tt * 128
                j = tt % 2
                DMA(mixT[j], mix_d[:, :, t0:t0 + 128].rearrange("j p t -> p j t"), W=["mixT%d" % j], N=["mixT%d" % j])
                DMA(xt[j], C.x_d[t0:t0 + 128, :], W=["xt%d" % j], N=["xt%d" % j])
                for half in range(2):
                    pb = 2 * j + half
                    pn = "psD%d" % pb
                    Sd.buf(pn).new()
                    for e in range(8):
                        PE("matmul", PSB(pb), mixT[j][:, e, :], wout[:, e, half * 512:(half + 1) * 512],
                           start=(e == 0), stop=(e == 7), R=["mixT%d" % j, "wout"], W=[pn])
                rn = "rr%d" % j
                Sd.buf(rn).new()
                for half in range(2):
                    pb = 2 * j + half
                    DVE("scalar_tensor_tensor", rr[j][:, half * 512:(half + 1) * 512], xt[j][:, half * 512:(half + 1) * 512],
                        ALPHA, PSB(pb), ALU.mult, ALU.add, R=["xt%d" % j, "psD%d" % pb], W=[rn])
                Sd.buf("st").new()
                for half in range(2):
                    DVE("bn_stats", st[:, half, :], rr[j][:, half * 512:(half + 1) * 512], R=[rn], W=["st"])
                DVE("bn_aggr", mvv, st.rearrange("p a b -> p (a b)"), R=["st"], W=["mvv"], N=["mvv"])
                DVE("tensor_scalar", rs_, mvv[:, 1:2], LN_EPS, None, ALU.add, R=["mvv"], W=["rs"], N=["rs"])
                ACT("activation", rs_, rs_, AF.Sqrt, R=["rs"], W=["rs"], N=["rs"])
                DVE("reciprocal", rs_, rs_, R=["rs"], W=["rs"], N=["rs"])
                DVE("tensor_scalar", rr[j], rr[j], mvv[:, 0:1], rs_[:, 0:1], ALU.subtract, ALU.mult,
                    R=["mvv", "rs", rn], W=[rn], N=[rn])
                POOL("tensor_tensor", rr[j], rr[j], lng, ALU.mult, R=[rn, "lng"], W=[rn], N=[rn])
                POOL("tensor_tensor", rr[j], rr[j], lnb, ALU.add, R=[rn, "lnb"], W=[rn], N=[rn])
                DMA(C.out_d[t0:t0 + 128, :], rr[j], R=[rn])
            Sd.barrier()
            AR.release(m0)

        for l in range(NL):
            set_layer(l)
            layer_consts()
            if "C" in phases:
                s5_setup()
            for seq in range(NSEQ):
                if "A" in phases:
                    phase_A(seq)
                if "B" in phases:
                    phase_B(seq)
                if "C" in phases:
                    phase_C(seq)
                if "D" in phases:
                    phase_D(seq)
        Sd.barrier()
        Sd.cnt["sp"] += 1

        def fin(e):
            return e.nop()
        Sd._emit("sp", fin, {}, (Sd.sem["sp"], 1))

        with nc.Block() as block:
            @block.tensor
            def _(e):
                for th in Sd.thunks["pe"]:
                    th(e)

            @block.scalar
            def _(e):
                for th in Sd.thunks["act"]:
                    th(e)

            @block.vector
            def _(e):
                for th in Sd.thunks["dve"]:
                    th(e)

            @block.gpsimd
            def _(e):
                for th in Sd.thunks["pool"]:
                    th(e)

            @block.sync
            def _(e):
                for th in Sd.thunks["sp"]:
                    th(e)
    return nc


def _consts():
    ident = np.eye(128, dtype=np.float32)
    t = np.arange(128)
    caus = np.where(t[None, :] <= t[:, None], 0.0, NEG).astype(np.float32)
    jj = t // 16
    tri = (jj[None, :] >= jj[:, None]).astype(np.float32)
    dg = np.zeros((128, 8, 352), np.float32)
    for g in range(8):
        for r in range(16 * g, 16 * g + 16):
            dg[r, g, 112 + r] = 1.0
    powers = [-(j + 1) for j in range(8)] + [tau + 1 for tau in range(8)] + [7 - j for j in range(8)]
    mvals = np.tile(np.asarray(powers, np.float32)[None, :], (128, 1))
    kk = np.tile(np.arange(1, 513, dtype=np.float32)[None, :], (128, 1))
    return dict(ident=ident, caus=caus, tri=tri, dg=dg, mvals=mvals, kk=kk)


def _ptile(a):
    rest = a.shape[2:]
    a = a.reshape((16, 2, 64) + rest)
    perm = (1, 2, 0) + tuple(range(3, 3 + len(rest)))
    return np.ascontiguousarray(a.transpose(perm).reshape((128, 16) + rest))


def prep_layer(l, inp):
    f = np.float32
    w_in = np.asarray(inp["w_in"][l], f)
    zeros32 = np.zeros((D, 32), f)
    q = w_in[:, 0:512]
    ckv = w_in[:, 512:640]
    qidx = w_in[:, 640:896]
    kidx = w_in[:, 896:928]
    widx = w_in[:, 928:936]
    ga = w_in[:, 936:1448]
    u = w_in[:, 1448:1960]
    gs = w_in[:, 1960:2472]
    qh = [qidx[:, 32 * h:32 * h + 32] for h in range(8)]
    idxA = np.concatenate([qh[0], qh[1], qh[2], zeros32], 1)
    idxB = np.concatenate([qh[3], qh[4], qh[5], zeros32], 1)
    idxC = np.concatenate([qh[6], qh[7], zeros32, zeros32], 1)
    k0_ = np.concatenate([kidx, zeros32, zeros32, zeros32], 1)
    k1_ = np.concatenate([zeros32, kidx, zeros32, zeros32], 1)
    k2_ = np.concatenate([zeros32, zeros32, kidx, zeros32], 1)
    wfm = np.concatenate([q, idxA, idxB, idxC, k0_, k1_, k2_, ga, u, gs], 1)
    wtm = np.concatenate([ckv, widx, qidx], 1)
    tile_k = lambda m: np.ascontiguousarray(m.reshape(8, 128, m.shape[1]).transpose(1, 0, 2))
    d = {}
    d["wf"] = tile_k(wfm)
    d["wt"] = tile_k(wtm)
    w_uk = np.asarray(inp["w_uk"][l], f)
    w_uv = np.asarray(inp["w_uv"][l], f)
    wukz = np.zeros((128, 8, 128), f)
    for h_ in range(8):
        e_ = h_ % 2
        wukz[64 * e_:64 * e_ + 64, h_, :] = w_uk[h_].T
    d["wuk"] = wukz
    d["wuv"] = np.ascontiguousarray(w_uv.transpose(1, 0, 2))
    g = np.asarray(inp["kv_norm_g"][l], f)
    d["kvg_bc"] = np.ascontiguousarray(np.tile(g[None, :], (128, 1)))
    d["kvg_col"] = np.ascontiguousarray(g[:, None])
    d["ldt"] = _ptile(np.tile(np.asarray(inp["log_dt"][l], f)[:, None], (1, 64)))
    d["are"] = _ptile(np.asarray(inp["a_re"][l], f))
    d["aim"] = _ptile(np.asarray(inp["a_im"][l], f))
    d["bre"] = _ptile(np.asarray(inp["b_re"][l], f))
    d["bim"] = _ptile(np.asarray(inp["b_im"][l], f))
    d["cre"] = _ptile(np.asarray(inp["c_re"][l], f).transpose(0, 2, 1))
    d["cim"] = _ptile(np.asarray(inp["c_im"][l], f).transpose(0, 2, 1))
    dsk = np.asarray(inp["d_skip"][l], f)
    d["dsk"] = np.ascontiguousarray(np.tile(dsk.T, (8, 1)))
    d["wglu"] = np.ascontiguousarray(np.asarray(inp["w_glu"][l], f).reshape(4, 128, 1024).transpose(1, 0, 2))
    d["bglu"] = np.ascontiguousarray(np.asarray(inp["b_glu"][l], f).reshape(8, 128).T)
    d["wout"] = np.ascontiguousarray(np.asarray(inp["w_out"][l], f).reshape(8, 128, 1024).transpose(1, 0, 2))
    d["lng"] = np.ascontiguousarray(np.tile(np.asarray(inp["ln_g"][l], f)[None, :], (128, 1)))
    d["lnb"] = np.ascontiguousarray(np.tile(np.asarray(inp["ln_b"][l], f)[None, :], (128, 1)))
    d.update(_consts())
    return d


def kernel(**inputs):
    x = np.ascontiguousarray(np.asarray(inputs["x"], np.float32))
    B = x.shape[0]
    h = x.reshape(NCORES, T, D)
    per = [prep_layer(l, inputs) for l in range(DEPTH)]
    cn = _consts()
    w = {k: np.ascontiguousarray(np.stack([p[k] for p in per])) for k in per[0] if k not in cn}
    w.update(cn)
    in_maps = [dict(w, x=np.ascontiguousarray(h[c])) for c in range(NCORES)]
    nc = build_model()
    res = run_bass_kernel_spmd(nc, in_maps, core_ids=list(range(NCORES)))
    out = np.stack([np.asarray(r["out"], np.float32) for r in res.results])
    return out.reshape(B, S, D).astype(np.float32)
```

```python
import math
import os
from contextlib import ExitStack

import numpy as np
import concourse.bass as bass
import concourse.mybir as mybir
from concourse.bass_utils import run_bass_kernel_spmd

F32 = mybir.dt.float32
BF16 = mybir.dt.bfloat16
F16 = mybir.dt.float16
I32 = mybir.dt.int32
U8 = mybir.dt.uint8
AF = mybir.ActivationFunctionType
ALU = mybir.AluOpType
AX = mybir.AxisListType

DEPTH = 4
NCORES = 8
NSEQ = 2
S = 4096
T = NSEQ * S
D = 1024
ALPHA = (2 * DEPTH) ** 0.25
LN_EPS = 1e-5
RMS_EPS = 1e-6
NEG = -60000.0
RB = 16.0
NIT = 15
PI = math.pi
DSZ = {F32: 4, BF16: 2, F16: 2, I32: 4, U8: 1}


def merge(*ds):
    out = {}
    for d in ds:
        if not d:
            continue
        for k, v in d.items():
            if out.get(k, 0) < v:
                out[k] = v
    return out


class Buf:
    def __init__(self):
        self.prev = {}
        self.ws = {}
        self.rs = {}

    def new(self):
        self.prev = merge(self.prev, self.rs, self.ws)
        self.ws = {}
        self.rs = {}


class Sched:
    ENG = ("pe", "act", "dve", "pool", "sp")

    def __init__(self, nc, es):
        self.nc = nc
        self.sem = {k: es.enter_context(nc.semaphore("s_" + k)) for k in self.ENG}
        self.cnt = {k: 0 for k in self.ENG}
        self.thunks = {k: [] for k in self.ENG}
        self.seen = {k: {} for k in self.ENG}
        self.pending = {k: {} for k in self.ENG}
        self.NDMA = 16
        for i in range(self.NDMA):
            self.sem["d%d" % i] = es.enter_context(nc.semaphore("s_d%d" % i))
        self.dcnt = [0] * self.NDMA
        self.dnext = 0
        self.bufs = {}
        self.ninst = 0

    def buf(self, name):
        b = self.bufs.get(name)
        if b is None:
            b = self.bufs[name] = Buf()
        return b

    def _emit(self, eng, fn, deps, own_inc):
        seen = self.seen[eng]
        deps = merge(deps, self.pending[eng])
        self.pending[eng] = {}
        waits = []
        for k, v in deps.items():
            if seen.get(k, 0) >= v:
                continue
            seen[k] = v
            waits.append((self.sem[k], v))
        sems = self.sem
        self.ninst += 1 + len(waits)

        def thunk(e, waits=waits, fn=fn, own_inc=own_inc):
            for s_, v_ in waits:
                e.wait_ge(s_, v_)
            ins = fn(e)
            ins.then_inc(own_inc[0], own_inc[1])

        self.thunks[eng].append(thunk)

    def op(self, eng, method, *args, R=(), W=(), N=(), deps=None, **kw):
        dr = merge(*[self.buf(n).ws for n in R])
        for n in N:
            self.buf(n).new()
        d = merge(deps, dr, *[self.buf(n).prev for n in W])
        self.cnt[eng] += 1
        tok = {eng: self.cnt[eng]}

        def fn(e, method=method, args=args, kw=kw):
            return getattr(e, method)(*args, **kw)

        self._emit(eng, fn, d, (self.sem[eng], 1))
        for n in R:
            b = self.buf(n)
            b.rs = merge(b.rs, tok)
        for n in W:
            b = self.buf(n)
            b.ws = merge(b.ws, tok)
        return tok

    def dma(self, out, in_, R=(), W=(), N=(), deps=None, q="sp"):
        dr = merge(*[self.buf(n).ws for n in R])
        for n in N:
            self.buf(n).new()
        s = self.dnext
        self.dnext = (s + 1) % self.NDMA
        key = "d%d" % s
        d = merge(deps, dr, *[self.buf(n).prev for n in W],
                  {key: 16 * self.dcnt[s]} if self.dcnt[s] else None)
        self.dcnt[s] += 1
        tok = {key: 16 * self.dcnt[s]}

        def fn(e, out=out, in_=in_):
            return e.dma_start(out=out, in_=in_)

        self._emit(q, fn, d, (self.sem[key], 16))
        for n in R:
            b = self.buf(n)
            b.rs = merge(b.rs, tok)
        for n in W:
            b = self.buf(n)
            b.ws = merge(b.ws, tok)
        return tok

    def all_tokens(self):
        t = {k: self.cnt[k] for k in self.ENG if self.cnt[k]}
        for i in range(self.NDMA):
            if self.dcnt[i]:
                t["d%d" % i] = 16 * self.dcnt[i]
        return t

    def barrier(self):
        t = self.all_tokens()
        for k in self.ENG:
            self.pending[k] = merge(self.pending[k], t)
        self.bufs = {}


class Arena:
    def __init__(self, t, size):
        self.t = t
        self.size = size
        self.off = 0

    def alloc(self, shape, dt):
        n = int(np.prod(shape)) * DSZ[dt]
        off = (self.off + 63) // 64 * 64
        assert off + n <= self.size, ("SBUF arena overflow", off, n, self.size)
        self.off = off + n
        ap = self.t[:, off:off + n].bitcast(dt)
        if len(shape) == 1:
            return ap
        names = " ".join("a%d" % i for i in range(len(shape)))
        kw = {"a%d" % i: int(shape[i]) for i in range(len(shape))}
        return ap.rearrange("p (%s) -> p %s" % (names, names), **kw)

    def mark(self):
        return self.off

    def release(self, m):
        self.off = m


def build_model(NL=DEPTH, debug=False, phases="ABCD"):
    nc = bass.Bass("TRN2", target_bir_lowering=False)

    def din(name, shape, dt=F32):
        return nc.dram_tensor(name, list(shape), dt, kind="ExternalInput").ap()

    class C:
        pass
    x_in = din("x", [T, D])
    out_final = nc.dram_tensor("out", [T, D], F32, kind="ExternalOutput").ap()
    xbufs = [nc.dram_tensor("xbuf%d" % i, [T, D], F32).ap() for i in range(2)]
    LAYERED = dict(wf=[128, 8, 2816], wt=[128, 8, 392], wuk=[128, 8, 128], wuv=[128, 8, 64], kvg_bc=[128, 128],
                   kvg_col=[128, 1], ldt=[128, 16], are=[128, 16], aim=[128, 16], bre=[128, 16, 16], bim=[128, 16, 16],
                   cre=[128, 16, 16], cim=[128, 16, 16], dsk=[128, 32], wglu=[128, 4, 1024], bglu=[128, 8],
                   wout=[128, 8, 1024], lng=[128, 1024], lnb=[128, 1024])
    LAY = {k: din(k, [NL] + v) for k, v in LAYERED.items()}

    def set_layer(l):
        C.x_d = x_in if l == 0 else xbufs[(l - 1) % 2]
        C.out_d = out_final if l == NL - 1 else xbufs[l % 2]
        for k in LAYERED:
            setattr(C, k.replace("_", "") + "_d", LAY[k][l])
    ident_d = din("ident", [128, 128])
    caus_d = din("caus", [128, 128])
    tri_d = din("tri", [128, 128])
    dg_d = din("dg", [128, 8, 352])
    mv_d = din("mvals", [128, 24])
    kk_d = din("kk", [128, 512])

    def scr(name, shape, dt):
        if debug and name in debug:
            return nc.dram_tensor(name, list(shape), dt, kind="ExternalOutput").ap()
        return nc.dram_tensor(name, list(shape), dt).ap()

    qT_d = scr("qT_s", [4, 128, T], BF16)
    idx_d = scr("idx_s", [6, 128, T], BF16)
    ga_d = scr("ga_s", [4, 128, T], BF16)
    u_d = scr("u_s", [4, 128, T], BF16)
    gs_d = scr("gs_s", [4, 128, T], BF16)
    chat_d = scr("chat_s", [T, 128], BF16)
    chatT_d = scr("chatT_s", [128, T], BF16)
    wn_d = scr("wn_s", [T, 8], F32)
    mix_d = scr("mix_s", [8, 128, T], BF16)
    dbgC = bool(debug) and "ygT_s" in debug
    if dbgC:
        ygT_dbg = scr("ygT_s", [4, 128, S], BF16)
        W_dbg = scr("W_s", [128, 32, 128], BF16)
        Vre_dbg = scr("Vre_s", [128, 16, 128], BF16)
        Vim_dbg = scr("Vim_s", [128, 16, 128], BF16)
        Wcre_dbg = scr("Wcre_s", [128, 16, 128], BF16)
        Wcim_dbg = scr("Wcim_s", [128, 16, 128], BF16)
        RT_dbg = scr("RT_s", [128, 32], F32)

    es = ExitStack()
    with es:
        SBSZ = 200 * 1024
        arena_t = es.enter_context(nc.sbuf_tensor("arena", [128, SBSZ], U8))
        AR = Arena(arena_t, SBSZ)
        psum = es.enter_context(nc.psum_tensor("ps", [128, 8, 512], F32))
        Sd = Sched(nc, es)

        def PSB(b, w=512):
            return psum[:, b, 0:w]

        def PE(m, *a, **k):
            return Sd.op("pe", m, *a, **k)

        def ACT(m, *a, **k):
            return Sd.op("act", m, *a, **k)

        def DVE(m, *a, **k):
            return Sd.op("dve", m, *a, **k)

        def POOL(m, *a, **k):
            return Sd.op("pool", m, *a, **k)

        DMA = Sd.dma

        identf = AR.alloc([128], F32)
        identb = AR.alloc([128], BF16)
        onesb = AR.alloc([128], BF16)
        causf = AR.alloc([128], F32)
        trif = AR.alloc([128], F32)
        dgb = AR.alloc([8, 352], BF16)
        wukp = AR.alloc([8, 128], BF16)
        wuvp = AR.alloc([8, 64], BF16)
        kvgbc = AR.alloc([128], F32)
        kvgcol = AR.alloc([1], F32)
        W_sb = AR.alloc([32, 128], BF16)
        Wc_re = AR.alloc([16, 128], BF16)
        Wc_im = AR.alloc([16, 128], BF16)
        V_re = AR.alloc([16, 128], BF16)
        V_im = AR.alloc([16, 128], BF16)
        Rsc = AR.alloc([16], F32)
        Thr = AR.alloc([16], F32)
        kkf = AR.alloc([512], F32)
        bglu = AR.alloc([8], F32)
        pm = AR.mark()

        stgA = AR.alloc([8, 352], F32)
        DMA(identf, ident_d, W=["identf"], N=["identf"])
        DMA(causf, caus_d, W=["causf"], N=["causf"])
        DMA(trif, tri_d, W=["trif"], N=["trif"])
        DMA(kkf, kk_d, W=["kkf"], N=["kkf"])
        DMA(stgA, dg_d, W=["stgA"], N=["stgA"])
        POOL("tensor_copy", identb, identf, R=["identf"], W=["identb"], N=["identb"])
        POOL("memset", onesb, 1.0, W=["onesb"], N=["onesb"])
        POOL("tensor_copy", dgb, stgA, R=["stgA"], W=["dgb"], N=["dgb"])
        Sd.barrier()
        AR.release(pm)

        def layer_consts():
            m0 = AR.mark()
            stgB = AR.alloc([8, 64], F32)
            stgC = AR.alloc([8, 128], F32)
            DMA(kvgbc, C.kvgbc_d, W=["kvgbc"], N=["kvgbc"])
            DMA(kvgcol, C.kvgcol_d, W=["kvgcol"], N=["kvgcol"])
            DMA(bglu, C.bglu_d, W=["bglu"], N=["bglu"])
            DMA(stgB, C.wuv_d, W=["stgB"], N=["stgB"])
            DMA(stgC, C.wuk_d, W=["stgC"], N=["stgC"])
            DVE("tensor_scalar", wuvp, stgB, kvgcol[:, 0:1], None, ALU.mult, R=["stgB", "kvgcol"], W=["wuvp"], N=["wuvp"])
            DVE("scalar_tensor_tensor", wukp, stgC, 0.125, kvgbc.unsqueeze(1).to_broadcast([128, 8, 128]),
                ALU.mult, ALU.mult, R=["stgC", "kvgbc"], W=["wukp"], N=["wukp"])
            Sd.barrier()
            AR.release(m0)

        def sincos(ang, F, out_cos, out_sin, tmp_f, tmp_i, tmp_g, tag):
            bn = lambda s: tag + s
            DVE("tensor_scalar", tmp_f, ang, 1.0 / (2 * PI), None, ALU.mult, R=[bn("ang")], W=[bn("tf")], N=[bn("tf")])
            DVE("tensor_copy", tmp_i, tmp_f, R=[bn("tf")], W=[bn("ti")], N=[bn("ti")])
            DVE("tensor_copy", tmp_f, tmp_i, R=[bn("ti")], W=[bn("tf")], N=[bn("tf")])
            DVE("scalar_tensor_tensor", out_sin, tmp_f, -2 * PI, ang, ALU.mult, ALU.add,
                R=[bn("tf"), bn("ang")], W=[bn("r")], N=[bn("r")])
            for (thr, cmp, adj) in ((PI, ALU.is_gt, -2 * PI), (-PI, ALU.is_lt, 2 * PI)):
                DVE("tensor_scalar", tmp_g, out_sin, thr, adj, cmp, ALU.mult, R=[bn("r")], W=[bn("tg")], N=[bn("tg")])
                DVE("tensor_tensor", out_sin, out_sin, tmp_g, ALU.add, R=[bn("tg")], W=[bn("r")], N=[bn("r")])
            DVE("tensor_scalar", out_cos, out_sin, PI / 2, None, ALU.add, R=[bn("r")], W=[bn("rc")], N=[bn("rc")])
            DVE("tensor_scalar", tmp_g, out_cos, PI, -2 * PI, ALU.is_gt, ALU.mult, R=[bn("rc")], W=[bn("tg")], N=[bn("tg")])
            DVE("tensor_tensor", out_cos, out_cos, tmp_g, ALU.add, R=[bn("tg")], W=[bn("rc")], N=[bn("rc")])
            DVE("tensor_scalar", out_sin, out_sin, PI, -PI, ALU.min, ALU.max, R=[bn("r")], W=[bn("r")], N=[bn("r")])
            DVE("tensor_scalar", out_cos, out_cos, PI, -PI, ALU.min, ALU.max, R=[bn("rc")], W=[bn("rc")], N=[bn("rc")])
            ACT("activation", out_sin, out_sin, AF.Sin, R=[bn("r")], W=[bn("sin")], N=[bn("sin"), bn("r")])
            ACT("activation", out_cos, out_cos, AF.Sin, R=[bn("rc")], W=[bn("cos")], N=[bn("cos"), bn("rc")])

        def s5_setup():
            m0 = AR.mark()
            ldt = AR.alloc([16], F32)
            are = AR.alloc([16], F32)
            aim = AR.alloc([16], F32)
            bre = AR.alloc([16, 16], F32)
            bim = AR.alloc([16, 16], F32)
            cre = AR.alloc([16, 16], F32)
            cim = AR.alloc([16, 16], F32)
            dsk = AR.alloc([32], F32)
            mv = AR.alloc([24], F32)
            for ap, d_, n in ((ldt, C.ldt_d, "ldt"), (are, C.are_d, "are"), (aim, C.aim_d, "aim"), (bre, C.bre_d, "bre"),
                              (bim, C.bim_d, "bim"), (cre, C.cre_d, "cre"), (cim, C.cim_d, "cim"), (dsk, C.dsk_d, "dsk"),
                              (mv, mv_d, "mv")):
                DMA(ap, d_, W=[n], N=[n])
            dt = AR.alloc([16], F32)
            dre = AR.alloc([16], F32)
            dim = AR.alloc([16], F32)
            ACT("activation", dt, ldt, AF.Exp, R=["ldt"], W=["dt"], N=["dt"])
            DVE("tensor_tensor", dre, are, dt, ALU.mult, R=["are", "dt"], W=["dre"], N=["dre"])
            DVE("tensor_tensor", dim, aim, dt, ALU.mult, R=["aim", "dt"], W=["dim"], N=["dim"])
            ACT("activation", Rsc, dre, AF.Exp, scale=8.0, R=["dre"], W=["Rsc"], N=["Rsc"])
            th8 = AR.alloc([16], F32)
            tq = AR.alloc([16], F32)
            tqi = AR.alloc([16], I32)
            tg = AR.alloc([16], F32)
            DVE("tensor_scalar", th8, dim, 8.0, None, ALU.mult, R=["dim"], W=["th8"], N=["th8"])
            DVE("tensor_scalar", tq, th8, 1.0 / (2 * PI), None, ALU.mult, R=["th8"], W=["tq"], N=["tq"])
            DVE("tensor_copy", tqi, tq, R=["tq"], W=["tqi"], N=["tqi"])
            DVE("tensor_copy", tq, tqi, R=["tqi"], W=["tq"], N=["tq"])
            DVE("scalar_tensor_tensor", Thr, tq, -2 * PI, th8, ALU.mult, ALU.add, R=["tq", "th8"], W=["Thr"], N=["Thr"])
            for (thr, cmp, adj) in ((PI, ALU.is_gt, -2 * PI), (-PI, ALU.is_lt, 2 * PI)):
                DVE("tensor_scalar", tg, Thr, thr, adj, cmp, ALU.mult, R=["Thr"], W=["tg8"], N=["tg8"])
                DVE("tensor_tensor", Thr, Thr, tg, ALU.add, R=["tg8"], W=["Thr"], N=["Thr"])
            F = 16 * 24
            mre = AR.alloc([16, 24], F32)
            ang = AR.alloc([16, 24], F32)
            Ere = AR.alloc([16, 24], F32)
            Eim = AR.alloc([16, 24], F32)
            tf_ = AR.alloc([16, 24], F32)
            tg_ = AR.alloc([16, 24], F32)
            ti_ = AR.alloc([16, 24], I32)
            mvb = mv.unsqueeze(1).to_broadcast([128, 16, 24])
            DVE("tensor_tensor", mre, dre.unsqueeze(2).to_broadcast([128, 16, 24]), mvb, ALU.mult,
                R=["dre", "mv"], W=["mre"], N=["mre"])
            DVE("tensor_tensor", ang, dim.unsqueeze(2).to_broadcast([128, 16, 24]), mvb, ALU.mult,
                R=["dim", "mv"], W=["Eang"], N=["Eang"])
            ACT("activation", mre, mre, AF.Exp, R=["mre"], W=["mre"], N=["mre"])
            sincos(ang, F, Ere, Eim, tf_, ti_, tg_, "E")
            DVE("tensor_tensor", Ere, Ere, mre, ALU.mult, R=["Ecos", "mre"], W=["Ere"], N=["Ere", "Ecos"])
            DVE("tensor_tensor", Eim, Eim, mre, ALU.mult, R=["Esin", "mre"], W=["Eim"], N=["Eim", "Esin"])
            nr = AR.alloc([16], F32)
            den = AR.alloc([16], F32)
            t1 = AR.alloc([16], F32)
            t2 = AR.alloc([16], F32)
            fre = AR.alloc([16], F32)
            fim = AR.alloc([16], F32)
            e1r = Ere[:, :, 8]
            e1i = Eim[:, :, 8]
            DVE("tensor_scalar", nr, e1r, -1.0, None, ALU.add, R=["Ere"], W=["nr"], N=["nr"])
            DVE("tensor_tensor", t1, are, are, ALU.mult, R=["are"], W=["t1"], N=["t1"])
            DVE("tensor_tensor", t2, aim, aim, ALU.mult, R=["aim"], W=["t2"], N=["t2"])
            DVE("tensor_tensor", den, t1, t2, ALU.add, R=["t1", "t2"], W=["den"], N=["den"])
            DVE("reciprocal", den, den, R=["den"], W=["den"], N=["den"])
            DVE("tensor_tensor", t1, nr, are, ALU.mult, R=["nr", "are"], W=["t1"], N=["t1"])
            DVE("tensor_tensor", t2, e1i, aim, ALU.mult, R=["Eim", "aim"], W=["t2"], N=["t2"])
            DVE("tensor_tensor", fre, t1, t2, ALU.add, R=["t1", "t2"], W=["fre"], N=["fre"])
            DVE("tensor_tensor", fre, fre, den, ALU.mult, R=["den"], W=["fre"], N=["fre"])
            DVE("tensor_tensor", t1, e1i, are, ALU.mult, R=["Eim", "are"], W=["t1"], N=["t1"])
            DVE("tensor_tensor", t2, nr, aim, ALU.mult, R=["nr", "aim"], W=["t2"], N=["t2"])
            DVE("tensor_tensor", fim, t1, t2, ALU.subtract, R=["t1", "t2"], W=["fim"], N=["fim"])
            DVE("tensor_tensor", fim, fim, den, ALU.mult, R=["den"], W=["fim"], N=["fim"])
            Bre = AR.alloc([16, 16], F32)
            Bim = AR.alloc([16, 16], F32)
            u1 = AR.alloc([16, 16], F32)
            freb = fre.unsqueeze(2).to_broadcast([128, 16, 16])
            fimb = fim.unsqueeze(2).to_broadcast([128, 16, 16])
            DVE("tensor_tensor", Bre, bre, freb, ALU.mult, R=["bre", "fre"], W=["Bre"], N=["Bre"])
            DVE("tensor_tensor", u1, bim, fimb, ALU.mult, R=["bim", "fim"], W=["u1"], N=["u1"])
            DVE("tensor_tensor", Bre, Bre, u1, ALU.subtract, R=["u1"], W=["Bre"], N=["Bre"])
            DVE("tensor_tensor", Bim, bim, freb, ALU.mult, R=["bim", "fre"], W=["Bim"], N=["Bim"])
            DVE("tensor_tensor", u1, bre, fimb, ALU.mult, R=["bre", "fim"], W=["u1"], N=["u1"])
            DVE("tensor_tensor", Bim, Bim, u1, ALU.add, R=["u1"], W=["Bim"], N=["Bim"])

            big = [16, 8, 16]
            tA = AR.alloc(big, F32)
            tB = AR.alloc(big, F32)

            def cprod(o_re, o_im, sl, yre, yim, yn_re, yn_im, neg_im, tag):
                er = Ere[:, :, sl].unsqueeze(3).to_broadcast([128, 16, 8, 16])
                ei = Eim[:, :, sl].unsqueeze(3).to_broadcast([128, 16, 8, 16])
                yr = yre.unsqueeze(2).to_broadcast([128, 16, 8, 16])
                yi = yim.unsqueeze(2).to_broadcast([128, 16, 8, 16])
                DVE("tensor_tensor", tA, er, yr, ALU.mult, R=["Ere", yn_re], W=["tA"], N=["tA"])
                DVE("tensor_tensor", tB, ei, yi, ALU.mult, R=["Eim", yn_im], W=["tB"], N=["tB"])
                DVE("tensor_tensor", o_re, tA, tB, ALU.subtract, R=["tA", "tB"], W=[tag + "re"], N=[tag + "re"])
                DVE("tensor_tensor", tA, er, yi, ALU.mult, R=["Ere", yn_im], W=["tA"], N=["tA"])
                DVE("tensor_tensor", tB, ei, yr, ALU.mult, R=["Eim", yn_re], W=["tB"], N=["tB"])
                if neg_im:
                    DVE("scalar_tensor_tensor", o_im, tA, -1.0, tB, ALU.mult, ALU.subtract,
                        R=["tA", "tB"], W=[tag + "im"], N=[tag + "im"])
                else:
                    DVE("tensor_tensor", o_im, tA, tB, ALU.add, R=["tA", "tB"], W=[tag + "im"], N=[tag + "im"])

            Bq_re = AR.alloc(big, F32)
            Bq_im = AR.alloc(big, F32)
            Cq_re = AR.alloc(big, F32)
            Cq_im = AR.alloc(big, F32)
            cprod(Bq_re, Bq_im, slice(0, 8), Bre, Bim, "Bre", "Bim", False, "Bq")
            cprod(Cq_re, Cq_im, slice(8, 16), cre, cim, "cre", "cim", True, "Cq")
            POOL("tensor_copy", Wc_re, Cq_re.rearrange("p i a b -> p i (a b)"), R=["Cqre"], W=["Wc_re"], N=["Wc_re"])
            POOL("tensor_copy", Wc_im, Cq_im.rearrange("p i a b -> p i (a b)"), R=["Cqim"], W=["Wc_im"], N=["Wc_im"])
            Bqm_re = [AR.alloc([16, 128], BF16) for _ in range(2)]
            Bqm_im = [AR.alloc([16, 128], BF16) for _ in range(2)]
            Sd.buf("Bqreb").new()
            Sd.buf("Bqimb").new()
            for e_ in range(2):
                o_ = slice(64 * (1 - e_), 64 * (1 - e_) + 64)
                k_ = slice(64 * e_, 64 * e_ + 64)
                POOL("memset", Bqm_re[e_][o_], 0.0, W=["Bqreb"])
                POOL("memset", Bqm_im[e_][o_], 0.0, W=["Bqimb"])
                POOL("tensor_copy", Bqm_re[e_][k_], Bq_re[k_].rearrange("p i a b -> p i (a b)"), R=["Bqre"], W=["Bqreb"])
                POOL("tensor_copy", Bqm_im[e_][k_], Bq_im[k_].rearrange("p i a b -> p i (a b)"), R=["Bqim"], W=["Bqimb"])
            tW = AR.alloc([128], F32)
            for g in range(32):
                i, e = g // 2, g % 2
                pb = 4 + (g % 2)
                sl = slice(64 * e, 64 * e + 64)
                PE("matmul", PSB(pb, 128), Bqm_re[e][:, i, :], Wc_re[:, i, :], start=True, stop=False,
                   R=["Bqreb", "Wc_re"], W=["psW%d" % pb], N=["psW%d" % pb])
                PE("matmul", PSB(pb, 128), Bqm_im[e][:, i, :], Wc_im[:, i, :], start=False, stop=True,
                   R=["Bqimb", "Wc_im"], W=["psW%d" % pb])
                DVE("tensor_tensor", tW, PSB(pb, 128), trif, ALU.mult, R=["psW%d" % pb, "trif"], W=["tW"], N=["tW"])
                DVE("scalar_tensor_tensor", W_sb[:, g, :], identf, dsk[:, g:g + 1], tW, ALU.mult, ALU.add,
                    R=["tW", "identf", "dsk"], W=["W_sb"])
            cprod(Bq_re, Bq_im, slice(16, 24), Bre, Bim, "Bre", "Bim", False, "Bq")
            for i in range(16):
                for (src, dst, sn, dn) in ((Bq_re, V_re, "Bqre", "V_re"), (Bq_im, V_im, "Bqim", "V_im")):
                    pb = 6 + (i % 2)
                    PE("transpose", PSB(pb, 128), src[:, i, :, :].rearrange("p a b -> p (a b)"), identf,
                       R=[sn, "identf"], W=["psV%d" % pb], N=["psV%d" % pb])
                    ACT("activation", dst[:, i, :], PSB(pb, 128), AF.Copy, R=["psV%d" % pb], W=[dn])
            if dbgC:
                DMA(W_dbg, W_sb, R=["W_sb"])
                DMA(Vre_dbg, V_re, R=["V_re"])
                DMA(Vim_dbg, V_im, R=["V_im"])
                DMA(Wcre_dbg, Wc_re, R=["Wc_re"])
                DMA(Wcim_dbg, Wc_im, R=["Wc_im"])
                DMA(RT_dbg[:, 0:16], Rsc, R=["Rsc"])
                DMA(RT_dbg[:, 16:32], Thr, R=["Thr"])
            Sd.barrier()
            AR.release(m0)

        def phase_A(seq):
            m0 = AR.mark()
            wf = AR.alloc([8, 2816], BF16)
            wt = AR.alloc([8, 392], BF16)
            stg = [AR.alloc([2816], F32) for _ in range(2)]
            for k in range(8):
                b = k % 2
                DMA(stg[b], C.wf_d[:, k, :], W=["stg%d" % b], N=["stg%d" % b])
                POOL("tensor_copy", wf[:, k, :], stg[b], R=["stg%d" % b], W=["wf"])
            wts = AR.alloc([8, 392], F32)
            DMA(wts, C.wt_d, W=["wts"], N=["wts"])
            POOL("tensor_copy", wt, wts, R=["wts"], W=["wt"], N=["wt"])
            xtok = [AR.alloc([1024], F32) for _ in range(2)]
            xbf = [AR.alloc([1024], BF16) for _ in range(2)]
            xT = [AR.alloc([8, 512], BF16) for _ in range(2)]
            fm = AR.alloc([22, 512], BF16)
            sqj = AR.alloc([128], BF16)
            ss = AR.alloc([1], F32)
            vv = AR.alloc([1], F32)
            sv = AR.alloc([1], F32)
            rstd = AR.alloc([1], F32)
            chat_c = AR.alloc([4, 128], BF16)
            chatT_c = AR.alloc([512], BF16)
            wq = AR.alloc([264], F32)
            sq = AR.alloc([256], F32)
            qn = AR.alloc([8], F32)
            aw = AR.alloc([8], F32)
            s1 = AR.alloc([1], F32)
            wn_c = AR.alloc([4, 8], F32)
            for ch in range(8):
                t0c = seq * S + ch * 512
                xb = xT[ch % 2]
                xn = "xT%d" % (ch % 2)
                Sd.buf(xn).new()
                Sd.buf("chat_c").new()
                Sd.buf("chatT_c").new()
                Sd.buf("wn_c").new()
                for tt in range(4):
                    t0 = t0c + tt * 128
                    j = (ch * 4 + tt) % 2
                    DMA(xtok[j], C.x_d[t0:t0 + 128, :], W=["xtok%d" % j], N=["xtok%d" % j])
                    POOL("tensor_copy", xbf[j], xtok[j], R=["xtok%d" % j], W=["xbf%d" % j], N=["xbf%d" % j])
                    tp = PSB(6 + j).bitcast(BF16).rearrange("p (a b) -> p a b", a=8)
                    Sd.buf("pstp%d" % j).new()
                    for k in range(8):
                        PE("transpose", tp[:, k, :], xbf[j][:, k * 128:(k + 1) * 128], identb,
                           R=["xbf%d" % j, "identb"], W=["pstp%d" % j])
                    ACT("activation", xb[:, :, tt * 128:(tt + 1) * 128], tp, AF.Copy, R=["pstp%d" % j], W=[xn])
                    tokps = PSB(5, 392)
                    Sd.buf("tokps").new()
                    for k in range(8):
                        PE("matmul", tokps, xb[:, k, tt * 128:(tt + 1) * 128], wt[:, k, :], start=(k == 0), stop=(k == 7),
                           R=[xn, "wt"], W=["tokps"])
                    ACT("activation", sqj, tokps[:, 0:128], AF.Square, accum_out=ss[:, 0:1], R=["tokps"], W=["ss", "sqj"], N=["ss", "sqj"])
                    DVE("tensor_scalar", vv, ss, 1.0 / 128, RMS_EPS, ALU.mult, ALU.add, R=["ss"], W=["vv"], N=["vv"])
                    ACT("activation", sv, vv, AF.Sqrt, R=["vv"], W=["sv"], N=["sv"])
                    DVE("reciprocal", rstd, sv, R=["sv"], W=["rstd"], N=["rstd"])
                    ACT("activation", chat_c[:, tt, :], tokps[:, 0:128], AF.Copy, scale=rstd[:, 0:1],
                        R=["tokps", "rstd"], W=["chat_c"])
                    tp2 = PSB(4).bitcast(BF16)[:, 0:128]
                    PE("transpose", tp2, chat_c[:, tt, :], identb, R=["chat_c", "identb"], W=["pstp2"], N=["pstp2"])
                    DVE("tensor_copy", chatT_c[:, tt * 128:(tt + 1) * 128], tp2, R=["pstp2"], W=["chatT_c"])
                    ACT("activation", wq, tokps[:, 128:392], AF.Copy, R=["tokps"], W=["wq"], N=["wq"])
                    DVE("tensor_tensor", sq, wq[:, 8:264], wq[:, 8:264], ALU.mult, R=["wq"], W=["sq"], N=["sq"])
                    DVE("tensor_reduce", qn, sq.rearrange("p (h d) -> p h d", h=8), AX.X, ALU.add, R=["sq"], W=["qn"], N=["qn"])
                    ACT("activation", qn, qn, AF.Sqrt, R=["qn"], W=["qn"], N=["qn"])
                    DVE("scalar_tensor_tensor", aw, wq[:, 0:8], -1.0, wq[:, 0:8], ALU.mult, ALU.max, R=["wq"], W=["aw"], N=["aw"])
                    DVE("tensor_tensor", aw, aw, qn, ALU.mult, R=["qn"], W=["aw"], N=["aw"])
                    DVE("tensor_reduce", s1, aw, AX.X, ALU.add, R=["aw"], W=["s1"], N=["s1"])
                    DVE("tensor_scalar", s1, s1, 1e-30, None, ALU.add, R=["s1"], W=["s1"], N=["s1"])
                    DVE("reciprocal", s1, s1, R=["s1"], W=["s1"], N=["s1"])
                    DVE("tensor_scalar", wn_c[:, tt, :], wq[:, 0:8], s1[:, 0:1], None, ALU.mult, R=["wq", "s1"], W=["wn_c"])
                DMA(chat_d[t0c:t0c + 512, :].rearrange("(a p) c -> p a c", p=128), chat_c, R=["chat_c"])
                DMA(chatT_d[:, t0c:t0c + 512], chatT_c, R=["chatT_c"])
                DMA(wn_d[t0c:t0c + 512, :].rearrange("(a p) h -> p a h", p=128), wn_c, R=["wn_c"])
                Sd.buf("fm").new()
                for cb in range(22):
                    pb = cb % 4
                    pn = "psfm%d" % pb
                    Sd.buf(pn).new()
                    for k in range(8):
                        PE("matmul", PSB(pb), wf[:, k, cb * 128:(cb + 1) * 128], xb[:, k, :], start=(k == 0), stop=(k == 7),
                           R=["wf", xn], W=[pn])
                    silu = (10 <= cb < 14) or cb >= 18
                    if silu:
                        ACT("activation", fm[:, cb, :], PSB(pb), AF.Silu, R=[pn], W=["fm"])
                    elif cb % 2 == 0:
                        ACT("activation", fm[:, cb, :], PSB(pb), AF.Copy, R=[pn], W=["fm"])
                    else:
                        DVE("tensor_copy", fm[:, cb, :], PSB(pb), R=[pn], W=["fm"])
                for (dd, c0_, c1_) in ((qT_d, 0, 4), (idx_d, 4, 10), (ga_d, 10, 14), (u_d, 14, 18), (gs_d, 18, 22)):
                    DMA(dd[:, :, t0c:t0c + 512].rearrange("j p t -> p j t"), fm[:, c0_:c1_, :], R=["fm"])
            Sd.barrier()
            AR.release(m0)

        def phase_B(seq):
            m0 = AR.mark()
            s0 = seq * S
            NQ = 32
            kiT = AR.alloc([3, S], BF16)
            chatT = AR.alloc([S], BF16)
            chtok = AR.alloc([32, 128], BF16)
            causb = AR.alloc([128], BF16)
            POOL("tensor_copy", causb, causf, R=["causf"], W=["causb"], N=["causb"])
            DMA(kiT, idx_d[3:6, :, s0:s0 + S].rearrange("j p t -> p j t"), W=["kiT"], N=["kiT"])
            DMA(chatT, chatT_d[:, s0:s0 + S], W=["chatT"], N=["chatT"])
            DMA(chtok, chat_d[s0:s0 + S, :].rearrange("(a p) c -> p a c", p=128), W=["chtok"], N=["chtok"])
            I_sb = [AR.alloc([S], F16) for _ in range(2)]
            junk = AR.alloc([S], BF16)
            mask = AR.alloc([S], BF16)
            maskT = [AR.alloc([32, 128], BF16) for _ in range(2)]
            Rsb = [AR.alloc([512], BF16) for _ in range(3)]
            diagw = [AR.alloc([8, 128], BF16) for _ in range(2)]
            qlat = [AR.alloc([8, 128], BF16) for _ in range(2)]
            sqs = AR.alloc([8, 128], BF16)
            esb = [AR.alloc([512], BF16) for _ in range(2)]
            psb_ = [AR.alloc([512], BF16) for _ in range(2)]
            rl = [AR.alloc([512], F32) for _ in range(2)]
            on = [AR.alloc([512], BF16) for _ in range(2)]
            qTb = [AR.alloc([4, 128], BF16) for _ in range(3)]
            idxq = [AR.alloc([3, 128], BF16) for _ in range(3)]
            wnb = [AR.alloc([8], F32) for _ in range(3)]
            gab = [AR.alloc([4, 128], BF16) for _ in range(3)]
            mixo = AR.alloc([4, 128], BF16)
            pmx = AR.alloc([1], BF16)
            nbp = AR.alloc([1], F32)
            negB = [AR.alloc([1], F32) for _ in range(2)]
            mid = [AR.alloc([1], F32) for _ in range(2)]
            cnt = AR.alloc([1], F32)
            mm_ = AR.alloc([1], F32)
            OB = (0, 6)
            LB = (1, 7)
            ridx = [0]

            def X(n):
                t0 = s0 + n * 128
                N = (n + 1) * 128
                a3, a2 = n % 3, n % 2
                DMA(qTb[a3], qT_d[:, :, t0:t0 + 128].rearrange("j p t -> p j t"), W=["qTb%d" % a3], N=["qTb%d" % a3])
                DMA(idxq[a3], idx_d[0:3, :, t0:t0 + 128].rearrange("j p t -> p j t"), W=["idxq%d" % a3], N=["idxq%d" % a3])
                DMA(wnb[a3], wn_d[t0:t0 + 128, :], W=["wnb%d" % a3], N=["wnb%d" % a3])
                DMA(gab[a3], ga_d[:, :, t0:t0 + 128].rearrange("j p t -> p j t"), W=["gab%d" % a3], N=["gab%d" % a3])
                qlps = psum[:, 2:4, :].rearrange("p b (h t) -> p (b h) t", h=4)
                Sd.buf("psL2").new()
                Sd.buf("psL3").new()
                for h in range(8):
                    PE("matmul", qlps[:, h, :], wukp[:, h, :], qTb[a3][:, h // 2, :], start=True, stop=True,
                       R=["qTb%d" % a3, "wukp"], W=["psL%d" % (2 + h // 4)])
                ACT("activation", qlat[a2], qlps, AF.Copy, R=["psL2", "psL3"], W=["qlat%d" % a2], N=["qlat%d" % a2])
                ACT("activation", sqs, qlps, AF.Square, R=["psL2", "psL3"], W=["sqs"], N=["sqs"])
                DVE("tensor_reduce", pmx, sqs.rearrange("p h t -> p (h t)"), AX.X, ALU.max, R=["sqs"], W=["pmx"], N=["pmx"])
                POOL("tensor_tensor", diagw[a2], identf.unsqueeze(1).to_broadcast([128, 8, 128]),
                     wnb[a3].unsqueeze(2).to_broadcast([128, 8, 128]), ALU.mult, R=["identf", "wnb%d" % a3],
                     W=["diagw%d" % a2], N=["diagw%d" % a2])
                nkc = (N + 511) // 512
                In = "I_sb%d" % a2
                Sd.buf(In).new()
                for kc in range(nkc):
                    w = min(512, N - kc * 512)
                    k0 = kc * 512
                    last = (kc == nkc - 1)
                    Sd.buf("psI").new()
                    pend = None
                    for h in range(8):
                        tl, r = h // 3, h % 3
                        pb = 2 + (h % 2)
                        pn = "psL%d" % pb
                        PE("matmul", PSB(pb, w), idxq[a3][:, tl, :], kiT[:, r, k0:k0 + w], start=True, stop=True,
                           R=["idxq%d" % a3, "kiT"], W=[pn], N=[pn])
                        if pend is not None:
                            ph, prb, prn = pend
                            PE("matmul", PSB(4, w), diagw[a2][:, ph, :], Rsb[prb][:, 0:w], start=(ph == 0), stop=False,
                               R=[prn, "diagw%d" % a2], W=["psI"])
                        rb = ridx[0] % 3
                        ridx[0] += 1
                        rn = "Rsb%d" % rb
                        ACT("activation", Rsb[rb][:, 0:w], PSB(pb, w), AF.Relu, R=[pn], W=[rn], N=[rn])
                        pend = (h, rb, rn)
                    ph, prb, prn = pend
                    PE("matmul", PSB(4, w), diagw[a2][:, ph, :], Rsb[prb][:, 0:w], start=False, stop=(not last),
                       R=[prn, "diagw%d" % a2], W=["psI"])
                    if last:
                        PE("matmul", psum[:, 4, w - 128:w], identb, causb, start=False, stop=True,
                           R=["identb", "causb"], W=["psI"])
                    ACT("activation", I_sb[a2][:, k0:k0 + w], PSB(4, w), AF.Copy, R=["psI"], W=[In])
                PE("matmul", psum[:, 2, 0:1], onesb, pmx, start=True, stop=True, R=["pmx", "onesb"], W=["psL2"], N=["psL2"])
                ACT("activation", nbp, psum[:, 2, 0:1], AF.Sqrt, scale=128.0 * 1.03, R=["psL2"], W=["nbp"], N=["nbp"])
                POOL("tensor_scalar", negB[a2], nbp, -1.0, None, ALU.mult, R=["nbp"], W=["negB%d" % a2], N=["negB%d" % a2])

            def Y_dve(n):
                N = (n + 1) * 128
                a2 = n % 2
                In = "I_sb%d" % a2
                DVE("memset", mid[0], 0.0, W=["mid0"], N=["mid0"])
                h_k = RB
                for it in range(NIT):
                    a, b = it % 2, (it + 1) % 2
                    DVE("tensor_scalar", junk[:, 0:N], I_sb[a2][:, 0:N], mid[a][:, 0:1], 0.0, ALU.is_ge, ALU.add,
                        accum_out=cnt[:, 0:1], R=[In, "mid%d" % a], W=["junk", "cnt"], N=["junk", "cnt"])
                    DVE("tensor_scalar", mm_, cnt, 255.5, h_k, ALU.is_ge, ALU.mult, R=["cnt"], W=["mm"], N=["mm"])
                    DVE("scalar_tensor_tensor", mid[b], mm_, -h_k / 2, mid[a], ALU.add, ALU.add,
                        R=["mm", "mid%d" % a], W=["mid%d" % b], N=["mid%d" % b])
                    h_k = h_k / 2
                fin = NIT % 2
                DVE("tensor_scalar", mask[:, 0:N], I_sb[a2][:, 0:N], mid[fin][:, 0:1], -h_k, ALU.subtract, ALU.is_ge,
                    R=[In, "mid%d" % fin], W=["mask"], N=["mask"])

            def Y_pe(n):
                a2 = n % 2
                Mn = "maskT%d" % a2
                Sd.buf(Mn).new()
                for g4 in range((n + 4) // 4):
                    nb = min(4, n + 1 - 4 * g4)
                    tpm = PSB(5).bitcast(BF16).rearrange("p (a b) -> p a b", a=8)
                    Sd.buf("ps5").new()
                    for u_ in range(nb):
                        kb = 4 * g4 + u_
                        PE("transpose", tpm[:, u_, :], mask[:, kb * 128:(kb + 1) * 128], identb,
                           R=["mask", "identb"], W=["ps5"])
                    DVE("tensor_copy", maskT[a2][:, 4 * g4:4 * g4 + nb, :], tpm[:, 0:nb, :], R=["ps5"], W=[Mn])

            def Z(n):
                a2 = n % 2
                Mn = "maskT%d" % a2
                for hh in range(2):
                    rhs_q = qlat[a2][:, 4 * hh:4 * hh + 4, :].rearrange("p h t -> p (h t)")
                    on_, ln_ = "pso%d" % hh, "psl%d" % hh
                    Sd.buf(on_).new()
                    Sd.buf(ln_).new()

                    def score(kb_):
                        pb_ = 2 + (kb_ % 2)
                        PE("matmul", PSB(pb_), chatT[:, kb_ * 128:(kb_ + 1) * 128], rhs_q, start=True, stop=True,
                           R=["chatT", "qlat%d" % a2], W=["psL%d" % pb_], N=["psL%d" % pb_])
                    score(0)
                    for kb in range(n + 1):
                        pb = 2 + (kb % 2)
                        pn = "psL%d" % pb
                        eb = kb % 2
                        if kb + 1 <= n:
                            score(kb + 1)
                        ACT("activation", esb[eb], PSB(pb), AF.Exp, bias=negB[a2][:, 0:1], R=[pn, "negB%d" % a2],
                            W=["esb%d" % eb], N=["esb%d" % eb])
                        POOL("tensor_tensor", psb_[eb].rearrange("p (h t) -> p h t", h=4),
                             esb[eb].rearrange("p (h t) -> p h t", h=4),
                             maskT[a2][:, kb, :].unsqueeze(1).to_broadcast([128, 4, 128]), ALU.mult,
                             R=["esb%d" % eb, Mn], W=["psb%d" % eb], N=["psb%d" % eb])
                        PE("matmul", PSB(OB[hh]), chtok[:, kb, :], psb_[eb], start=(kb == 0), stop=(kb == n),
                           R=["chtok", "psb%d" % eb], W=[on_])
                        PE("matmul", PSB(LB[hh]), onesb, psb_[eb], start=(kb == 0), stop=(kb == n),
                           R=["onesb", "psb%d" % eb], W=[ln_])

            def E(n):
                t0 = s0 + n * 128
                a3 = n % 3
                for hh in range(2):
                    DVE("reciprocal", rl[hh], PSB(LB[hh]), R=["psl%d" % hh], W=["rl%d" % hh], N=["rl%d" % hh])
                    DVE("tensor_tensor", on[hh], PSB(OB[hh]), rl[hh], ALU.mult, R=["pso%d" % hh, "rl%d" % hh],
                        W=["on%d" % hh], N=["on%d" % hh])
                attps = psum[:, 5, :].rearrange("p (j t) -> p j t", j=4)
                Sd.buf("ps5").new()
                for h in range(8):
                    hh, hl = h // 4, h % 4
                    j, e = h // 2, h % 2
                    PE("matmul", attps[64 * e:64 * e + 64, j, :], wuvp[:, h, :], on[hh][:, hl * 128:(hl + 1) * 128],
                       start=True, stop=True, R=["on%d" % hh, "wuvp"], W=["ps5"])
                DVE("tensor_tensor", mixo, attps, gab[a3], ALU.mult, R=["ps5", "gab%d" % a3], W=["mixo"], N=["mixo"])
                DMA(mix_d[0:4, :, t0:t0 + 128].rearrange("j p t -> p j t"), mixo, R=["mixo"])

            X(0)
            X(1)
            Y_dve(0)
            Y_pe(0)
            for n in range(NQ):
                if n + 1 < NQ:
                    Y_dve(n + 1)
                Z(n)
                if n + 2 < NQ:
                    X(n + 2)
                if n + 1 < NQ:
                    Y_pe(n + 1)
                E(n)
            Sd.barrier()
            AR.release(m0)

        def phase_C(seq):
            m0 = AR.mark()
            s0 = seq * S
            wglu = AR.alloc([4, 1024], BF16)
            stg = AR.alloc([4, 1024], F32)
            DMA(stg, C.wglu_d, W=["stgg"], N=["stgg"])
            POOL("tensor_copy", wglu, stg, R=["stgg"], W=["wglu"], N=["wglu"])
            uT = [AR.alloc([S], BF16) for _ in range(2)]
            Ub = [AR.alloc([512], BF16) for _ in range(2)]
            angt = AR.alloc([512], F32)
            C1 = AR.alloc([512], F32)
            S1 = AR.alloc([512], F32)
            tf_ = AR.alloc([512], F32)
            tg_ = AR.alloc([512], F32)
            ti_ = AR.alloc([512], I32)
            ta = AR.alloc([512], F32)
            tb = AR.alloc([512], F32)
            Stre = AR.alloc([512], F32)
            Stim = AR.alloc([512], F32)
            Zre = AR.alloc([512], F32)
            Zim = AR.alloc([512], F32)
            Xre = [AR.alloc([512], BF16) for _ in range(2)]
            Xim = [AR.alloc([512], BF16) for _ in range(2)]
            Yg = AR.alloc([8, 512], BF16)
            ygT = AR.alloc([4, S], BF16)
            sig = AR.alloc([512], F32)
            ssm = AR.alloc([512], F32)
            gsb = AR.alloc([512], BF16)
            mxo = AR.alloc([512], BF16)
            for e in range(2):
                POOL("memset", Xre[e], 0.0, W=["Xre%d" % e], N=["Xre%d" % e])
                POOL("memset", Xim[e], 0.0, W=["Xim%d" % e], N=["Xim%d" % e])
            for q in range(4):
                ub = uT[q % 2]
                un = "uT%d" % (q % 2)
                DMA(ub, u_d[q][:, s0:s0 + S], W=[un], N=[un])
                Sd.buf("Yg").new()
                for ip in range(4):
                    i = 4 * q + ip
                    for e in range(2):
                        gl = 2 * ip + e
                        pn = "psU%d" % e
                        Sd.buf(pn).new()
                        for j in range(8):
                            x0 = 112 + 16 * (gl - j)
                            PE("matmul", PSB(e), dgb[:, gl, x0:x0 + 128], ub[:, j:S:8], start=(j == 0), stop=(j == 7),
                               R=[un, "dgb"], W=[pn])
                        ACT("activation", Ub[e], PSB(e), AF.Copy, R=[pn], W=["Ub%d" % e], N=["Ub%d" % e])
                    Sd.buf("psSre").new()
                    Sd.buf("psSim").new()
                    for e in range(2):
                        sl = slice(64 * e, 64 * e + 64)
                        PE("matmul", psum[sl, 2, :], V_re[:, i, sl], Ub[e], start=True, stop=True,
                           R=["V_re", "Ub%d" % e], W=["psSre"])
                        PE("matmul", psum[sl, 3, :], V_im[:, i, sl], Ub[e], start=True, stop=True,
                           R=["V_im", "Ub%d" % e], W=["psSim"])
                    DVE("tensor_scalar", angt, kkf, Thr[:, i:i + 1], None, ALU.mult, R=["kkf", "Thr"], W=["Tang"], N=["Tang"])
                    sincos(angt, 512, C1, S1, tf_, ti_, tg_, "T")
                    DVE("tensor_tensor", ta, PSB(2), C1, ALU.mult, R=["psSre", "Tcos"], W=["ta"], N=["ta"])
                    DVE("tensor_tensor", tb, PSB(3), S1, ALU.mult, R=["psSim", "Tsin"], W=["tb"], N=["tb"])
                    DVE("tensor_tensor", Stre, ta, tb, ALU.add, R=["ta", "tb"], W=["Stre"], N=["Stre"])
                    DVE("tensor_tensor", ta, PSB(3), C1, ALU.mult, R=["psSim", "Tcos"], W=["ta"], N=["ta"])
                    DVE("tensor_tensor", tb, PSB(2), S1, ALU.mult, R=["psSre", "Tsin"], W=["tb"], N=["tb"])
                    DVE("tensor_tensor", Stim, ta, tb, ALU.subtract, R=["ta", "tb"], W=["Stim"], N=["Stim"])
                    Rbc = Rsc[:, i:i + 1].to_broadcast([128, 512])
                    DVE("tensor_tensor_scan", Zre, Rbc, Stre, 0.0, ALU.mult, ALU.add, R=["Stre", "Rsc"], W=["Zre"], N=["Zre"])
                    DVE("tensor_tensor_scan", Zim, Rbc, Stim, 0.0, ALU.mult, ALU.add, R=["Stim", "Rsc"], W=["Zim"], N=["Zim"])
                    DVE("tensor_tensor", ta, Zre, C1, ALU.mult, R=["Zre", "Tcos"], W=["ta"], N=["ta"])
                    DVE("tensor_tensor", tb, Zim, S1, ALU.mult, R=["Zim", "Tsin"], W=["tb"], N=["tb"])
                    for e in range(2):
                        sl = slice(64 * e, 64 * e + 64)
                        DVE("tensor_tensor", Xre[e][sl, 1:512], ta[sl, 0:511], tb[sl, 0:511], ALU.subtract,
                            R=["ta", "tb"], W=["Xre%d" % e], N=["Xre%d" % e])
                    DVE("tensor_tensor", ta, Zim, C1, ALU.mult, R=["Zim", "Tcos"], W=["ta"], N=["ta"])
                    DVE("tensor_tensor", tb, Zre, S1, ALU.mult, R=["Zre", "Tsin"], W=["tb"], N=["tb"])
                    for e in range(2):
                        sl = slice(64 * e, 64 * e + 64)
                        DVE("tensor_tensor", Xim[e][sl, 1:512], ta[sl, 0:511], tb[sl, 0:511], ALU.add,
                            R=["ta", "tb"], W=["Xim%d" % e], N=["Xim%d" % e])
                    for e in range(2):
                        g = 2 * i + e
                        gl = 2 * ip + e
                        pb = 4 + e
                        pn = "psY%d" % e
                        PE("matmul", PSB(pb), W_sb[:, g, :], Ub[e], start=True, stop=False,
                           R=["W_sb", "Ub%d" % e], W=[pn], N=[pn])
                        PE("matmul", PSB(pb), Wc_re[:, i, :], Xre[e], start=False, stop=False,
                           R=["Wc_re", "Xre%d" % e], W=[pn])
                        PE("matmul", PSB(pb), Wc_im[:, i, :], Xim[e], start=False, stop=True,
                           R=["Wc_im", "Xim%d" % e], W=[pn])
                        ACT("activation", Yg[:, gl, :], PSB(pb), AF.Gelu_apprx_tanh, R=[pn], W=["Yg"])
                Sd.buf("ygT").new() if q == 0 else None
                ygv = ygT[:, q, :].rearrange("p (k t) -> p k t", t=8)
                for tau in range(8):
                    pb = 6 + (tau % 2)
                    pn = "psC%d" % pb
                    Sd.buf(pn).new()
                    for gl in range(8):
                        x0 = 112 + 16 * (tau - gl)
                        PE("matmul", PSB(pb), dgb[:, tau, x0:x0 + 128], Yg[:, gl, :], start=(gl == 0), stop=(gl == 7),
                           R=["Yg", "dgb"], W=[pn])
                    if tau % 2 == 0:
                        DVE("tensor_copy", ygv[:, :, tau], PSB(pb), R=[pn], W=["ygT"])
                    else:
                        ACT("activation", ygv[:, :, tau], PSB(pb), AF.Copy, R=[pn], W=["ygT"])
            if dbgC and seq == 0:
                DMA(ygT_dbg.rearrange("q p t -> p q t"), ygT, R=["ygT"])
            for tc in range(8):
                c0 = tc * 512
                for v in range(4):
                    Sd.buf("psU0").new()
                    Sd.buf("psU1").new()
                    for q in range(4):
                        PE("matmul", PSB(0), wglu[:, q, v * 128:(v + 1) * 128], ygT[:, q, c0:c0 + 512],
                           start=(q == 0), stop=(q == 3), R=["wglu", "ygT"], W=["psU0"])
                    for q in range(4):
                        PE("matmul", PSB(1), wglu[:, q, 512 + v * 128:512 + (v + 1) * 128], ygT[:, q, c0:c0 + 512],
                           start=(q == 0), stop=(q == 3), R=["wglu", "ygT"], W=["psU1"])
                    DMA(gsb, gs_d[v][:, s0 + c0:s0 + c0 + 512], W=["gsb"], N=["gsb"])
                    ACT("activation", sig, PSB(1), AF.Sigmoid, bias=bglu[:, 4 + v:5 + v], R=["psU1", "bglu"], W=["sig"], N=["sig"])
                    DVE("scalar_tensor_tensor", ssm, PSB(0), bglu[:, v:v + 1], sig, ALU.add, ALU.mult,
                        R=["psU0", "sig", "bglu"], W=["ssm"], N=["ssm"])
                    DVE("tensor_tensor", mxo, ssm, gsb, ALU.mult, R=["ssm", "gsb"], W=["mxo"], N=["mxo"])
                    DMA(mix_d[4 + v][:, s0 + c0:s0 + c0 + 512], mxo, R=["mxo"])
            Sd.barrier()
            AR.release(m0)

        def phase_D(seq):
            m0 = AR.mark()
            s0 = seq * S
            wout = AR.alloc([8, 1024], BF16)
            stg = [AR.alloc([1024], F32) for _ in range(2)]
            for k in range(8):
                b = k % 2
                DMA(stg[b], C.wout_d[:, k, :], W=["stgo%d" % b], N=["stgo%d" % b])
                POOL("tensor_copy", wout[:, k, :], stg[b], R=["stgo%d" % b], W=["wout"])
            lng = AR.alloc([1024], F32)
            lnb = AR.alloc([1024], F32)
            DMA(lng, C.lng_d, W=["lng"], N=["lng"])
            DMA(lnb, C.lnb_d, W=["lnb"], N=["lnb"])
            mixT = [AR.alloc([8, 128], BF16) for _ in range(2)]
            xt = [AR.alloc([1024], F32) for _ in range(2)]
            rr = [AR.alloc([1024], F32) for _ in range(2)]
            st = AR.alloc([2, 6], F32)
            mvv = AR.alloc([2], F32)
            rs_ = AR.alloc([1], F32)
            for tt in range(32):
                t0 = s0 + tt * 128
                j = tt % 2
                DMA(mixT[j], mix_d[:, :, t0:t0 + 128].rearrange("j p t -> p j t"), W=["mixT%d" % j], N=["mixT%d" % j])
                DMA(xt[j], C.x_d[t0:t0 + 128, :], W=["xt%d" % j], N=["xt%d" % j])
                for half in range(2):
                    pb = 2 * j + half
                    pn = "psD%d" % pb
                    Sd.buf(pn).new()
                    for e in range(8):
                        PE("matmul", PSB(pb), mixT[j][:, e, :], wout[:, e, half * 512:(half + 1) * 512],
                           start=(e == 0), stop=(e == 7), R=["mixT%d" % j, "wout"], W=[pn])
                rn = "rr%d" % j
                Sd.buf(rn).new()
                for half in range(2):
                    pb = 2 * j + half
                    DVE("scalar_tensor_tensor", rr[j][:, half * 512:(half + 1) * 512], xt[j][:, half * 512:(half + 1) * 512],
                        ALPHA, PSB(pb), ALU.mult, ALU.add, R=["xt%d" % j, "psD%d" % pb], W=[rn])
                Sd.buf("st").new()
                for half in range(2):
                    DVE("bn_stats", st[:, half, :], rr[j][:, half * 512:(half + 1) * 512], R=[rn], W=["st"])
                DVE("bn_aggr", mvv, st.rearrange("p a b -> p (a b)"), R=["st"], W=["mvv"], N=["mvv"])
                DVE("tensor_scalar", rs_, mvv[:, 1:2], LN_EPS, None, ALU.add, R=["mvv"], W=["rs"], N=["rs"])
                ACT("activation", rs_, rs_, AF.Sqrt, R=["rs"], W=["rs"], N=["rs"])
                DVE("reciprocal", rs_, rs_, R=["rs"], W=["rs"], N=["rs"])
                DVE("tensor_scalar", rr[j], rr[j], mvv[:, 0:1], rs_[:, 0:1], ALU.subtract, ALU.mult,
                    R=["mvv", "rs", rn], W=[rn], N=[rn])
                POOL("tensor_tensor", rr[j], rr[j], lng, ALU.mult, R=[rn, "lng"], W=[rn], N=[rn])
                POOL("tensor_tensor", rr[j], rr[j], lnb, ALU.add, R=[rn, "lnb"], W=[rn], N=[rn])
                DMA(C.out_d[t0:t0 + 128, :], rr[j], R=[rn])
            Sd.barrier()
            AR.release(m0)

        for l in range(NL):
            set_layer(l)
            layer_consts()
            if "C" in phases:
                s5_setup()
            for seq in range(NSEQ):
                if "A" in phases:
                    phase_A(seq)
                if "B" in phases:
                    phase_B(seq)
                if "C" in phases:
                    phase_C(seq)
                if "D" in phases:
                    phase_D(seq)
        Sd.barrier()
        Sd.cnt["sp"] += 1

        def fin(e):
            return e.nop()
        Sd._emit("sp", fin, {}, (Sd.sem["sp"], 1))

        with nc.Block() as block:
            @block.tensor
            def _(e):
                for th in Sd.thunks["pe"]:
                    th(e)

            @block.scalar
            def _(e):
                for th in Sd.thunks["act"]:
                    th(e)

            @block.vector
            def _(e):
                for th in Sd.thunks["dve"]:
                    th(e)

            @block.gpsimd
            def _(e):
                for th in Sd.thunks["pool"]:
                    th(e)

            @block.sync
            def _(e):
                for th in Sd.thunks["sp"]:
                    th(e)
    return nc


def _consts():
    ident = np.eye(128, dtype=np.float32)
    t = np.arange(128)
    caus = np.where(t[None, :] <= t[:, None], 0.0, NEG).astype(np.float32)
    jj = t // 16
    tri = (jj[None, :] >= jj[:, None]).astype(np.float32)
    dg = np.zeros((128, 8, 352), np.float32)
    for g in range(8):
        for r in range(16 * g, 16 * g + 16):
            dg[r, g, 112 + r] = 1.0
    powers = [-(j + 1) for j in range(8)] + [tau + 1 for tau in range(8)] + [7 - j for j in range(8)]
    mvals = np.tile(np.asarray(powers, np.float32)[None, :], (128, 1))
    kk = np.tile(np.arange(1, 513, dtype=np.float32)[None, :], (128, 1))
    return dict(ident=ident, caus=caus, tri=tri, dg=dg, mvals=mvals, kk=kk)


def _ptile(a):
    rest = a.shape[2:]
    a = a.reshape((16, 2, 64) + rest)
    perm = (1, 2, 0) + tuple(range(3, 3 + len(rest)))
    return np.ascontiguousarray(a.transpose(perm).reshape((128, 16) + rest))


def prep_layer(l, inp):
    f = np.float32
    w_in = np.asarray(inp["w_in"][l], f)
    zeros32 = np.zeros((D, 32), f)
    q = w_in[:, 0:512]
    ckv = w_in[:, 512:640]
    qidx = w_in[:, 640:896]
    kidx = w_in[:, 896:928]
    widx = w_in[:, 928:936]
    ga = w_in[:, 936:1448]
    u = w_in[:, 1448:1960]
    gs = w_in[:, 1960:2472]
    qh = [qidx[:, 32 * h:32 * h + 32] for h in range(8)]
    idxA = np.concatenate([qh[0], qh[1], qh[2], zeros32], 1)
    idxB = np.concatenate([qh[3], qh[4], qh[5], zeros32], 1)
    idxC = np.concatenate([qh[6], qh[7], zeros32, zeros32], 1)
    k0_ = np.concatenate([kidx, zeros32, zeros32, zeros32], 1)
    k1_ = np.concatenate([zeros32, kidx, zeros32, zeros32], 1)
    k2_ = np.concatenate([zeros32, zeros32, kidx, zeros32], 1)
    wfm = np.concatenate([q, idxA, idxB, idxC, k0_, k1_, k2_, ga, u, gs], 1)
    wtm = np.concatenate([ckv, widx, qidx], 1)
    tile_k = lambda m: np.ascontiguousarray(m.reshape(8, 128, m.shape[1]).transpose(1, 0, 2))
    d = {}
    d["wf"] = tile_k(wfm)
    d["wt"] = tile_k(wtm)
    w_uk = np.asarray(inp["w_uk"][l], f)
    w_uv = np.asarray(inp["w_uv"][l], f)
    wukz = np.zeros((128, 8, 128), f)
    for h_ in range(8):
        e_ = h_ % 2
        wukz[64 * e_:64 * e_ + 64, h_, :] = w_uk[h_].T
    d["wuk"] = wukz
    d["wuv"] = np.ascontiguousarray(w_uv.transpose(1, 0, 2))
    g = np.asarray(inp["kv_norm_g"][l], f)
    d["kvg_bc"] = np.ascontiguousarray(np.tile(g[None, :], (128, 1)))
    d["kvg_col"] = np.ascontiguousarray(g[:, None])
    d["ldt"] = _ptile(np.tile(np.asarray(inp["log_dt"][l], f)[:, None], (1, 64)))
    d["are"] = _ptile(np.asarray(inp["a_re"][l], f))
    d["aim"] = _ptile(np.asarray(inp["a_im"][l], f))
    d["bre"] = _ptile(np.asarray(inp["b_re"][l], f))
    d["bim"] = _ptile(np.asarray(inp["b_im"][l], f))
    d["cre"] = _ptile(np.asarray(inp["c_re"][l], f).transpose(0, 2, 1))
    d["cim"] = _ptile(np.asarray(inp["c_im"][l], f).transpose(0, 2, 1))
    dsk = np.asarray(inp["d_skip"][l], f)
    d["dsk"] = np.ascontiguousarray(np.tile(dsk.T, (8, 1)))
    d["wglu"] = np.ascontiguousarray(np.asarray(inp["w_glu"][l], f).reshape(4, 128, 1024).transpose(1, 0, 2))
    d["bglu"] = np.ascontiguousarray(np.asarray(inp["b_glu"][l], f).reshape(8, 128).T)
    d["wout"] = np.ascontiguousarray(np.asarray(inp["w_out"][l], f).reshape(8, 128, 1024).transpose(1, 0, 2))
    d["lng"] = np.ascontiguousarray(np.tile(np.asarray(inp["ln_g"][l], f)[None, :], (128, 1)))
    d["lnb"] = np.ascontiguousarray(np.tile(np.asarray(inp["ln_b"][l], f)[None, :], (128, 1)))
    d.update(_consts())
    return d


def kernel(**inputs):
    x = np.ascontiguousarray(np.asarray(inputs["x"], np.float32))
    B = x.shape[0]
    h = x.reshape(NCORES, T, D)
    per = [prep_layer(l, inputs) for l in range(DEPTH)]
    cn = _consts()
    w = {k: np.ascontiguousarray(np.stack([p[k] for p in per])) for k in per[0] if k not in cn}
    w.update(cn)
    in_maps = [dict(w, x=np.ascontiguousarray(h[c])) for c in range(NCORES)]
    nc = build_model()
    res = run_bass_kernel_spmd(nc, in_maps, core_ids=list(range(NCORES)))
    out = np.stack([np.asarray(r["out"], np.float32) for r in res.results])
    return out.reshape(B, S, D).astype(np.float32)
```

```python
import math
import os
from contextlib import ExitStack

import numpy as np
import concourse.bass as bass
import concourse.mybir as mybir
from concourse.bass_utils import run_bass_kernel_spmd

F32 = mybir.dt.float32
BF16 = mybir.dt.bfloat16
F16 = mybir.dt.float16
I32 = mybir.dt.int32
U8 = mybir.dt.uint8
AF = mybir.ActivationFunctionType
ALU = mybir.AluOpType
AX = mybir.AxisListType

DEPTH = 4
NCORES = 8
NSEQ = 2
S = 4096
T = NSEQ * S
D = 1024
ALPHA = (2 * DEPTH) ** 0.25
LN_EPS = 1e-5
RMS_EPS = 1e-6
NEG = -60000.0
RB = 16.0
NIT = 15
PI = math.pi
DSZ = {F32: 4, BF16: 2, F16: 2, I32: 4, U8: 1}


def merge(*ds):
    out = {}
    for d in ds:
        if not d:
            continue
        for k, v in d.items():
            if out.get(k, 0) < v:
                out[k] = v
    return out


class Buf:
    def __init__(self):
        self.prev = {}
        self.ws = {}
        self.rs = {}

    def new(self):
        self.prev = merge(self.prev, self.rs, self.ws)
        self.ws = {}
        self.rs = {}


class Sched:
    ENG = ("pe", "act", "dve", "pool", "sp")

    def __init__(self, nc, es):
        self.nc = nc
        self.sem = {k: es.enter_context(nc.semaphore("s_" + k)) for k in self.ENG}
        self.cnt = {k: 0 for k in self.ENG}
        self.thunks = {k: [] for k in self.ENG}
        self.seen = {k: {} for k in self.ENG}
        self.pending = {k: {} for k in self.ENG}
        self.NDMA = 16
        for i in range(self.NDMA):
            self.sem["d%d" % i] = es.enter_context(nc.semaphore("s_d%d" % i))
        self.dcnt = [0] * self.NDMA
        self.dnext = 0
        self.bufs = {}
        self.ninst = 0

    def buf(self, name):
        b = self.bufs.get(name)
        if b is None:
            b = self.bufs[name] = Buf()
        return b

    def _emit(self, eng, fn, deps, own_inc):
        seen = self.seen[eng]
        deps = merge(deps, self.pending[eng])
        self.pending[eng] = {}
        waits = []
        for k, v in deps.items():
            if seen.get(k, 0) >= v:
                continue
            seen[k] = v
            waits.append((self.sem[k], v))
        sems = self.sem
        self.ninst += 1 + len(waits)

        def thunk(e, waits=waits, fn=fn, own_inc=own_inc):
            for s_, v_ in waits:
                e.wait_ge(s_, v_)
            ins = fn(e)
            ins.then_inc(own_inc[0], own_inc[1])

        self.thunks[eng].append(thunk)

    def op(self, eng, method, *args, R=(), W=(), N=(), deps=None, **kw):
        dr = merge(*[self.buf(n).ws for n in R])
        for n in N:
            self.buf(n).new()
        d = merge(deps, dr, *[self.buf(n).prev for n in W])
        self.cnt[eng] += 1
        tok = {eng: self.cnt[eng]}

        def fn(e, method=method, args=args, kw=kw):
            return getattr(e, method)(*args, **kw)

        self._emit(eng, fn, d, (self.sem[eng], 1))
        for n in R:
            b = self.buf(n)
            b.rs = merge(b.rs, tok)
        for n in W:
            b = self.buf(n)
            b.ws = merge(b.ws, tok)
        return tok

    def dma(self, out, in_, R=(), W=(), N=(), deps=None, q="sp"):
        dr = merge(*[self.buf(n).ws for n in R])
        for n in N:
            self.buf(n).new()
        s = self.dnext
        self.dnext = (s + 1) % self.NDMA
        key = "d%d" % s
        d = merge(deps, dr, *[self.buf(n).prev for n in W],
                  {key: 16 * self.dcnt[s]} if self.dcnt[s] else None)
        self.dcnt[s] += 1
        tok = {key: 16 * self.dcnt[s]}

        def fn(e, out=out, in_=in_):
            return e.dma_start(out=out, in_=in_)

        self._emit(q, fn, d, (self.sem[key], 16))
        for n in R:
            b = self.buf(n)
            b.rs = merge(b.rs, tok)
        for n in W:
            b = self.buf(n)
            b.ws = merge(b.ws, tok)
        return tok

    def all_tokens(self):
        t = {k: self.cnt[k] for k in self.ENG if self.cnt[k]}
        for i in range(self.NDMA):
            if self.dcnt[i]:
                t["d%d" % i] = 16 * self.dcnt[i]
        return t

    def barrier(self):
        t = self.all_tokens()
        for k in self.ENG:
            self.pending[k] = merge(self.pending[k], t)
        self.bufs = {}


class Arena:
    def __init__(self, t, size):
        self.t = t
        self.size = size
        self.off = 0

    def alloc(self, shape, dt):
        n = int(np.prod(shape)) * DSZ[dt]
        off = (self.off + 63) // 64 * 64
        assert off + n <= self.size, ("SBUF arena overflow", off, n, self.size)
        self.off = off + n
        ap = self.t[:, off:off + n].bitcast(dt)
        if len(shape) == 1:
            return ap
        names = " ".join("a%d" % i for i in range(len(shape)))
        kw = {"a%d" % i: int(shape[i]) for i in range(len(shape))}
        return ap.rearrange("p (%s) -> p %s" % (names, names), **kw)

    def mark(self):
        return self.off

    def release(self, m):
        self.off = m


def build_model(NL=DEPTH, debug=False, phases="ABCD"):
    nc = bass.Bass("TRN2", target_bir_lowering=False)

    def din(name, shape, dt=F32):
        return nc.dram_tensor(name, list(shape), dt, kind="ExternalInput").ap()

    class C:
        pass
    x_in = din("x", [T, D])
    out_final = nc.dram_tensor("out", [T, D], F32, kind="ExternalOutput").ap()
    xbufs = [nc.dram_tensor("xbuf%d" % i, [T, D], F32).ap() for i in range(2)]
    LAYERED = dict(wf=[128, 8, 2816], wt=[128, 8, 392], wuk=[128, 8, 128], wuv=[128, 8, 64], kvg_bc=[128, 128],
                   kvg_col=[128, 1], ldt=[128, 16], are=[128, 16], aim=[128, 16], bre=[128, 16, 16], bim=[128, 16, 16],
                   cre=[128, 16, 16], cim=[128, 16, 16], dsk=[128, 32], wglu=[128, 4, 1024], bglu=[128, 8],
                   wout=[128, 8, 1024], lng=[128, 1024], lnb=[128, 1024])
    LAY = {k: din(k, [NL] + v) for k, v in LAYERED.items()}

    def set_layer(l):
        C.x_d = x_in if l == 0 else xbufs[(l - 1) % 2]
        C.out_d = out_final if l == NL - 1 else xbufs[l % 2]
        for k in LAYERED:
            setattr(C, k.replace("_", "") + "_d", LAY[k][l])
    ident_d = din("ident", [128, 128])
    caus_d = din("caus", [128, 128])
    tri_d = din("tri", [128, 128])
    dg_d = din("dg", [128, 8, 352])
    mv_d = din("mvals", [128, 24])
    kk_d = din("kk", [128, 512])

    def scr(name, shape, dt):
        if debug and name in debug:
            return nc.dram_tensor(name, list(shape), dt, kind="ExternalOutput").ap()
        return nc.dram_tensor(name, list(shape), dt).ap()

    qT_d = scr("qT_s", [4, 128, T], BF16)
    idx_d = scr("idx_s", [6, 128, T], BF16)
    ga_d = scr("ga_s", [4, 128, T], BF16)
    u_d = scr("u_s", [4, 128, T], BF16)
    gs_d = scr("gs_s", [4, 128, T], BF16)
    chat_d = scr("chat_s", [T, 128], BF16)
    chatT_d = scr("chatT_s", [128, T], BF16)
    wn_d = scr("wn_s", [T, 8], F32)
    mix_d = scr("mix_s", [8, 128, T], BF16)
    dbgC = bool(debug) and "ygT_s" in debug
    if dbgC:
        ygT_dbg = scr("ygT_s", [4, 128, S], BF16)
        W_dbg = scr("W_s", [128, 32, 128], BF16)
        Vre_dbg = scr("Vre_s", [128, 16, 128], BF16)
        Vim_dbg = scr("Vim_s", [128, 16, 128], BF16)
        Wcre_dbg = scr("Wcre_s", [128, 16, 128], BF16)
        Wcim_dbg = scr("Wcim_s", [128, 16, 128], BF16)
        RT_dbg = scr("RT_s", [128, 32], F32)

    es = ExitStack()
    with es:
        SBSZ = 200 * 1024
        arena_t = es.enter_context(nc.sbuf_tensor("arena", [128, SBSZ], U8))
        AR = Arena(arena_t, SBSZ)
        psum = es.enter_context(nc.psum_tensor("ps", [128, 8, 512], F32))
        Sd = Sched(nc, es)

        def PSB(b, w=512):
            return psum[:, b, 0:w]

        def PE(m, *a, **k):
            return Sd.op("pe", m, *a, **k)

        def ACT(m, *a, **k):
            return Sd.op("act", m, *a, **k)

        def DVE(m, *a, **k):
            return Sd.op("dve", m, *a, **k)

        def POOL(m, *a, **k):
            return Sd.op("pool", m, *a, **k)

        DMA = Sd.dma

        identf = AR.alloc([128], F32)
        identb = AR.alloc([128], BF16)
        onesb = AR.alloc([128], BF16)
        causf = AR.alloc([128], F32)
        trif = AR.alloc([128], F32)
        dgb = AR.alloc([8, 352], BF16)
        wukp = AR.alloc([8, 128], BF16)
        wuvp = AR.alloc([8, 64], BF16)
        kvgbc = AR.alloc([128], F32)
        kvgcol = AR.alloc([1], F32)
        W_sb = AR.alloc([32, 128], BF16)
        Wc_re = AR.alloc([16, 128], BF16)
        Wc_im = AR.alloc([16, 128], BF16)
        V_re = AR.alloc([16, 128], BF16)
        V_im = AR.alloc([16, 128], BF16)
        Rsc = AR.alloc([16], F32)
        Thr = AR.alloc([16], F32)
        kkf = AR.alloc([512], F32)
        bglu = AR.alloc([8], F32)
        pm = AR.mark()

        stgA = AR.alloc([8, 352], F32)
        DMA(identf, ident_d, W=["identf"], N=["identf"])
        DMA(causf, caus_d, W=["causf"], N=["causf"])
        DMA(trif, tri_d, W=["trif"], N=["trif"])
        DMA(kkf, kk_d, W=["kkf"], N=["kkf"])
        DMA(stgA, dg_d, W=["stgA"], N=["stgA"])
        POOL("tensor_copy", identb, identf, R=["identf"], W=["identb"], N=["identb"])
        POOL("memset", onesb, 1.0, W=["onesb"], N=["onesb"])
        POOL("tensor_copy", dgb, stgA, R=["stgA"], W=["dgb"], N=["dgb"])
        Sd.barrier()
        AR.release(pm)

        def layer_consts():
            m0 = AR.mark()
            stgB = AR.alloc([8, 64], F32)
            stgC = AR.alloc([8, 128], F32)
            DMA(kvgbc, C.kvgbc_d, W=["kvgbc"], N=["kvgbc"])
            DMA(kvgcol, C.kvgcol_d, W=["kvgcol"], N=["kvgcol"])
            DMA(bglu, C.bglu_d, W=["bglu"], N=["bglu"])
            DMA(stgB, C.wuv_d, W=["stgB"], N=["stgB"])
            DMA(stgC, C.wuk_d, W=["stgC"], N=["stgC"])
            DVE("tensor_scalar", wuvp, stgB, kvgcol[:, 0:1], None, ALU.mult, R=["stgB", "kvgcol"], W=["wuvp"], N=["wuvp"])
            DVE("scalar_tensor_tensor", wukp, stgC, 0.125, kvgbc.unsqueeze(1).to_broadcast([128, 8, 128]),
                ALU.mult, ALU.mult, R=["stgC", "kvgbc"], W=["wukp"], N=["wukp"])
            Sd.barrier()
            AR.release(m0)

        def sincos(ang, F, out_cos, out_sin, tmp_f, tmp_i, tmp_g, tag):
            bn = lambda s: tag + s
            DVE("tensor_scalar", tmp_f, ang, 1.0 / (2 * PI), None, ALU.mult, R=[bn("ang")], W=[bn("tf")], N=[bn("tf")])
            DVE("tensor_copy", tmp_i, tmp_f, R=[bn("tf")], W=[bn("ti")], N=[bn("ti")])
            DVE("tensor_copy", tmp_f, tmp_i, R=[bn("ti")], W=[bn("tf")], N=[bn("tf")])
            DVE("scalar_tensor_tensor", out_sin, tmp_f, -2 * PI, ang, ALU.mult, ALU.add,
                R=[bn("tf"), bn("ang")], W=[bn("r")], N=[bn("r")])
            for (thr, cmp, adj) in ((PI, ALU.is_gt, -2 * PI), (-PI, ALU.is_lt, 2 * PI)):
                DVE("tensor_scalar", tmp_g, out_sin, thr, adj, cmp, ALU.mult, R=[bn("r")], W=[bn("tg")], N=[bn("tg")])
                DVE("tensor_tensor", out_sin, out_sin, tmp_g, ALU.add, R=[bn("tg")], W=[bn("r")], N=[bn("r")])
            DVE("tensor_scalar", out_cos, out_sin, PI / 2, None, ALU.add, R=[bn("r")], W=[bn("rc")], N=[bn("rc")])
            DVE("tensor_scalar", tmp_g, out_cos, PI, -2 * PI, ALU.is_gt, ALU.mult, R=[bn("rc")], W=[bn("tg")], N=[bn("tg")])
            DVE("tensor_tensor", out_cos, out_cos, tmp_g, ALU.add, R=[bn("tg")], W=[bn("rc")], N=[bn("rc")])
            DVE("tensor_scalar", out_sin, out_sin, PI, -PI, ALU.min, ALU.max, R=[bn("r")], W=[bn("r")], N=[bn("r")])
            DVE("tensor_scalar", out_cos, out_cos, PI, -PI, ALU.min, ALU.max, R=[bn("rc")], W=[bn("rc")], N=[bn("rc")])
            ACT("activation", out_sin, out_sin, AF.Sin, R=[bn("r")], W=[bn("sin")], N=[bn("sin"), bn("r")])
            ACT("activation", out_cos, out_cos, AF.Sin, R=[bn("rc")], W=[bn("cos")], N=[bn("cos"), bn("rc")])

        def s5_setup():
            m0 = AR.mark()
            ldt = AR.alloc([16], F32)
            are = AR.alloc([16], F32)
            aim = AR.alloc([16], F32)
            bre = AR.alloc([16, 16], F32)
            bim = AR.alloc([16, 16], F32)
            cre = AR.alloc([16, 16], F32)
            cim = AR.alloc([16, 16], F32)
            dsk = AR.alloc([32], F32)
            mv = AR.alloc([24], F32)
            for ap, d_, n in ((ldt, C.ldt_d, "ldt"), (are, C.are_d, "are"), (aim, C.aim_d, "aim"), (bre, C.bre_d, "bre"),
                              (bim, C.bim_d, "bim"), (cre, C.cre_d, "cre"), (cim, C.cim_d, "cim"), (dsk, C.dsk_d, "dsk"),
                              (mv, mv_d, "mv")):
                DMA(ap, d_, W=[n], N=[n])
            dt = AR.alloc([16], F32)
            dre = AR.alloc([16], F32)
            dim = AR.alloc([16], F32)
            ACT("activation", dt, ldt, AF.Exp, R=["ldt"], W=["dt"], N=["dt"])
            DVE("tensor_tensor", dre, are, dt, ALU.mult, R=["are", "dt"], W=["dre"], N=["dre"])
            DVE("tensor_tensor", dim, aim, dt, ALU.mult, R=["aim", "dt"], W=["dim"], N=["dim"])
            ACT("activation", Rsc, dre, AF.Exp, scale=8.0, R=["dre"], W=["Rsc"], N=["Rsc"])
            th8 = AR.alloc([16], F32)
            tq = AR.alloc([16], F32)
            tqi = AR.alloc([16], I32)
            tg = AR.alloc([16], F32)
            DVE("tensor_scalar", th8, dim, 8.0, None, ALU.mult, R=["dim"], W=["th8"], N=["th8"])
            DVE("tensor_scalar", tq, th8, 1.0 / (2 * PI), None, ALU.mult, R=["th8"], W=["tq"], N=["tq"])
            DVE("tensor_copy", tqi, tq, R=["tq"], W=["tqi"], N=["tqi"])
            DVE("tensor_copy", tq, tqi, R=["tqi"], W=["tq"], N=["tq"])
            DVE("scalar_tensor_tensor", Thr, tq, -2 * PI, th8, ALU.mult, ALU.add, R=["tq", "th8"], W=["Thr"], N=["Thr"])
            for (thr, cmp, adj) in ((PI, ALU.is_gt, -2 * PI), (-PI, ALU.is_lt, 2 * PI)):
                DVE("tensor_scalar", tg, Thr, thr, adj, cmp, ALU.mult, R=["Thr"], W=["tg8"], N=["tg8"])
                DVE("tensor_tensor", Thr, Thr, tg, ALU.add, R=["tg8"], W=["Thr"], N=["Thr"])
            F = 16 * 24
            mre = AR.alloc([16, 24], F32)
            ang = AR.alloc([16, 24], F32)
            Ere = AR.alloc([16, 24], F32)
            Eim = AR.alloc([16, 24], F32)
            tf_ = AR.alloc([16, 24], F32)
            tg_ = AR.alloc([16, 24], F32)
            ti_ = AR.alloc([16, 24], I32)
            mvb = mv.unsqueeze(1).to_broadcast([128, 16, 24])
            DVE("tensor_tensor", mre, dre.unsqueeze(2).to_broadcast([128, 16, 24]), mvb, ALU.mult,
                R=["dre", "mv"], W=["mre"], N=["mre"])
            DVE("tensor_tensor", ang, dim.unsqueeze(2).to_broadcast([128, 16, 24]), mvb, ALU.mult,
                R=["dim", "mv"], W=["Eang"], N=["Eang"])
            ACT("activation", mre, mre, AF.Exp, R=["mre"], W=["mre"], N=["mre"])
            sincos(ang, F, Ere, Eim, tf_, ti_, tg_, "E")
            DVE("tensor_tensor", Ere, Ere, mre, ALU.mult, R=["Ecos", "mre"], W=["Ere"], N=["Ere", "Ecos"])
            DVE("tensor_tensor", Eim, Eim, mre, ALU.mult, R=["Esin", "mre"], W=["Eim"], N=["Eim", "Esin"])
            nr = AR.alloc([16], F32)
            den = AR.alloc([16], F32)
            t1 = AR.alloc([16], F32)
            t2 = AR.alloc([16], F32)
            fre = AR.alloc([16], F32)
            fim = AR.alloc([16], F32)
            e1r = Ere[:, :, 8]
            e1i = Eim[:, :, 8]
            DVE("tensor_scalar", nr, e1r, -1.0, None, ALU.add, R=["Ere"], W=["nr"], N=["nr"])
            DVE("tensor_tensor", t1, are, are, ALU.mult, R=["are"], W=["t1"], N=["t1"])
            DVE("tensor_tensor", t2, aim, aim, ALU.mult, R=["aim"], W=["t2"], N=["t2"])
            DVE("tensor_tensor", den, t1, t2, ALU.add, R=["t1", "t2"], W=["den"], N=["den"])
            DVE("reciprocal", den, den, R=["den"], W=["den"], N=["den"])
            DVE("tensor_tensor", t1, nr, are, ALU.mult, R=["nr", "are"], W=["t1"], N=["t1"])
            DVE("tensor_tensor", t2, e1i, aim, ALU.mult, R=["Eim", "aim"], W=["t2"], N=["t2"])
            DVE("tensor_tensor", fre, t1, t2, ALU.add, R=["t1", "t2"], W=["fre"], N=["fre"])
            DVE("tensor_tensor", fre, fre, den, ALU.mult, R=["den"], W=["fre"], N=["fre"])
            DVE("tensor_tensor", t1, e1i, are, ALU.mult, R=["Eim", "are"], W=["t1"], N=["t1"])
            DVE("tensor_tensor", t2, nr, aim, ALU.mult, R=["nr", "aim"], W=["t2"], N=["t2"])
            DVE("tensor_tensor", fim, t1, t2, ALU.subtract, R=["t1", "t2"], W=["fim"], N=["fim"])
            DVE("tensor_tensor", fim, fim, den, ALU.mult, R=["den"], W=["fim"], N=["fim"])
            Bre = AR.alloc([16, 16], F32)
            Bim = AR.alloc([16, 16], F32)
            u1 = AR.alloc([16, 16], F32)
            freb = fre.unsqueeze(2).to_broadcast([128, 16, 16])
            fimb = fim.unsqueeze(2).to_broadcast([128, 16, 16])
            DVE("tensor_tensor", Bre, bre, freb, ALU.mult, R=["bre", "fre"], W=["Bre"], N=["Bre"])
            DVE("tensor_tensor", u1, bim, fimb, ALU.mult, R=["bim", "fim"], W=["u1"], N=["u1"])
            DVE("tensor_tensor", Bre, Bre, u1, ALU.subtract, R=["u1"], W=["Bre"], N=["Bre"])
            DVE("tensor_tensor", Bim, bim, freb, ALU.mult, R=["bim", "fre"], W=["Bim"], N=["Bim"])
            DVE("tensor_tensor", u1, bre, fimb, ALU.mult, R=["bre", "fim"], W=["u1"], N=["u1"])
            DVE("tensor_tensor", Bim, Bim, u1, ALU.add, R=["u1"], W=["Bim"], N=["Bim"])

            big = [16, 8, 16]
            tA = AR.alloc(big, F32)
            tB = AR.alloc(big, F32)

            def cprod(o_re, o_im, sl, yre, yim, yn_re, yn_im, neg_im, tag):
                er = Ere[:, :, sl].unsqueeze(3).to_broadcast([128, 16, 8, 16])
                ei = Eim[:, :, sl].unsqueeze(3).to_broadcast([128, 16, 8, 16])
                yr = yre.unsqueeze(2).to_broadcast([128, 16, 8, 16])
                yi = yim.unsqueeze(2).to_broadcast([128, 16, 8, 16])
                DVE("tensor_tensor", tA, er, yr, ALU.mult, R=["Ere", yn_re], W=["tA"], N=["tA"])
                DVE("tensor_tensor", tB, ei, yi, ALU.mult, R=["Eim", yn_im], W=["tB"], N=["tB"])
                DVE("tensor_tensor", o_re, tA, tB, ALU.subtract, R=["tA", "tB"], W=[tag + "re"], N=[tag + "re"])
                DVE("tensor_tensor", tA, er, yi, ALU.mult, R=["Ere", yn_im], W=["tA"], N=["tA"])
                DVE("tensor_tensor", tB, ei, yr, ALU.mult, R=["Eim", yn_re], W=["tB"], N=["tB"])
                if neg_im:
                    DVE("scalar_tensor_tensor", o_im, tA, -1.0, tB, ALU.mult, ALU.subtract,
                        R=["tA", "tB"], W=[tag + "im"], N=[tag + "im"])
                else:
                    DVE("tensor_tensor", o_im, tA, tB, ALU.add, R=["tA", "tB"], W=[tag + "im"], N=[tag + "im"])

            Bq_re = AR.alloc(big, F32)
            Bq_im = AR.alloc(big, F32)
            Cq_re = AR.alloc(big, F32)
            Cq_im = AR.alloc(big, F32)
            cprod(Bq_re, Bq_im, slice(0, 8), Bre, Bim, "Bre", "Bim", False, "Bq")
            cprod(Cq_re, Cq_im, slice(8, 16), cre, cim, "cre", "cim", True, "Cq")
            POOL("tensor_copy", Wc_re, Cq_re.rearrange("p i a b -> p i (a b)"), R=["Cqre"], W=["Wc_re"], N=["Wc_re"])
            POOL("tensor_copy", Wc_im, Cq_im.rearrange("p i a b -> p i (a b)"), R=["Cqim"], W=["Wc_im"], N=["Wc_im"])
            Bqm_re = [AR.alloc([16, 128], BF16) for _ in range(2)]
            Bqm_im = [AR.alloc([16, 128], BF16) for _ in range(2)]
            Sd.buf("Bqreb").new()
            Sd.buf("Bqimb").new()
            for e_ in range(2):
                o_ = slice(64 * (1 - e_), 64 * (1 - e_) + 64)
                k_ = slice(64 * e_, 64 * e_ + 64)
                POOL("memset", Bqm_re[e_][o_], 0.0, W=["Bqreb"])
                POOL("memset", Bqm_im[e_][o_], 0.0, W=["Bqimb"])
                POOL("tensor_copy", Bqm_re[e_][k_], Bq_re[k_].rearrange("p i a b -> p i (a b)"), R=["Bqre"], W=["Bqreb"])
                POOL("tensor_copy", Bqm_im[e_][k_], Bq_im[k_].rearrange("p i a b -> p i (a b)"), R=["Bqim"], W=["Bqimb"])
            tW = AR.alloc([128], F32)
            for g in range(32):
                i, e = g // 2, g % 2
                pb = 4 + (g % 2)
                sl = slice(64 * e, 64 * e + 64)
                PE("matmul", PSB(pb, 128), Bqm_re[e][:, i, :], Wc_re[:, i, :], start=True, stop=False,
                   R=["Bqreb", "Wc_re"], W=["psW%d" % pb], N=["psW%d" % pb])
                PE("matmul", PSB(pb, 128), Bqm_im[e][:, i, :], Wc_im[:, i, :], start=False, stop=True,
                   R=["Bqimb", "Wc_im"], W=["psW%d" % pb])
                DVE("tensor_tensor", tW, PSB(pb, 128), trif, ALU.mult, R=["psW%d" % pb, "trif"], W=["tW"], N=["tW"])
                DVE("scalar_tensor_tensor", W_sb[:, g, :], identf, dsk[:, g:g + 1], tW, ALU.mult, ALU.add,
                    R=["tW", "identf", "dsk"], W=["W_sb"])
            cprod(Bq_re, Bq_im, slice(16, 24), Bre, Bim, "Bre", "Bim", False, "Bq")
            for i in range(16):
                for (src, dst, sn, dn) in ((Bq_re, V_re, "Bqre", "V_re"), (Bq_im, V_im, "Bqim", "V_im")):
                    pb = 6 + (i % 2)
                    PE("transpose", PSB(pb, 128), src[:, i, :, :].rearrange("p a b -> p (a b)"), identf,
                       R=[sn, "identf"], W=["psV%d" % pb], N=["psV%d" % pb])
                    ACT("activation", dst[:, i, :], PSB(pb, 128), AF.Copy, R=["psV%d" % pb], W=[dn])
            if dbgC:
                DMA(W_dbg, W_sb, R=["W_sb"])
                DMA(Vre_dbg, V_re, R=["V_re"])
                DMA(Vim_dbg, V_im, R=["V_im"])
                DMA(Wcre_dbg, Wc_re, R=["Wc_re"])
                DMA(Wcim_dbg, Wc_im, R=["Wc_im"])
                DMA(RT_dbg[:, 0:16], Rsc, R=["Rsc"])
                DMA(RT_dbg[:, 16:32], Thr, R=["Thr"])
            Sd.barrier()
            AR.release(m0)

        def phase_A(seq):
            m0 = AR.mark()
            wf = AR.alloc([8, 2816], BF16)
            wt = AR.alloc([8, 392], BF16)
            stg = [AR.alloc([2816], F32) for _ in range(2)]
            for k in range(8):
                b = k % 2
                DMA(stg[b], C.wf_d[:, k, :], W=["stg%d" % b], N=["stg%d" % b])
                POOL("tensor_copy", wf[:, k, :], stg[b], R=["stg%d" % b], W=["wf"])
            wts = AR.alloc([8, 392], F32)
            DMA(wts, C.wt_d, W=["wts"], N=["wts"])
            POOL("tensor_copy", wt, wts, R=["wts"], W=["wt"], N=["wt"])
            xtok = [AR.alloc([1024], F32) for _ in range(2)]
            xbf = [AR.alloc([1024], BF16) for _ in range(2)]
            xT = [AR.alloc([8, 512], BF16) for _ in range(2)]
            fm = AR.alloc([22, 512], BF16)
            sqj = AR.alloc([128], BF16)
            ss = AR.alloc([1], F32)
            vv = AR.alloc([1], F32)
            sv = AR.alloc([1], F32)
            rstd = AR.alloc([1], F32)
            chat_c = AR.alloc([4, 128], BF16)
            chatT_c = AR.alloc([512], BF16)
            wq = AR.alloc([264], F32)
            sq = AR.alloc([256], F32)
            qn = AR.alloc([8], F32)
            aw = AR.alloc([8], F32)
            s1 = AR.alloc([1], F32)
            wn_c = AR.alloc([4, 8], F32)
            for ch in range(8):
                t0c = seq * S + ch * 512
                xb = xT[ch % 2]
                xn = "xT%d" % (ch % 2)
                Sd.buf(xn).new()
                Sd.buf("chat_c").new()
                Sd.buf("chatT_c").new()
                Sd.buf("wn_c").new()
                for tt in range(4):
                    t0 = t0c + tt * 128
                    j = (ch * 4 + tt) % 2
                    DMA(xtok[j], C.x_d[t0:t0 + 128, :], W=["xtok%d" % j], N=["xtok%d" % j])
                    POOL("tensor_copy", xbf[j], xtok[j], R=["xtok%d" % j], W=["xbf%d" % j], N=["xbf%d" % j])
                    tp = PSB(6 + j).bitcast(BF16).rearrange("p (a b) -> p a b", a=8)
                    Sd.buf("pstp%d" % j).new()
                    for k in range(8):
                        PE("transpose", tp[:, k, :], xbf[j][:, k * 128:(k + 1) * 128], identb,
                           R=["xbf%d" % j, "identb"], W=["pstp%d" % j])
                    ACT("activation", xb[:, :, tt * 128:(tt + 1) * 128], tp, AF.Copy, R=["pstp%d" % j], W=[xn])
                    tokps = PSB(5, 392)
                    Sd.buf("tokps").new()
                    for k in range(8):
                        PE("matmul", tokps, xb[:, k, tt * 128:(tt + 1) * 128], wt[:, k, :], start=(k == 0), stop=(k == 7),
                           R=[xn, "wt"], W=["tokps"])
                    ACT("activation", sqj, tokps[:, 0:128], AF.Square, accum_out=ss[:, 0:1], R=["tokps"], W=["ss", "sqj"], N=["ss", "sqj"])
                    DVE("tensor_scalar", vv, ss, 1.0 / 128, RMS_EPS, ALU.mult, ALU.add, R=["ss"], W=["vv"], N=["vv"])
                    ACT("activation", sv, vv, AF.Sqrt, R=["vv"], W=["sv"], N=["sv"])
                    DVE("reciprocal", rstd, sv, R=["sv"], W=["rstd"], N=["rstd"])
                    ACT("activation", chat_c[:, tt, :], tokps[:, 0:128], AF.Copy, scale=rstd[:, 0:1],
                        R=["tokps", "rstd"], W=["chat_c"])
                    tp2 = PSB(4).bitcast(BF16)[:, 0:128]
                    PE("transpose", tp2, chat_c[:, tt, :], identb, R=["chat_c", "identb"], W=["pstp2"], N=["pstp2"])
                    DVE("tensor_copy", chatT_c[:, tt * 128:(tt + 1) * 128], tp2, R=["pstp2"], W=["chatT_c"])
                    ACT("activation", wq, tokps[:, 128:392], AF.Copy, R=["tokps"], W=["wq"], N=["wq"])
                    DVE("tensor_tensor", sq, wq[:, 8:264], wq[:, 8:264], ALU.mult, R=["wq"], W=["sq"], N=["sq"])
                    DVE("tensor_reduce", qn, sq.rearrange("p (h d) -> p h d", h=8), AX.X, ALU.add, R=["sq"], W=["qn"], N=["qn"])
                    ACT("activation", qn, qn, AF.Sqrt, R=["qn"], W=["qn"], N=["qn"])
                    DVE("scalar_tensor_tensor", aw, wq[:, 0:8], -1.0, wq[:, 0:8], ALU.mult, ALU.max, R=["wq"], W=["aw"], N=["aw"])
                    DVE("tensor_tensor", aw, aw, qn, ALU.mult, R=["qn"], W=["aw"], N=["aw"])
                    DVE("tensor_reduce", s1, aw, AX.X, ALU.add, R=["aw"], W=["s1"], N=["s1"])
                    DVE("tensor_scalar", s1, s1, 1e-30, None, ALU.add, R=["s1"], W=["s1"], N=["s1"])
                    DVE("reciprocal", s1, s1, R=["s1"], W=["s1"], N=["s1"])
                    DVE("tensor_scalar", wn_c[:, tt, :], wq[:, 0:8], s1[:, 0:1], None, ALU.mult, R=["wq", "s1"], W=["wn_c"])
                DMA(chat_d[t0c:t0c + 512, :].rearrange("(a p) c -> p a c", p=128), chat_c, R=["chat_c"])
                DMA(chatT_d[:, t0c:t0c + 512], chatT_c, R=["chatT_c"])
                DMA(wn_d[t0c:t0c + 512, :].rearrange("(a p) h -> p a h", p=128), wn_c, R=["wn_c"])
                Sd.buf("fm").new()
                for cb in range(22):
                    pb = cb % 4
                    pn = "psfm%d" % pb
                    Sd.buf(pn).new()
                    for k in range(8):
                        PE("matmul", PSB(pb), wf[:, k, cb * 128:(cb + 1) * 128], xb[:, k, :], start=(k == 0), stop=(k == 7),
                           R=["wf", xn], W=[pn])
                    silu = (10 <= cb < 14) or cb >= 18
                    if silu:
                        ACT("activation", fm[:, cb, :], PSB(pb), AF.Silu, R=[pn], W=["fm"])
                    elif cb % 2 == 0:
                        ACT("activation", fm[:, cb, :], PSB(pb), AF.Copy, R=[pn], W=["fm"])
                    else:
                        DVE("tensor_copy", fm[:, cb, :], PSB(pb), R=[pn], W=["fm"])
                for (dd, c0_, c1_) in ((qT_d, 0, 4), (idx_d, 4, 10), (ga_d, 10, 14), (u_d, 14, 18), (gs_d, 18, 22)):
                    DMA(dd[:, :, t0c:t0c + 512].rearrange("j p t -> p j t"), fm[:, c0_:c1_, :], R=["fm"])
            Sd.barrier()
            AR.release(m0)

        def phase_B(seq):
            m0 = AR.mark()
            s0 = seq * S
            NQ = 32
            kiT = AR.alloc([3, S], BF16)
            chatT = AR.alloc([S], BF16)
            chtok = AR.alloc([32, 128], BF16)
            causb = AR.alloc([128], BF16)
            POOL("tensor_copy", causb, causf, R=["causf"], W=["causb"], N=["causb"])
            DMA(kiT, idx_d[3:6, :, s0:s0 + S].rearrange("j p t -> p j t"), W=["kiT"], N=["kiT"])
            DMA(chatT, chatT_d[:, s0:s0 + S], W=["chatT"], N=["chatT"])
            DMA(chtok, chat_d[s0:s0 + S, :].rearrange("(a p) c -> p a c", p=128), W=["chtok"], N=["chtok"])
            I_sb = [AR.alloc([S], F16) for _ in range(3)]
            mask = [AR.alloc([S], BF16) for _ in range(2)]
            maskT = [AR.alloc([32, 128], BF16) for _ in range(2)]
            Rsb = [AR.alloc([512], BF16) for _ in range(3)]
            diagw = [AR.alloc([8, 128], BF16) for _ in range(3)]
            qlat = [AR.alloc([8, 128], BF16) for _ in range(3)]
            sqs = [AR.alloc([8, 128], BF16) for _ in range(3)]
            esb = [AR.alloc([512], BF16) for _ in range(3)]
            psb_ = [AR.alloc([512], BF16) for _ in range(3)]
            rl = [AR.alloc([512], F32) for _ in range(2)]
            on = [AR.alloc([512], BF16) for _ in range(2)]
            qTb = [AR.alloc([4, 128], BF16) for _ in range(4)]
            idxq = [AR.alloc([3, 128], BF16) for _ in range(4)]
            wnb = [AR.alloc([8], F32) for _ in range(4)]
            gab = [AR.alloc([4, 128], BF16) for _ in range(4)]
            mixo = AR.alloc([4, 128], BF16)
            o_sb = [AR.alloc([512], F32) for _ in range(2)]
            att_sb = AR.alloc([4, 128], F32)
            pmx = [AR.alloc([1], BF16) for _ in range(3)]
            nbp = AR.alloc([1], F32)
            negB = [AR.alloc([1], F32) for _ in range(3)]
            mid = [AR.alloc([1], F32) for _ in range(2)]
            cnt = AR.alloc([1], F32)
            mm_ = AR.alloc([1], F32)
            OB = (0, 6)
            LB = (1, 7)
            ridx = [0]

            def X(n, gen=None, RX=0):
                t0 = s0 + n * 128
                N = (n + 1) * 128
                a3, a2 = n % 4, n % 3
                DMA(qTb[a3], qT_d[:, :, t0:t0 + 128].rearrange("j p t -> p j t"), W=["qTb%d" % a3], N=["qTb%d" % a3])
                DMA(idxq[a3], idx_d[0:3, :, t0:t0 + 128].rearrange("j p t -> p j t"), W=["idxq%d" % a3], N=["idxq%d" % a3])
                DMA(wnb[a3], wn_d[t0:t0 + 128, :], W=["wnb%d" % a3], N=["wnb%d" % a3])
                DMA(gab[a3], ga_d[:, :, t0:t0 + 128].rearrange("j p t -> p j t"), W=["gab%d" % a3], N=["gab%d" % a3])
                qlps = psum[:, 2:4, :].rearrange("p b (h t) -> p (b h) t", h=4)
                Sd.buf("psL2").new()
                Sd.buf("psL3").new()
                for h in range(8):
                    PE("matmul", qlps[:, h, :], wukp[:, h, :], qTb[a3][:, h // 2, :], start=True, stop=True,
                       R=["qTb%d" % a3, "wukp"], W=["psL%d" % (2 + h // 4)])
                ACT("activation", qlat[a2], qlps, AF.Copy, R=["psL2", "psL3"], W=["qlat%d" % a2], N=["qlat%d" % a2])
                ACT("activation", sqs[a2], qlps, AF.Square, R=["psL2", "psL3"], W=["sqs%d" % a2], N=["sqs%d" % a2])
                POOL("tensor_tensor", diagw[a2], identf.unsqueeze(1).to_broadcast([128, 8, 128]),
                     wnb[a3].unsqueeze(2).to_broadcast([128, 8, 128]), ALU.mult, R=["identf", "wnb%d" % a3],
                     W=["diagw%d" % a2], N=["diagw%d" % a2])
                nkc = (N + 511) // 512
                xstep = [0]
                xem = [0]
                In = "I_sb%d" % a2
                Sd.buf(In).new()
                for kc in range(nkc):
                    w = min(512, N - kc * 512)
                    k0 = kc * 512
                    last = (kc == nkc - 1)
                    Sd.buf("psI").new()
                    pend = None
                    for h in range(8):
                        tl, r = h // 3, h % 3
                        pb = 2 + (h % 2)
                        pn = "psL%d" % pb
                        PE("matmul", PSB(pb, w), idxq[a3][:, tl, :], kiT[:, r, k0:k0 + w], start=True, stop=True,
                           R=["idxq%d" % a3, "kiT"], W=[pn], N=[pn])
                        if pend is not None:
                            ph, prb, prn = pend
                            PE("matmul", PSB(4, w), diagw[a2][:, ph, :], Rsb[prb][:, 0:w], start=(ph == 0), stop=False,
                               R=[prn, "diagw%d" % a2], W=["psI"])
                        rb = ridx[0] % 3
                        ridx[0] += 1
                        rn = "Rsb%d" % rb
                        ACT("activation", Rsb[rb][:, 0:w], PSB(pb, w), AF.Relu, R=[pn], W=[rn], N=[rn])
                        pend = (h, rb, rn)
                        if gen is not None:
                            xstep[0] += 1
                            want = (RX * xstep[0]) // (8 * nkc)
                            while xem[0] < want:
                                try:
                                    next(gen)
                                except StopIteration:
                                    break
                                xem[0] += 1
                    ph, prb, prn = pend
                    PE("matmul", PSB(4, w), diagw[a2][:, ph, :], Rsb[prb][:, 0:w], start=False, stop=(not last),
                       R=[prn, "diagw%d" % a2], W=["psI"])
                    if last:
                        PE("matmul", psum[:, 4, w - 128:w], identb, causb, start=False, stop=True,
                           R=["identb", "causb"], W=["psI"])
                    ACT("activation", I_sb[a2][:, k0:k0 + w], PSB(4, w), AF.Copy, R=["psI"], W=[In])

            def F_dve(n):
                a2 = n % 3
                DVE("tensor_reduce", pmx[a2], sqs[a2].rearrange("p h t -> p (h t)"), AX.X, ALU.max, R=["sqs%d" % a2], W=["pmx%d" % a2], N=["pmx%d" % a2])

            def F_rest(n):
                a2 = n % 3
                PE("matmul", psum[:, 1, 0:1], onesb, pmx[a2], start=True, stop=True, R=["pmx%d" % a2, "onesb"], W=["psl0"], N=["psl0"])
                ACT("activation", nbp, psum[:, 1, 0:1], AF.Sqrt, scale=128.0 * 1.03, R=["psl0"], W=["nbp"], N=["nbp"])
                POOL("tensor_scalar", negB[a2], nbp, -1.0, None, ALU.mult, R=["nbp"], W=["negB%d" % a2], N=["negB%d" % a2])

            def Y_dve_gen(n):
                N = (n + 1) * 128
                a2 = n % 3
                In = "I_sb%d" % a2
                hs = [12.0] + [4.0 * (0.5 ** i) for i in range(13)]
                DVE("memset", mid[0], -RB + hs[0], W=["mid0"], N=["mid0"])
                for it in range(len(hs)):
                    a, b = it % 2, (it + 1) % 2
                    h_k = hs[it]
                    h_n = hs[it + 1] if it + 1 < len(hs) else 0.0
                    DVE("tensor_scalar", mask[n % 2][:, 0:N], I_sb[a2][:, 0:N], mid[a][:, 0:1], 0.0, ALU.is_ge, ALU.add,
                        accum_out=cnt[:, 0:1], R=[In, "mid%d" % a], W=["mask%d" % (n % 2), "cnt"], N=["mask%d" % (n % 2), "cnt"])
                    DVE("tensor_scalar", mm_, cnt, 255.5, h_k, ALU.is_ge, ALU.mult, R=["cnt"], W=["mm"], N=["mm"])
                    DVE("scalar_tensor_tensor", mid[b], mm_, h_n - h_k, mid[a], ALU.add, ALU.add,
                        R=["mm", "mid%d" % a], W=["mid%d" % b], N=["mid%d" % b])
                    if it + 1 < len(hs):
                        yield
                fin = len(hs) % 2
                DVE("tensor_scalar", mask[n % 2][:, 0:N], I_sb[a2][:, 0:N], mid[fin][:, 0:1], None, ALU.is_ge,
                    R=[In, "mid%d" % fin], W=["mask%d" % (n % 2)], N=["mask%d" % (n % 2)])

            def Y_dve(n):
                for _ in Y_dve_gen(n):
                    pass

            def Y_pe(n):
                a2 = n % 2
                Mn = "maskT%d" % a2
                Sd.buf(Mn).new()
                for g4 in range((n + 4) // 4):
                    nb = min(4, n + 1 - 4 * g4)
                    tpm = PSB(5).bitcast(BF16).rearrange("p (a b) -> p a b", a=8)
                    Sd.buf("ps5").new()
                    for u_ in range(nb):
                        kb = 4 * g4 + u_
                        PE("transpose", tpm[:, u_, :], mask[n % 2][:, kb * 128:(kb + 1) * 128], identb,
                           R=["mask%d" % (n % 2), "identb"], W=["ps5"])
                    ACT("activation", maskT[a2][:, 4 * g4:4 * g4 + nb, :], tpm[:, 0:nb, :], AF.Copy, R=["ps5"], W=[Mn])

            def Z(n, gen=None, RZ=14):
                a2 = n % 3
                am = n % 2
                Mn = "maskT%d" % am
                SB = (2, 3, 4)
                SN = ("psL2", "psL3", "psI")
                nsteps = 2 * (n + 1)
                step = 0
                emitted = 0
                for hh in range(2):
                    rhs_q = qlat[a2][:, 4 * hh:4 * hh + 4, :].rearrange("p h t -> p (h t)")
                    on_, ln_ = "pso%d" % hh, "psl%d" % hh
                    Sd.buf(on_).new()
                    Sd.buf(ln_).new()

                    def score(kb_):
                        i_ = kb_ % 3
                        PE("matmul", PSB(SB[i_]), chatT[:, kb_ * 128:(kb_ + 1) * 128], rhs_q, start=True, stop=True,
                           R=["chatT", "qlat%d" % a2], W=[SN[i_]], N=[SN[i_]])
                    score(0)
                    if n >= 1:
                        score(1)
                    for kb in range(n + 1):
                        i3 = kb % 3
                        if kb + 2 <= n:
                            score(kb + 2)
                        ACT("activation", esb[i3], PSB(SB[i3]), AF.Exp, bias=negB[a2][:, 0:1], R=[SN[i3], "negB%d" % a2],
                            W=["esb%d" % i3], N=["esb%d" % i3])
                        eng = POOL
                        eng("tensor_tensor", psb_[i3].rearrange("p (h t) -> p h t", h=4),
                            esb[i3].rearrange("p (h t) -> p h t", h=4),
                            maskT[am][:, kb, :].unsqueeze(1).to_broadcast([128, 4, 128]), ALU.mult,
                            R=["esb%d" % i3, Mn], W=["psb%d" % i3], N=["psb%d" % i3])
                        PE("matmul", PSB(OB[hh]), chtok[:, kb, :], psb_[i3], start=(kb == 0), stop=(kb == n),
                           R=["chtok", "psb%d" % i3], W=[on_])
                        PE("matmul", PSB(LB[hh]), onesb, psb_[i3], start=(kb == 0), stop=(kb == n),
                           R=["onesb", "psb%d" % i3], W=[ln_])
                        step += 1
                        if gen is not None:
                            want = (RZ * step) // nsteps
                            while emitted < want:
                                try:
                                    next(gen)
                                except StopIteration:
                                    gen = None
                                    break
                                emitted += 1

            def E(n):
                t0 = s0 + n * 128
                a3 = n % 4
                for hh in range(2):
                    ACT("activation", rl[hh], PSB(LB[hh]), AF.Ln, R=["psl%d" % hh], W=["rl%d" % hh], N=["rl%d" % hh])
                    ACT("activation", rl[hh], rl[hh], AF.Exp, scale=-1.0, R=["rl%d" % hh], W=["rl%d" % hh], N=["rl%d" % hh])
                    ACT("activation", o_sb[hh], PSB(OB[hh]), AF.Copy, R=["pso%d" % hh], W=["o_sb%d" % hh], N=["o_sb%d" % hh])
                    POOL("tensor_tensor", on[hh], o_sb[hh], rl[hh], ALU.mult, R=["o_sb%d" % hh, "rl%d" % hh],
                         W=["on%d" % hh], N=["on%d" % hh])
                attps = psum[:, 5, :].rearrange("p (j t) -> p j t", j=4)
                Sd.buf("ps5").new()
                for h in range(8):
                    hh, hl = h // 4, h % 4
                    j, e = h // 2, h % 2
                    PE("matmul", attps[64 * e:64 * e + 64, j, :], wuvp[:, h, :], on[hh][:, hl * 128:(hl + 1) * 128],
                       start=True, stop=True, R=["on%d" % hh, "wuvp"], W=["ps5"])
                ACT("activation", att_sb, attps, AF.Copy, R=["ps5"], W=["att_sb"], N=["att_sb"])
                POOL("tensor_tensor", mixo, att_sb, gab[a3], ALU.mult, R=["att_sb", "gab%d" % a3], W=["mixo"], N=["mixo"])
                DMA(mix_d[0:4, :, t0:t0 + 128].rearrange("j p t -> p j t"), mixo, R=["mixo"])

            X(0)
            X(1)
            X(2)
            for k_ in range(2):
                F_dve(k_)
                F_rest(k_)
            Y_dve(0)
            Y_pe(0)
            for n in range(NQ):
                g_ = Y_dve_gen(n + 1) if n + 1 < NQ else None
                zt = 1.8 * (n + 1)
                xt = 1.2 * (n + 4) if n + 3 < NQ else 0.0
                RZ = int(round(14 * zt / (zt + xt)))
                Z(n, g_, RZ)
                if n + 3 < NQ:
                    X(n + 3, g_, 14 - RZ)
                if g_ is not None:
                    for _ in g_:
                        pass
                if n + 2 < NQ:
                    F_dve(n + 2)
                if n + 1 < NQ:
                    Y_pe(n + 1)
                E(n)
                if n + 2 < NQ:
                    F_rest(n + 2)
            Sd.barrier()
            AR.release(m0)

        def phase_C(seq):
            m0 = AR.mark()
            s0 = seq * S
            wglu = AR.alloc([4, 1024], BF16)
            stg = AR.alloc([4, 1024], F32)
            DMA(stg, C.wglu_d, W=["stgg"], N=["stgg"])
            POOL("tensor_copy", wglu, stg, R=["stgg"], W=["wglu"], N=["wglu"])
            uT = [AR.alloc([S], BF16) for _ in range(2)]
            Ub = [AR.alloc([512], BF16) for _ in range(2)]
            angt = AR.alloc([512], F32)
            C1 = AR.alloc([512], F32)
            S1 = AR.alloc([512], F32)
            tf_ = AR.alloc([512], F32)
            tg_ = AR.alloc([512], F32)
            ti_ = AR.alloc([512], I32)
            ta = AR.alloc([512], F32)
            tb = AR.alloc([512], F32)
            Stre = AR.alloc([512], F32)
            Stim = AR.alloc([512], F32)
            Zre = AR.alloc([512], F32)
            Zim = AR.alloc([512], F32)
            Xre = [AR.alloc([512], BF16) for _ in range(2)]
            Xim = [AR.alloc([512], BF16) for _ in range(2)]
            Yg = AR.alloc([8, 512], BF16)
            ygT = AR.alloc([4, S], BF16)
            sig = AR.alloc([512], F32)
            ssm = AR.alloc([512], F32)
            gsb = AR.alloc([512], BF16)
            mxo = AR.alloc([512], BF16)
            for e in range(2):
                POOL("memset", Xre[e], 0.0, W=["Xre%d" % e], N=["Xre%d" % e])
                POOL("memset", Xim[e], 0.0, W=["Xim%d" % e], N=["Xim%d" % e])
            for q in range(4):
                ub = uT[q % 2]
                un = "uT%d" % (q % 2)
                DMA(ub, u_d[q][:, s0:s0 + S], W=[un], N=[un])
                Sd.buf("Yg").new()
                for ip in range(4):
                    i = 4 * q + ip
                    for e in range(2):
                        gl = 2 * ip + e
                        pn = "psU%d" % e
                        Sd.buf(pn).new()
                        for j in range(8):
                            x0 = 112 + 16 * (gl - j)
                            PE("matmul", PSB(e), dgb[:, gl, x0:x0 + 128], ub[:, j:S:8], start=(j == 0), stop=(j == 7),
                               R=[un, "dgb"], W=[pn])
                        ACT("activation", Ub[e], PSB(e), AF.Copy, R=[pn], W=["Ub%d" % e], N=["Ub%d" % e])
                    Sd.buf("psSre").new()
                    Sd.buf("psSim").new()
                    for e in range(2):
                        sl = slice(64 * e, 64 * e + 64)
                        PE("matmul", psum[sl, 2, :], V_re[:, i, sl], Ub[e], start=True, stop=True,
                           R=["V_re", "Ub%d" % e], W=["psSre"])
                        PE("matmul", psum[sl, 3, :], V_im[:, i, sl], Ub[e], start=True, stop=True,
                           R=["V_im", "Ub%d" % e], W=["psSim"])
                    DVE("tensor_scalar", angt, kkf, Thr[:, i:i + 1], None, ALU.mult, R=["kkf", "Thr"], W=["Tang"], N=["Tang"])
                    sincos(angt, 512, C1, S1, tf_, ti_, tg_, "T")
                    DVE("tensor_tensor", ta, PSB(2), C1, ALU.mult, R=["psSre", "Tcos"], W=["ta"], N=["ta"])
                    DVE("tensor_tensor", tb, PSB(3), S1, ALU.mult, R=["psSim", "Tsin"], W=["tb"], N=["tb"])
                    DVE("tensor_tensor", Stre, ta, tb, ALU.add, R=["ta", "tb"], W=["Stre"], N=["Stre"])
                    DVE("tensor_tensor", ta, PSB(3), C1, ALU.mult, R=["psSim", "Tcos"], W=["ta"], N=["ta"])
                    DVE("tensor_tensor", tb, PSB(2), S1, ALU.mult, R=["psSre", "Tsin"], W=["tb"], N=["tb"])
                    DVE("tensor_tensor", Stim, ta, tb, ALU.subtract, R=["ta", "tb"], W=["Stim"], N=["Stim"])
                    Rbc = Rsc[:, i:i + 1].to_broadcast([128, 512])
                    DVE("tensor_tensor_scan", Zre, Rbc, Stre, 0.0, ALU.mult, ALU.add, R=["Stre", "Rsc"], W=["Zre"], N=["Zre"])
                    DVE("tensor_tensor_scan", Zim, Rbc, Stim, 0.0, ALU.mult, ALU.add, R=["Stim", "Rsc"], W=["Zim"], N=["Zim"])
                    DVE("tensor_tensor", ta, Zre, C1, ALU.mult, R=["Zre", "Tcos"], W=["ta"], N=["ta"])
                    DVE("tensor_tensor", tb, Zim, S1, ALU.mult, R=["Zim", "Tsin"], W=["tb"], N=["tb"])
                    for e in range(2):
                        sl = slice(64 * e, 64 * e + 64)
                        DVE("tensor_tensor", Xre[e][sl, 1:512], ta[sl, 0:511], tb[sl, 0:511], ALU.subtract,
                            R=["ta", "tb"], W=["Xre%d" % e], N=["Xre%d" % e])
                    DVE("tensor_tensor", ta, Zim, C1, ALU.mult, R=["Zim", "Tcos"], W=["ta"], N=["ta"])
                    DVE("tensor_tensor", tb, Zre, S1, ALU.mult, R=["Zre", "Tsin"], W=["tb"], N=["tb"])
                    for e in range(2):
                        sl = slice(64 * e, 64 * e + 64)
                        DVE("tensor_tensor", Xim[e][sl, 1:512], ta[sl, 0:511], tb[sl, 0:511], ALU.add,
                            R=["ta", "tb"], W=["Xim%d" % e], N=["Xim%d" % e])
                    for e in range(2):
                        g = 2 * i + e
                        gl = 2 * ip + e
                        pb = 4 + e
                        pn = "psY%d" % e
                        PE("matmul", PSB(pb), W_sb[:, g, :], Ub[e], start=True, stop=False,
                           R=["W_sb", "Ub%d" % e], W=[pn], N=[pn])
                        PE("matmul", PSB(pb), Wc_re[:, i, :], Xre[e], start=False, stop=False,
                           R=["Wc_re", "Xre%d" % e], W=[pn])
                        PE("matmul", PSB(pb), Wc_im[:, i, :], Xim[e], start=False, stop=True,
                           R=["Wc_im", "Xim%d" % e], W=[pn])
                        ACT("activation", Yg[:, gl, :], PSB(pb), AF.Gelu_apprx_tanh, R=[pn], W=["Yg"])
                Sd.buf("ygT").new() if q == 0 else None
                ygv = ygT[:, q, :].rearrange("p (k t) -> p k t", t=8)
                for tau in range(8):
                    pb = 6 + (tau % 2)
                    pn = "psC%d" % pb
                    Sd.buf(pn).new()
                    for gl in range(8):
                        x0 = 112 + 16 * (tau - gl)
                        PE("matmul", PSB(pb), dgb[:, tau, x0:x0 + 128], Yg[:, gl, :], start=(gl == 0), stop=(gl == 7),
                           R=["Yg", "dgb"], W=[pn])
                    if tau % 2 == 0:
                        DVE("tensor_copy", ygv[:, :, tau], PSB(pb), R=[pn], W=["ygT"])
                    else:
                        ACT("activation", ygv[:, :, tau], PSB(pb), AF.Copy, R=[pn], W=["ygT"])
            if dbgC and seq == 0:
                DMA(ygT_dbg.rearrange("q p t -> p q t"), ygT, R=["ygT"])
            for tc in range(8):
                c0 = tc * 512
                for v in range(4):
                    Sd.buf("psU0").new()
                    Sd.buf("psU1").new()
                    for q in range(4):
                        PE("matmul", PSB(0), wglu[:, q, v * 128:(v + 1) * 128], ygT[:, q, c0:c0 + 512],
                           start=(q == 0), stop=(q == 3), R=["wglu", "ygT"], W=["psU0"])
                    for q in range(4):
                        PE("matmul", PSB(1), wglu[:, q, 512 + v * 128:512 + (v + 1) * 128], ygT[:, q, c0:c0 + 512],
                           start=(q == 0), stop=(q == 3), R=["wglu", "ygT"], W=["psU1"])
                    DMA(gsb, gs_d[v][:, s0 + c0:s0 + c0 + 512], W=["gsb"], N=["gsb"])
                    ACT("activation", sig, PSB(1), AF.Sigmoid, bias=bglu[:, 4 + v:5 + v], R=["psU1", "bglu"], W=["sig"], N=["sig"])
                    DVE("scalar_tensor_tensor", ssm, PSB(0), bglu[:, v:v + 1], sig, ALU.add, ALU.mult,
                        R=["psU0", "sig", "bglu"], W=["ssm"], N=["ssm"])
                    DVE("tensor_tensor", mxo, ssm, gsb, ALU.mult, R=["ssm", "gsb"], W=["mxo"], N=["mxo"])
                    DMA(mix_d[4 + v][:, s0 + c0:s0 + c0 + 512], mxo, R=["mxo"])
            Sd.barrier()
            AR.release(m0)

        def phase_D(seq):
            m0 = AR.mark()
            s0 = seq * S
            wout = AR.alloc([8, 1024], BF16)
            stg = [AR.alloc([1024], F32) for _ in range(2)]
            for k in range(8):
                b = k % 2
                DMA(stg[b], C.wout_d[:, k, :], W=["stgo%d" % b], N=["stgo%d" % b])
                POOL("tensor_copy", wout[:, k, :], stg[b], R=["stgo%d" % b], W=["wout"])
            lng = AR.alloc([1024], F32)
            lnb = AR.alloc([1024], F32)
            DMA(lng, C.lng_d, W=["lng"], N=["lng"])
            DMA(lnb, C.lnb_d, W=["lnb"], N=["lnb"])
            mixT = [AR.alloc([8, 128], BF16) for _ in range(2)]
            xt = [AR.alloc([1024], F32) for _ in range(2)]
            rr = [AR.alloc([1024], F32) for _ in range(2)]
            st = AR.alloc([2, 6], F32)
            mvv = AR.alloc([2], F32)
            rs_ = AR.alloc([1], F32)
            for tt in range(32):
                t0 = s0 + tt * 128
                j = tt % 2
                DMA(mixT[j], mix_d[:, :, t0:t0 + 128].rearrange("j p t -> p j t"), W=["mixT%d" % j], N=["mixT%d" % j])
                DMA(xt[j], C.x_d[t0:t0 + 128, :], W=["xt%d" % j], N=["xt%d" % j])
                for half in range(2):
                    pb = 2 * j + half
                    pn = "psD%d" % pb
                    Sd.buf(pn).new()
                    for e in range(8):
                        PE("matmul", PSB(pb), mixT[j][:, e, :], wout[:, e, half * 512:(half + 1) * 512],
                           start=(e == 0), stop=(e == 7), R=["mixT%d" % j, "wout"], W=[pn])
                rn = "rr%d" % j
                Sd.buf(rn).new()
                for half in range(2):
                    pb = 2 * j + half
                    DVE("scalar_tensor_tensor", rr[j][:, half * 512:(half + 1) * 512], xt[j][:, half * 512:(half + 1) * 512],
                        ALPHA, PSB(pb), ALU.mult, ALU.add, R=["xt%d" % j, "psD%d" % pb], W=[rn])
                Sd.buf("st").new()
                for half in range(2):
                    DVE("bn_stats", st[:, half, :], rr[j][:, half * 512:(half + 1) * 512], R=[rn], W=["st"])
                DVE("bn_aggr", mvv, st.rearrange("p a b -> p (a b)"), R=["st"], W=["mvv"], N=["mvv"])
                DVE("tensor_scalar", rs_, mvv[:, 1:2], LN_EPS, None, ALU.add, R=["mvv"], W=["rs"], N=["rs"])
                ACT("activation", rs_, rs_, AF.Sqrt, R=["rs"], W=["rs"], N=["rs"])
                DVE("reciprocal", rs_, rs_, R=["rs"], W=["rs"], N=["rs"])
                DVE("tensor_scalar", rr[j], rr[j], mvv[:, 0:1], rs_[:, 0:1], ALU.subtract, ALU.mult,
                    R=["mvv", "rs", rn], W=[rn], N=[rn])
                POOL("tensor_tensor", rr[j], rr[j], lng, ALU.mult, R=[rn, "lng"], W=[rn], N=[rn])
                POOL("tensor_tensor", rr[j], rr[j], lnb, ALU.add, R=[rn, "lnb"], W=[rn], N=[rn])
                DMA(C.out_d[t0:t0 + 128, :], rr[j], R=[rn])
            Sd.barrier()
            AR.release(m0)

        for l in range(NL):
            set_layer(l)
            layer_consts()
            if "C" in phases:
                s5_setup()
            for seq in range(NSEQ):
                if "A" in phases:
                    phase_A(seq)
                if "B" in phases:
                    phase_B(seq)
                if "C" in phases:
                    phase_C(seq)
                if "D" in phases:
                    phase_D(seq)
        Sd.barrier()
        Sd.cnt["sp"] += 1

        def fin(e):
            return e.nop()
        Sd._emit("sp", fin, {}, (Sd.sem["sp"], 1))

        with nc.Block() as block:
            @block.tensor
            def _(e):
                for th in Sd.thunks["pe"]:
                    th(e)

            @block.scalar
            def _(e):
                for th in Sd.thunks["act"]:
                    th(e)

            @block.vector
            def _(e):
                for th in Sd.thunks["dve"]:
                    th(e)

            @block.gpsimd
            def _(e):
                for th in Sd.thunks["pool"]:
                    th(e)

            @block.sync
            def _(e):
                for th in Sd.thunks["sp"]:
                    th(e)
    return nc


def _consts():
    ident = np.eye(128, dtype=np.float32)
    t = np.arange(128)
    caus = np.where(t[None, :] <= t[:, None], 0.0, NEG).astype(np.float32)
    jj = t // 16
    tri = (jj[None, :] >= jj[:, None]).astype(np.float32)
    dg = np.zeros((128, 8, 352), np.float32)
    for g in range(8):
        for r in range(16 * g, 16 * g + 16):
            dg[r, g, 112 + r] = 1.0
    powers = [-(j + 1) for j in range(8)] + [tau + 1 for tau in range(8)] + [7 - j for j in range(8)]
    mvals = np.tile(np.asarray(powers, np.float32)[None, :], (128, 1))
    kk = np.tile(np.arange(1, 513, dtype=np.float32)[None, :], (128, 1))
    return dict(ident=ident, caus=caus, tri=tri, dg=dg, mvals=mvals, kk=kk)


def _ptile(a):
    rest = a.shape[2:]
    a = a.reshape((16, 2, 64) + rest)
    perm = (1, 2, 0) + tuple(range(3, 3 + len(rest)))
    return np.ascontiguousarray(a.transpose(perm).reshape((128, 16) + rest))


def prep_layer(l, inp):
    f = np.float32
    w_in = np.asarray(inp["w_in"][l], f)
    zeros32 = np.zeros((D, 32), f)
    q = w_in[:, 0:512]
    ckv = w_in[:, 512:640]
    qidx = w_in[:, 640:896]
    kidx = w_in[:, 896:928]
    widx = w_in[:, 928:936]
    ga = w_in[:, 936:1448]
    u = w_in[:, 1448:1960]
    gs = w_in[:, 1960:2472]
    qh = [qidx[:, 32 * h:32 * h + 32] for h in range(8)]
    idxA = np.concatenate([qh[0], qh[1], qh[2], zeros32], 1)
    idxB = np.concatenate([qh[3], qh[4], qh[5], zeros32], 1)
    idxC = np.concatenate([qh[6], qh[7], zeros32, zeros32], 1)
    k0_ = np.concatenate([kidx, zeros32, zeros32, zeros32], 1)
    k1_ = np.concatenate([zeros32, kidx, zeros32, zeros32], 1)
    k2_ = np.concatenate([zeros32, zeros32, kidx, zeros32], 1)
    wfm = np.concatenate([q, idxA, idxB, idxC, k0_, k1_, k2_, ga, u, gs], 1)
    wtm = np.concatenate([ckv, widx, qidx], 1)
    tile_k = lambda m: np.ascontiguousarray(m.reshape(8, 128, m.shape[1]).transpose(1, 0, 2))
    d = {}
    d["wf"] = tile_k(wfm)
    d["wt"] = tile_k(wtm)
    w_uk = np.asarray(inp["w_uk"][l], f)
    w_uv = np.asarray(inp["w_uv"][l], f)
    wukz = np.zeros((128, 8, 128), f)
    for h_ in range(8):
        e_ = h_ % 2
        wukz[64 * e_:64 * e_ + 64, h_, :] = w_uk[h_].T
    d["wuk"] = wukz
    d["wuv"] = np.ascontiguousarray(w_uv.transpose(1, 0, 2))
    g = np.asarray(inp["kv_norm_g"][l], f)
    d["kvg_bc"] = np.ascontiguousarray(np.tile(g[None, :], (128, 1)))
    d["kvg_col"] = np.ascontiguousarray(g[:, None])
    d["ldt"] = _ptile(np.tile(np.asarray(inp["log_dt"][l], f)[:, None], (1, 64)))
    d["are"] = _ptile(np.asarray(inp["a_re"][l], f))
    d["aim"] = _ptile(np.asarray(inp["a_im"][l], f))
    d["bre"] = _ptile(np.asarray(inp["b_re"][l], f))
    d["bim"] = _ptile(np.asarray(inp["b_im"][l], f))
    d["cre"] = _ptile(np.asarray(inp["c_re"][l], f).transpose(0, 2, 1))
    d["cim"] = _ptile(np.asarray(inp["c_im"][l], f).transpose(0, 2, 1))
    dsk = np.asarray(inp["d_skip"][l], f)
    d["dsk"] = np.ascontiguousarray(np.tile(dsk.T, (8, 1)))
    d["wglu"] = np.ascontiguousarray(np.asarray(inp["w_glu"][l], f).reshape(4, 128, 1024).transpose(1, 0, 2))
    d["bglu"] = np.ascontiguousarray(np.asarray(inp["b_glu"][l], f).reshape(8, 128).T)
    d["wout"] = np.ascontiguousarray(np.asarray(inp["w_out"][l], f).reshape(8, 128, 1024).transpose(1, 0, 2))
    d["lng"] = np.ascontiguousarray(np.tile(np.asarray(inp["ln_g"][l], f)[None, :], (128, 1)))
    d["lnb"] = np.ascontiguousarray(np.tile(np.asarray(inp["ln_b"][l], f)[None, :], (128, 1)))
    d.update(_consts())
    return d


def kernel(**inputs):
    x = np.ascontiguousarray(np.asarray(inputs["x"], np.float32))
    B = x.shape[0]
    h = x.reshape(NCORES, T, D)
    per = [prep_layer(l, inputs) for l in range(DEPTH)]
    cn = _consts()
    w = {k: np.ascontiguousarray(np.stack([p[k] for p in per])) for k in per[0] if k not in cn}
    w.update(cn)
    in_maps = [dict(w, x=np.ascontiguousarray(h[c])) for c in range(NCORES)]
    nc = build_model()
    res = run_bass_kernel_spmd(nc, in_maps, core_ids=list(range(NCORES)))
    out = np.stack([np.asarray(r["out"], np.float32) for r in res.results])
    return out.reshape(B, S, D).astype(np.float32)
```

```python
import math
import os
from contextlib import ExitStack

import numpy as np
import concourse.bass as bass
import concourse.mybir as mybir
from concourse.bass_utils import run_bass_kernel_spmd

F32 = mybir.dt.float32
BF16 = mybir.dt.bfloat16
F16 = mybir.dt.float16
I32 = mybir.dt.int32
U8 = mybir.dt.uint8
AF = mybir.ActivationFunctionType
ALU = mybir.AluOpType
AX = mybir.AxisListType

DEPTH = 4
NCORES = 8
NSEQ = 2
S = 4096
T = NSEQ * S
D = 1024
ALPHA = (2 * DEPTH) ** 0.25
LN_EPS = 1e-5
RMS_EPS = 1e-6
NEG = -60000.0
RB = 16.0
NIT = 15
PI = math.pi
DSZ = {F32: 4, BF16: 2, F16: 2, I32: 4, U8: 1}


def merge(*ds):
    out = {}
    for d in ds:
        if not d:
            continue
        for k, v in d.items():
            if out.get(k, 0) < v:
                out[k] = v
    return out


class Buf:
    def __init__(self):
        self.prev = {}
        self.ws = {}
        self.rs = {}

    def new(self):
        self.prev = merge(self.prev, self.rs, self.ws)
        self.ws = {}
        self.rs = {}


class Sched:
    ENG = ("pe", "act", "dve", "pool", "sp")

    def __init__(self, nc, es):
        self.nc = nc
        self.sem = {k: es.enter_context(nc.semaphore("s_" + k)) for k in self.ENG}
        self.cnt = {k: 0 for k in self.ENG}
        self.thunks = {k: [] for k in self.ENG}
        self.seen = {k: {} for k in self.ENG}
        self.pending = {k: {} for k in self.ENG}
        self.NDMA = 16
        for i in range(self.NDMA):
            self.sem["d%d" % i] = es.enter_context(nc.semaphore("s_d%d" % i))
        self.dcnt = [0] * self.NDMA
        self.dnext = 0
        self.bufs = {}
        self.ninst = 0

    def buf(self, name):
        b = self.bufs.get(name)
        if b is None:
            b = self.bufs[name] = Buf()
        return b

    def _emit(self, eng, fn, deps, own_inc):
        seen = self.seen[eng]
        deps = merge(deps, self.pending[eng])
        self.pending[eng] = {}
        waits = []
        for k, v in deps.items():
            if seen.get(k, 0) >= v:
                continue
            seen[k] = v
            waits.append((self.sem[k], v))
        sems = self.sem
        self.ninst += 1 + len(waits)

        def thunk(e, waits=waits, fn=fn, own_inc=own_inc):
            for s_, v_ in waits:
                e.wait_ge(s_, v_)
            ins = fn(e)
            ins.then_inc(own_inc[0], own_inc[1])

        self.thunks[eng].append(thunk)

    def op(self, eng, method, *args, R=(), W=(), N=(), deps=None, **kw):
        dr = merge(*[self.buf(n).ws for n in R])
        for n in N:
            self.buf(n).new()
        d = merge(deps, dr, *[self.buf(n).prev for n in W])
        self.cnt[eng] += 1
        tok = {eng: self.cnt[eng]}

        def fn(e, method=method, args=args, kw=kw):
            return getattr(e, method)(*args, **kw)

        self._emit(eng, fn, d, (self.sem[eng], 1))
        for n in R:
            b = self.buf(n)
            b.rs = merge(b.rs, tok)
        for n in W:
            b = self.buf(n)
            b.ws = merge(b.ws, tok)
        return tok

    def dma(self, out, in_, R=(), W=(), N=(), deps=None, q="sp"):
        dr = merge(*[self.buf(n).ws for n in R])
        for n in N:
            self.buf(n).new()
        s = self.dnext
        self.dnext = (s + 1) % self.NDMA
        key = "d%d" % s
        d = merge(deps, dr, *[self.buf(n).prev for n in W],
                  {key: 16 * self.dcnt[s]} if self.dcnt[s] else None)
        self.dcnt[s] += 1
        tok = {key: 16 * self.dcnt[s]}

        def fn(e, out=out, in_=in_):
            return e.dma_start(out=out, in_=in_)

        self._emit(q, fn, d, (self.sem[key], 16))
        for n in R:
            b = self.buf(n)
            b.rs = merge(b.rs, tok)
        for n in W:
            b = self.buf(n)
            b.ws = merge(b.ws, tok)
        return tok

    def all_tokens(self):
        t = {k: self.cnt[k] for k in self.ENG if self.cnt[k]}
        for i in range(self.NDMA):
            if self.dcnt[i]:
                t["d%d" % i] = 16 * self.dcnt[i]
        return t

    def barrier(self):
        t = self.all_tokens()
        for k in self.ENG:
            self.pending[k] = merge(self.pending[k], t)
        self.bufs = {}


class Arena:
    def __init__(self, t, size):
        self.t = t
        self.size = size
        self.off = 0

    def alloc(self, shape, dt):
        n = int(np.prod(shape)) * DSZ[dt]
        off = (self.off + 63) // 64 * 64
        assert off + n <= self.size, ("SBUF arena overflow", off, n, self.size)
        self.off = off + n
        ap = self.t[:, off:off + n].bitcast(dt)
        if len(shape) == 1:
            return ap
        names = " ".join("a%d" % i for i in range(len(shape)))
        kw = {"a%d" % i: int(shape[i]) for i in range(len(shape))}
        return ap.rearrange("p (%s) -> p %s" % (names, names), **kw)

    def mark(self):
        return self.off

    def release(self, m):
        self.off = m


def build_model(NL=DEPTH, debug=False, phases="ABCD"):
    nc = bass.Bass("TRN2", target_bir_lowering=False)

    def din(name, shape, dt=F32):
        return nc.dram_tensor(name, list(shape), dt, kind="ExternalInput").ap()

    class C:
        pass
    x_in = din("x", [T, D])
    out_final = nc.dram_tensor("out", [T, D], F32, kind="ExternalOutput").ap()
    xbufs = [nc.dram_tensor("xbuf%d" % i, [T, D], F32).ap() for i in range(2)]
    LAYERED = dict(wf=[128, 8, 2816], wt=[128, 8, 392], wuk=[128, 8, 128], wuv=[128, 8, 64], kvg_bc=[128, 128],
                   kvg_col=[128, 1], ldt=[128, 16], are=[128, 16], aim=[128, 16], bre=[128, 16, 16], bim=[128, 16, 16],
                   cre=[128, 16, 16], cim=[128, 16, 16], dsk=[128, 32], wglu=[128, 4, 1024], bglu=[128, 8],
                   wout=[128, 8, 1024], lng=[128, 1024], lnb=[128, 1024])
    LAY = {k: din(k, [NL] + v) for k, v in LAYERED.items()}

    def set_layer(l):
        C.x_d = x_in if l == 0 else xbufs[(l - 1) % 2]
        C.out_d = out_final if l == NL - 1 else xbufs[l % 2]
        for k in LAYERED:
            setattr(C, k.replace("_", "") + "_d", LAY[k][l])
    ident_d = din("ident", [128, 128])
    caus_d = din("caus", [128, 128])
    tri_d = din("tri", [128, 128])
    dg_d = din("dg", [128, 8, 352])
    mv_d = din("mvals", [128, 24])
    kk_d = din("kk", [128, 512])

    def scr(name, shape, dt):
        if debug and name in debug:
            return nc.dram_tensor(name, list(shape), dt, kind="ExternalOutput").ap()
        return nc.dram_tensor(name, list(shape), dt).ap()

    qT_d = scr("qT_s", [4, 128, T], BF16)
    idx_d = scr("idx_s", [6, 128, T], BF16)
    ga_d = scr("ga_s", [4, 128, T], BF16)
    u_d = scr("u_s", [4, 128, T], BF16)
    gs_d = scr("gs_s", [4, 128, T], BF16)
    chat_d = scr("chat_s", [T, 128], BF16)
    chatT_d = scr("chatT_s", [128, T], BF16)
    wn_d = scr("wn_s", [T, 8], F32)
    mix_d = scr("mix_s", [8, 128, T], BF16)
    dbgC = bool(debug) and "ygT_s" in debug
    if dbgC:
        ygT_dbg = scr("ygT_s", [4, 128, S], BF16)
        W_dbg = scr("W_s", [128, 32, 128], BF16)
        Vre_dbg = scr("Vre_s", [128, 16, 128], BF16)
        Vim_dbg = scr("Vim_s", [128, 16, 128], BF16)
        Wcre_dbg = scr("Wcre_s", [128, 16, 128], BF16)
        Wcim_dbg = scr("Wcim_s", [128, 16, 128], BF16)
        RT_dbg = scr("RT_s", [128, 32], F32)

    es = ExitStack()
    with es:
        SBSZ = 200 * 1024
        arena_t = es.enter_context(nc.sbuf_tensor("arena", [128, SBSZ], U8))
        AR = Arena(arena_t, SBSZ)
        psum = es.enter_context(nc.psum_tensor("ps", [128, 8, 512], F32))
        Sd = Sched(nc, es)

        def PSB(b, w=512):
            return psum[:, b, 0:w]

        def PE(m, *a, **k):
            return Sd.op("pe", m, *a, **k)

        def ACT(m, *a, **k):
            return Sd.op("act", m, *a, **k)

        def DVE(m, *a, **k):
            return Sd.op("dve", m, *a, **k)

        def POOL(m, *a, **k):
            return Sd.op("pool", m, *a, **k)

        DMA = Sd.dma

        identf = AR.alloc([128], F32)
        identb = AR.alloc([128], BF16)
        onesb = AR.alloc([128], BF16)
        causf = AR.alloc([128], F32)
        trif = AR.alloc([128], F32)
        dgb = AR.alloc([8, 352], BF16)
        wukp = AR.alloc([8, 128], BF16)
        wuvp = AR.alloc([8, 64], BF16)
        kvgbc = AR.alloc([128], F32)
        kvgcol = AR.alloc([1], F32)
        W_sb = AR.alloc([32, 128], BF16)
        Wc_re = AR.alloc([16, 128], BF16)
        Wc_im = AR.alloc([16, 128], BF16)
        V_re = AR.alloc([16, 128], BF16)
        V_im = AR.alloc([16, 128], BF16)
        Rsc = AR.alloc([16], F32)
        Thr = AR.alloc([16], F32)
        kkf = AR.alloc([512], F32)
        bglu = AR.alloc([8], F32)
        pm = AR.mark()

        stgA = AR.alloc([8, 352], F32)
        DMA(identf, ident_d, W=["identf"], N=["identf"])
        DMA(causf, caus_d, W=["causf"], N=["causf"])
        DMA(trif, tri_d, W=["trif"], N=["trif"])
        DMA(kkf, kk_d, W=["kkf"], N=["kkf"])
        DMA(stgA, dg_d, W=["stgA"], N=["stgA"])
        POOL("tensor_copy", identb, identf, R=["identf"], W=["identb"], N=["identb"])
        POOL("memset", onesb, 1.0, W=["onesb"], N=["onesb"])
        POOL("tensor_copy", dgb, stgA, R=["stgA"], W=["dgb"], N=["dgb"])
        Sd.barrier()
        AR.release(pm)

        def layer_consts():
            m0 = AR.mark()
            stgB = AR.alloc([8, 64], F32)
            stgC = AR.alloc([8, 128], F32)
            DMA(kvgbc, C.kvgbc_d, W=["kvgbc"], N=["kvgbc"])
            DMA(kvgcol, C.kvgcol_d, W=["kvgcol"], N=["kvgcol"])
            DMA(bglu, C.bglu_d, W=["bglu"], N=["bglu"])
            DMA(stgB, C.wuv_d, W=["stgB"], N=["stgB"])
            DMA(stgC, C.wuk_d, W=["stgC"], N=["stgC"])
            DVE("tensor_scalar", wuvp, stgB, kvgcol[:, 0:1], None, ALU.mult, R=["stgB", "kvgcol"], W=["wuvp"], N=["wuvp"])
            DVE("scalar_tensor_tensor", wukp, stgC, 0.125, kvgbc.unsqueeze(1).to_broadcast([128, 8, 128]),
                ALU.mult, ALU.mult, R=["stgC", "kvgbc"], W=["wukp"], N=["wukp"])
            Sd.barrier()
            AR.release(m0)

        def sincos(ang, F, out_cos, out_sin, tmp_f, tmp_i, tmp_g, tag):
            bn = lambda s: tag + s
            DVE("tensor_scalar", tmp_f, ang, 1.0 / (2 * PI), None, ALU.mult, R=[bn("ang")], W=[bn("tf")], N=[bn("tf")])
            DVE("tensor_copy", tmp_i, tmp_f, R=[bn("tf")], W=[bn("ti")], N=[bn("ti")])
            DVE("tensor_copy", tmp_f, tmp_i, R=[bn("ti")], W=[bn("tf")], N=[bn("tf")])
            DVE("scalar_tensor_tensor", out_sin, tmp_f, -2 * PI, ang, ALU.mult, ALU.add,
                R=[bn("tf"), bn("ang")], W=[bn("r")], N=[bn("r")])
            for (thr, cmp, adj) in ((PI, ALU.is_gt, -2 * PI), (-PI, ALU.is_lt, 2 * PI)):
                DVE("tensor_scalar", tmp_g, out_sin, thr, adj, cmp, ALU.mult, R=[bn("r")], W=[bn("tg")], N=[bn("tg")])
                DVE("tensor_tensor", out_sin, out_sin, tmp_g, ALU.add, R=[bn("tg")], W=[bn("r")], N=[bn("r")])
            DVE("tensor_scalar", out_cos, out_sin, PI / 2, None, ALU.add, R=[bn("r")], W=[bn("rc")], N=[bn("rc")])
            DVE("tensor_scalar", tmp_g, out_cos, PI, -2 * PI, ALU.is_gt, ALU.mult, R=[bn("rc")], W=[bn("tg")], N=[bn("tg")])
            DVE("tensor_tensor", out_cos, out_cos, tmp_g, ALU.add, R=[bn("tg")], W=[bn("rc")], N=[bn("rc")])
            DVE("tensor_scalar", out_sin, out_sin, PI, -PI, ALU.min, ALU.max, R=[bn("r")], W=[bn("r")], N=[bn("r")])
            DVE("tensor_scalar", out_cos, out_cos, PI, -PI, ALU.min, ALU.max, R=[bn("rc")], W=[bn("rc")], N=[bn("rc")])
            ACT("activation", out_sin, out_sin, AF.Sin, R=[bn("r")], W=[bn("sin")], N=[bn("sin"), bn("r")])
            ACT("activation", out_cos, out_cos, AF.Sin, R=[bn("rc")], W=[bn("cos")], N=[bn("cos"), bn("rc")])

        def s5_setup():
            m0 = AR.mark()
            ldt = AR.alloc([16], F32)
            are = AR.alloc([16], F32)
            aim = AR.alloc([16], F32)
            bre = AR.alloc([16, 16], F32)
            bim = AR.alloc([16, 16], F32)
            cre = AR.alloc([16, 16], F32)
            cim = AR.alloc([16, 16], F32)
            dsk = AR.alloc([32], F32)
            mv = AR.alloc([24], F32)
            for ap, d_, n in ((ldt, C.ldt_d, "ldt"), (are, C.are_d, "are"), (aim, C.aim_d, "aim"), (bre, C.bre_d, "bre"),
                              (bim, C.bim_d, "bim"), (cre, C.cre_d, "cre"), (cim, C.cim_d, "cim"), (dsk, C.dsk_d, "dsk"),
                              (mv, mv_d, "mv")):
                DMA(ap, d_, W=[n], N=[n])
            dt = AR.alloc([16], F32)
            dre = AR.alloc([16], F32)
            dim = AR.alloc([16], F32)
            ACT("activation", dt, ldt, AF.Exp, R=["ldt"], W=["dt"], N=["dt"])
            DVE("tensor_tensor", dre, are, dt, ALU.mult, R=["are", "dt"], W=["dre"], N=["dre"])
            DVE("tensor_tensor", dim, aim, dt, ALU.mult, R=["aim", "dt"], W=["dim"], N=["dim"])
            ACT("activation", Rsc, dre, AF.Exp, scale=8.0, R=["dre"], W=["Rsc"], N=["Rsc"])
            th8 = AR.alloc([16], F32)
            tq = AR.alloc([16], F32)
            tqi = AR.alloc([16], I32)
            tg = AR.alloc([16], F32)
            DVE("tensor_scalar", th8, dim, 8.0, None, ALU.mult, R=["dim"], W=["th8"], N=["th8"])
            DVE("tensor_scalar", tq, th8, 1.0 / (2 * PI), None, ALU.mult, R=["th8"], W=["tq"], N=["tq"])
            DVE("tensor_copy", tqi, tq, R=["tq"], W=["tqi"], N=["tqi"])
            DVE("tensor_copy", tq, tqi, R=["tqi"], W=["tq"], N=["tq"])
            DVE("scalar_tensor_tensor", Thr, tq, -2 * PI, th8, ALU.mult, ALU.add, R=["tq", "th8"], W=["Thr"], N=["Thr"])
            for (thr, cmp, adj) in ((PI, ALU.is_gt, -2 * PI), (-PI, ALU.is_lt, 2 * PI)):
                DVE("tensor_scalar", tg, Thr, thr, adj, cmp, ALU.mult, R=["Thr"], W=["tg8"], N=["tg8"])
                DVE("tensor_tensor", Thr, Thr, tg, ALU.add, R=["tg8"], W=["Thr"], N=["Thr"])
            F = 16 * 24
            mre = AR.alloc([16, 24], F32)
            ang = AR.alloc([16, 24], F32)
            Ere = AR.alloc([16, 24], F32)
            Eim = AR.alloc([16, 24], F32)
            tf_ = AR.alloc([16, 24], F32)
            tg_ = AR.alloc([16, 24], F32)
            ti_ = AR.alloc([16, 24], I32)
            mvb = mv.unsqueeze(1).to_broadcast([128, 16, 24])
            DVE("tensor_tensor", mre, dre.unsqueeze(2).to_broadcast([128, 16, 24]), mvb, ALU.mult,
                R=["dre", "mv"], W=["mre"], N=["mre"])
            DVE("tensor_tensor", ang, dim.unsqueeze(2).to_broadcast([128, 16, 24]), mvb, ALU.mult,
                R=["dim", "mv"], W=["Eang"], N=["Eang"])
            ACT("activation", mre, mre, AF.Exp, R=["mre"], W=["mre"], N=["mre"])
            sincos(ang, F, Ere, Eim, tf_, ti_, tg_, "E")
            DVE("tensor_tensor", Ere, Ere, mre, ALU.mult, R=["Ecos", "mre"], W=["Ere"], N=["Ere", "Ecos"])
            DVE("tensor_tensor", Eim, Eim, mre, ALU.mult, R=["Esin", "mre"], W=["Eim"], N=["Eim", "Esin"])
            nr = AR.alloc([16], F32)
            den = AR.alloc([16], F32)
            t1 = AR.alloc([16], F32)
            t2 = AR.alloc([16], F32)
            fre = AR.alloc([16], F32)
            fim = AR.alloc([16], F32)
            e1r = Ere[:, :, 8]
            e1i = Eim[:, :, 8]
            DVE("tensor_scalar", nr, e1r, -1.0, None, ALU.add, R=["Ere"], W=["nr"], N=["nr"])
            DVE("tensor_tensor", t1, are, are, ALU.mult, R=["are"], W=["t1"], N=["t1"])
            DVE("tensor_tensor", t2, aim, aim, ALU.mult, R=["aim"], W=["t2"], N=["t2"])
            DVE("tensor_tensor", den, t1, t2, ALU.add, R=["t1", "t2"], W=["den"], N=["den"])
            DVE("reciprocal", den, den, R=["den"], W=["den"], N=["den"])
            DVE("tensor_tensor", t1, nr, are, ALU.mult, R=["nr", "are"], W=["t1"], N=["t1"])
            DVE("tensor_tensor", t2, e1i, aim, ALU.mult, R=["Eim", "aim"], W=["t2"], N=["t2"])
            DVE("tensor_tensor", fre, t1, t2, ALU.add, R=["t1", "t2"], W=["fre"], N=["fre"])
            DVE("tensor_tensor", fre, fre, den, ALU.mult, R=["den"], W=["fre"], N=["fre"])
            DVE("tensor_tensor", t1, e1i, are, ALU.mult, R=["Eim", "are"], W=["t1"], N=["t1"])
            DVE("tensor_tensor", t2, nr, aim, ALU.mult, R=["nr", "aim"], W=["t2"], N=["t2"])
            DVE("tensor_tensor", fim, t1, t2, ALU.subtract, R=["t1", "t2"], W=["fim"], N=["fim"])
            DVE("tensor_tensor", fim, fim, den, ALU.mult, R=["den"], W=["fim"], N=["fim"])
            Bre = AR.alloc([16, 16], F32)
            Bim = AR.alloc([16, 16], F32)
            u1 = AR.alloc([16, 16], F32)
            freb = fre.unsqueeze(2).to_broadcast([128, 16, 16])
            fimb = fim.unsqueeze(2).to_broadcast([128, 16, 16])
            DVE("tensor_tensor", Bre, bre, freb, ALU.mult, R=["bre", "fre"], W=["Bre"], N=["Bre"])
            DVE("tensor_tensor", u1, bim, fimb, ALU.mult, R=["bim", "fim"], W=["u1"], N=["u1"])
            DVE("tensor_tensor", Bre, Bre, u1, ALU.subtract, R=["u1"], W=["Bre"], N=["Bre"])
            DVE("tensor_tensor", Bim, bim, freb, ALU.mult, R=["bim", "fre"], W=["Bim"], N=["Bim"])
            DVE("tensor_tensor", u1, bre, fimb, ALU.mult, R=["bre", "fim"], W=["u1"], N=["u1"])
            DVE("tensor_tensor", Bim, Bim, u1, ALU.add, R=["u1"], W=["Bim"], N=["Bim"])

            big = [16, 8, 16]
            tA = AR.alloc(big, F32)
            tB = AR.alloc(big, F32)

            def cprod(o_re, o_im, sl, yre, yim, yn_re, yn_im, neg_im, tag):
                er = Ere[:, :, sl].unsqueeze(3).to_broadcast([128, 16, 8, 16])
                ei = Eim[:, :, sl].unsqueeze(3).to_broadcast([128, 16, 8, 16])
                yr = yre.unsqueeze(2).to_broadcast([128, 16, 8, 16])
                yi = yim.unsqueeze(2).to_broadcast([128, 16, 8, 16])
                DVE("tensor_tensor", tA, er, yr, ALU.mult, R=["Ere", yn_re], W=["tA"], N=["tA"])
                DVE("tensor_tensor", tB, ei, yi, ALU.mult, R=["Eim", yn_im], W=["tB"], N=["tB"])
                DVE("tensor_tensor", o_re, tA, tB, ALU.subtract, R=["tA", "tB"], W=[tag + "re"], N=[tag + "re"])
                DVE("tensor_tensor", tA, er, yi, ALU.mult, R=["Ere", yn_im], W=["tA"], N=["tA"])
                DVE("tensor_tensor", tB, ei, yr, ALU.mult, R=["Eim", yn_re], W=["tB"], N=["tB"])
                if neg_im:
                    DVE("scalar_tensor_tensor", o_im, tA, -1.0, tB, ALU.mult, ALU.subtract,
                        R=["tA", "tB"], W=[tag + "im"], N=[tag + "im"])
                else:
                    DVE("tensor_tensor", o_im, tA, tB, ALU.add, R=["tA", "tB"], W=[tag + "im"], N=[tag + "im"])

            Bq_re = AR.alloc(big, F32)
            Bq_im = AR.alloc(big, F32)
            Cq_re = AR.alloc(big, F32)
            Cq_im = AR.alloc(big, F32)
            cprod(Bq_re, Bq_im, slice(0, 8), Bre, Bim, "Bre", "Bim", False, "Bq")
            cprod(Cq_re, Cq_im, slice(8, 16), cre, cim, "cre", "cim", True, "Cq")
            POOL("tensor_copy", Wc_re, Cq_re.rearrange("p i a b -> p i (a b)"), R=["Cqre"], W=["Wc_re"], N=["Wc_re"])
            POOL("tensor_copy", Wc_im, Cq_im.rearrange("p i a b -> p i (a b)"), R=["Cqim"], W=["Wc_im"], N=["Wc_im"])
            Bqm_re = [AR.alloc([16, 128], BF16) for _ in range(2)]
            Bqm_im = [AR.alloc([16, 128], BF16) for _ in range(2)]
            Sd.buf("Bqreb").new()
            Sd.buf("Bqimb").new()
            for e_ in range(2):
                o_ = slice(64 * (1 - e_), 64 * (1 - e_) + 64)
                k_ = slice(64 * e_, 64 * e_ + 64)
                POOL("memset", Bqm_re[e_][o_], 0.0, W=["Bqreb"])
                POOL("memset", Bqm_im[e_][o_], 0.0, W=["Bqimb"])
                POOL("tensor_copy", Bqm_re[e_][k_], Bq_re[k_].rearrange("p i a b -> p i (a b)"), R=["Bqre"], W=["Bqreb"])
                POOL("tensor_copy", Bqm_im[e_][k_], Bq_im[k_].rearrange("p i a b -> p i (a b)"), R=["Bqim"], W=["Bqimb"])
            tW = AR.alloc([128], F32)
            for g in range(32):
                i, e = g // 2, g % 2
                pb = 4 + (g % 2)
                sl = slice(64 * e, 64 * e + 64)
                PE("matmul", PSB(pb, 128), Bqm_re[e][:, i, :], Wc_re[:, i, :], start=True, stop=False,
                   R=["Bqreb", "Wc_re"], W=["psW%d" % pb], N=["psW%d" % pb])
                PE("matmul", PSB(pb, 128), Bqm_im[e][:, i, :], Wc_im[:, i, :], start=False, stop=True,
                   R=["Bqimb", "Wc_im"], W=["psW%d" % pb])
                DVE("tensor_tensor", tW, PSB(pb, 128), trif, ALU.mult, R=["psW%d" % pb, "trif"], W=["tW"], N=["tW"])
                DVE("scalar_tensor_tensor", W_sb[:, g, :], identf, dsk[:, g:g + 1], tW, ALU.mult, ALU.add,
                    R=["tW", "identf", "dsk"], W=["W_sb"])
            cprod(Bq_re, Bq_im, slice(16, 24), Bre, Bim, "Bre", "Bim", False, "Bq")
            for i in range(16):
                for (src, dst, sn, dn) in ((Bq_re, V_re, "Bqre", "V_re"), (Bq_im, V_im, "Bqim", "V_im")):
                    pb = 6 + (i % 2)
                    PE("transpose", PSB(pb, 128), src[:, i, :, :].rearrange("p a b -> p (a b)"), identf,
                       R=[sn, "identf"], W=["psV%d" % pb], N=["psV%d" % pb])
                    ACT("activation", dst[:, i, :], PSB(pb, 128), AF.Copy, R=["psV%d" % pb], W=[dn])
            if dbgC:
                DMA(W_dbg, W_sb, R=["W_sb"])
                DMA(Vre_dbg, V_re, R=["V_re"])
                DMA(Vim_dbg, V_im, R=["V_im"])
                DMA(Wcre_dbg, Wc_re, R=["Wc_re"])
                DMA(Wcim_dbg, Wc_im, R=["Wc_im"])
                DMA(RT_dbg[:, 0:16], Rsc, R=["Rsc"])
                DMA(RT_dbg[:, 16:32], Thr, R=["Thr"])
            Sd.barrier()
            AR.release(m0)

        def phase_A(seq):
            m0 = AR.mark()
            wf = AR.alloc([8, 2816], BF16)
            wt = AR.alloc([8, 392], BF16)
            stg = [AR.alloc([2816], F32) for _ in range(2)]
            for k in range(8):
                b = k % 2
                DMA(stg[b], C.wf_d[:, k, :], W=["stg%d" % b], N=["stg%d" % b])
                POOL("tensor_copy", wf[:, k, :], stg[b], R=["stg%d" % b], W=["wf"])
            wts = AR.alloc([8, 392], F32)
            DMA(wts, C.wt_d, W=["wts"], N=["wts"])
            POOL("tensor_copy", wt, wts, R=["wts"], W=["wt"], N=["wt"])
            xtok = [AR.alloc([1024], F32) for _ in range(2)]
            xbf = [AR.alloc([1024], BF16) for _ in range(2)]
            xT = [AR.alloc([8, 512], BF16) for _ in range(2)]
            fm = AR.alloc([22, 512], BF16)
            sqj = AR.alloc([128], BF16)
            ss = AR.alloc([1], F32)
            vv = AR.alloc([1], F32)
            sv = AR.alloc([1], F32)
            rstd = AR.alloc([1], F32)
            chat_c = AR.alloc([4, 128], BF16)
            chatT_c = AR.alloc([512], BF16)
            wq = AR.alloc([264], F32)
            sq = AR.alloc([256], F32)
            qn = AR.alloc([8], F32)
            aw = AR.alloc([8], F32)
            s1 = AR.alloc([1], F32)
            wn_c = AR.alloc([4, 8], F32)
            for ch in range(8):
                t0c = seq * S + ch * 512
                xb = xT[ch % 2]
                xn = "xT%d" % (ch % 2)
                Sd.buf(xn).new()
                Sd.buf("chat_c").new()
                Sd.buf("chatT_c").new()
                Sd.buf("wn_c").new()
                for tt in range(4):
                    t0 = t0c + tt * 128
                    j = (ch * 4 + tt) % 2
                    DMA(xtok[j], C.x_d[t0:t0 + 128, :], W=["xtok%d" % j], N=["xtok%d" % j])
                    POOL("tensor_copy", xbf[j], xtok[j], R=["xtok%d" % j], W=["xbf%d" % j], N=["xbf%d" % j])
                    tp = PSB(6 + j).bitcast(BF16).rearrange("p (a b) -> p a b", a=8)
                    Sd.buf("pstp%d" % j).new()
                    for k in range(8):
                        PE("transpose", tp[:, k, :], xbf[j][:, k * 128:(k + 1) * 128], identb,
                           R=["xbf%d" % j, "identb"], W=["pstp%d" % j])
                    ACT("activation", xb[:, :, tt * 128:(tt + 1) * 128], tp, AF.Copy, R=["pstp%d" % j], W=[xn])
                    tokps = PSB(5, 392)
                    Sd.buf("tokps").new()
                    for k in range(8):
                        PE("matmul", tokps, xb[:, k, tt * 128:(tt + 1) * 128], wt[:, k, :], start=(k == 0), stop=(k == 7),
                           R=[xn, "wt"], W=["tokps"])
                    ACT("activation", sqj, tokps[:, 0:128], AF.Square, accum_out=ss[:, 0:1], R=["tokps"], W=["ss", "sqj"], N=["ss", "sqj"])
                    DVE("tensor_scalar", vv, ss, 1.0 / 128, RMS_EPS, ALU.mult, ALU.add, R=["ss"], W=["vv"], N=["vv"])
                    ACT("activation", sv, vv, AF.Sqrt, R=["vv"], W=["sv"], N=["sv"])
                    DVE("reciprocal", rstd, sv, R=["sv"], W=["rstd"], N=["rstd"])
                    ACT("activation", chat_c[:, tt, :], tokps[:, 0:128], AF.Copy, scale=rstd[:, 0:1],
                        R=["tokps", "rstd"], W=["chat_c"])
                    tp2 = PSB(4).bitcast(BF16)[:, 0:128]
                    PE("transpose", tp2, chat_c[:, tt, :], identb, R=["chat_c", "identb"], W=["pstp2"], N=["pstp2"])
                    DVE("tensor_copy", chatT_c[:, tt * 128:(tt + 1) * 128], tp2, R=["pstp2"], W=["chatT_c"])
                    ACT("activation", wq, tokps[:, 128:392], AF.Copy, R=["tokps"], W=["wq"], N=["wq"])
                    DVE("tensor_tensor", sq, wq[:, 8:264], wq[:, 8:264], ALU.mult, R=["wq"], W=["sq"], N=["sq"])
                    DVE("tensor_reduce", qn, sq.rearrange("p (h d) -> p h d", h=8), AX.X, ALU.add, R=["sq"], W=["qn"], N=["qn"])
                    ACT("activation", qn, qn, AF.Sqrt, R=["qn"], W=["qn"], N=["qn"])
                    DVE("scalar_tensor_tensor", aw, wq[:, 0:8], -1.0, wq[:, 0:8], ALU.mult, ALU.max, R=["wq"], W=["aw"], N=["aw"])
                    DVE("tensor_tensor", aw, aw, qn, ALU.mult, R=["qn"], W=["aw"], N=["aw"])
                    DVE("tensor_reduce", s1, aw, AX.X, ALU.add, R=["aw"], W=["s1"], N=["s1"])
                    DVE("tensor_scalar", s1, s1, 1e-30, None, ALU.add, R=["s1"], W=["s1"], N=["s1"])
                    DVE("reciprocal", s1, s1, R=["s1"], W=["s1"], N=["s1"])
                    DVE("tensor_scalar", wn_c[:, tt, :], wq[:, 0:8], s1[:, 0:1], None, ALU.mult, R=["wq", "s1"], W=["wn_c"])
                DMA(chat_d[t0c:t0c + 512, :].rearrange("(a p) c -> p a c", p=128), chat_c, R=["chat_c"])
                DMA(chatT_d[:, t0c:t0c + 512], chatT_c, R=["chatT_c"])
                DMA(wn_d[t0c:t0c + 512, :].rearrange("(a p) h -> p a h", p=128), wn_c, R=["wn_c"])
                Sd.buf("fm").new()
                for cb in range(22):
                    pb = cb % 4
                    pn = "psfm%d" % pb
                    Sd.buf(pn).new()
                    for k in range(8):
                        PE("matmul", PSB(pb), wf[:, k, cb * 128:(cb + 1) * 128], xb[:, k, :], start=(k == 0), stop=(k == 7),
                           R=["wf", xn], W=[pn])
                    silu = (10 <= cb < 14) or cb >= 18
                    if silu:
                        ACT("activation", fm[:, cb, :], PSB(pb), AF.Silu, R=[pn], W=["fm"])
                    elif cb % 2 == 0:
                        ACT("activation", fm[:, cb, :], PSB(pb), AF.Copy, R=[pn], W=["fm"])
                    else:
                        DVE("tensor_copy", fm[:, cb, :], PSB(pb), R=[pn], W=["fm"])
                for (dd, c0_, c1_) in ((qT_d, 0, 4), (idx_d, 4, 10), (ga_d, 10, 14), (u_d, 14, 18), (gs_d, 18, 22)):
                    DMA(dd[:, :, t0c:t0c + 512].rearrange("j p t -> p j t"), fm[:, c0_:c1_, :], R=["fm"])
            Sd.barrier()
            AR.release(m0)

        def phase_B(seq):
            m0 = AR.mark()
            s0 = seq * S
            NQ = 32
            kiT = AR.alloc([3, S], BF16)
            chatT = AR.alloc([S], BF16)
            chtok = AR.alloc([32, 128], BF16)
            causb = AR.alloc([128], BF16)
            POOL("tensor_copy", causb, causf, R=["causf"], W=["causb"], N=["causb"])
            DMA(kiT, idx_d[3:6, :, s0:s0 + S].rearrange("j p t -> p j t"), W=["kiT"], N=["kiT"])
            DMA(chatT, chatT_d[:, s0:s0 + S], W=["chatT"], N=["chatT"])
            DMA(chtok, chat_d[s0:s0 + S, :].rearrange("(a p) c -> p a c", p=128), W=["chtok"], N=["chtok"])
            I_sb = [AR.alloc([S], F16) for _ in range(3)]
            mask = [AR.alloc([S], BF16) for _ in range(2)]
            maskT = [AR.alloc([32, 128], BF16) for _ in range(2)]
            Rsb = [AR.alloc([512], BF16) for _ in range(3)]
            diagw = [AR.alloc([8, 128], BF16) for _ in range(3)]
            qlat = [AR.alloc([8, 128], BF16) for _ in range(3)]
            sqs = [AR.alloc([8, 128], BF16) for _ in range(3)]
            esb = [AR.alloc([512], BF16) for _ in range(3)]
            psb_ = [AR.alloc([512], BF16) for _ in range(3)]
            rl = [AR.alloc([512], F32) for _ in range(2)]
            on = [AR.alloc([512], BF16) for _ in range(2)]
            qTb = [AR.alloc([4, 128], BF16) for _ in range(4)]
            idxq = [AR.alloc([3, 128], BF16) for _ in range(4)]
            wnb = [AR.alloc([8], F32) for _ in range(4)]
            gab = [AR.alloc([4, 128], BF16) for _ in range(4)]
            mixo = AR.alloc([4, 128], BF16)
            o_sb = [AR.alloc([512], F32) for _ in range(2)]
            att_sb = AR.alloc([4, 128], F32)
            pmx = [AR.alloc([1], BF16) for _ in range(3)]
            nbp = AR.alloc([1], F32)
            negB = [AR.alloc([1], F32) for _ in range(3)]
            mid = [AR.alloc([1], F32) for _ in range(2)]
            cnt = AR.alloc([1], F32)
            mm_ = AR.alloc([1], F32)
            OB = (0, 6)
            LB = (1, 7)
            ridx = [0]

            def X(n, gen=None, RX=0):
                t0 = s0 + n * 128
                N = (n + 1) * 128
                a3, a2 = n % 4, n % 3
                DMA(qTb[a3], qT_d[:, :, t0:t0 + 128].rearrange("j p t -> p j t"), W=["qTb%d" % a3], N=["qTb%d" % a3])
                DMA(idxq[a3], idx_d[0:3, :, t0:t0 + 128].rearrange("j p t -> p j t"), W=["idxq%d" % a3], N=["idxq%d" % a3])
                DMA(wnb[a3], wn_d[t0:t0 + 128, :], W=["wnb%d" % a3], N=["wnb%d" % a3])
                DMA(gab[a3], ga_d[:, :, t0:t0 + 128].rearrange("j p t -> p j t"), W=["gab%d" % a3], N=["gab%d" % a3])
                qlps = psum[:, 2:4, :].rearrange("p b (h t) -> p (b h) t", h=4)
                Sd.buf("psL2").new()
                Sd.buf("psL3").new()
                for h in range(8):
                    PE("matmul", qlps[:, h, :], wukp[:, h, :], qTb[a3][:, h // 2, :], start=True, stop=True,
                       R=["qTb%d" % a3, "wukp"], W=["psL%d" % (2 + h // 4)])
                ACT("activation", qlat[a2], qlps, AF.Copy, R=["psL2", "psL3"], W=["qlat%d" % a2], N=["qlat%d" % a2])
                ACT("activation", sqs[a2], qlps, AF.Square, R=["psL2", "psL3"], W=["sqs%d" % a2], N=["sqs%d" % a2])
                POOL("tensor_tensor", diagw[a2], identf.unsqueeze(1).to_broadcast([128, 8, 128]),
                     wnb[a3].unsqueeze(2).to_broadcast([128, 8, 128]), ALU.mult, R=["identf", "wnb%d" % a3],
                     W=["diagw%d" % a2], N=["diagw%d" % a2])
                nkc = (N + 511) // 512
                xstep = [0]
                xem = [0]
                In = "I_sb%d" % a2
                Sd.buf(In).new()
                for kc in range(nkc):
                    w = min(512, N - kc * 512)
                    k0 = kc * 512
                    last = (kc == nkc - 1)
                    Sd.buf("psI").new()
                    pend = None
                    for h in range(8):
                        tl, r = h // 3, h % 3
                        pb = 2 + (h % 2)
                        pn = "psL%d" % pb
                        PE("matmul", PSB(pb, w), idxq[a3][:, tl, :], kiT[:, r, k0:k0 + w], start=True, stop=True,
                           R=["idxq%d" % a3, "kiT"], W=[pn], N=[pn])
                        if pend is not None:
                            ph, prb, prn = pend
                            PE("matmul", PSB(4, w), diagw[a2][:, ph, :], Rsb[prb][:, 0:w], start=(ph == 0), stop=False,
                               R=[prn, "diagw%d" % a2], W=["psI"])
                        rb = ridx[0] % 3
                        ridx[0] += 1
                        rn = "Rsb%d" % rb
                        ACT("activation", Rsb[rb][:, 0:w], PSB(pb, w), AF.Relu, R=[pn], W=[rn], N=[rn])
                        pend = (h, rb, rn)
                        if gen is not None:
                            xstep[0] += 1
                            want = (RX * xstep[0]) // (8 * nkc)
                            while xem[0] < want:
                                try:
                                    next(gen)
                                except StopIteration:
                                    break
                                xem[0] += 1
                    ph, prb, prn = pend
                    PE("matmul", PSB(4, w), diagw[a2][:, ph, :], Rsb[prb][:, 0:w], start=False, stop=(not last),
                       R=[prn, "diagw%d" % a2], W=["psI"])
                    if last:
                        PE("matmul", psum[:, 4, w - 128:w], identb, causb, start=False, stop=True,
                           R=["identb", "causb"], W=["psI"])
                    ACT("activation", I_sb[a2][:, k0:k0 + w], PSB(4, w), AF.Copy, R=["psI"], W=[In])

            def F_dve(n):
                a2 = n % 3
                DVE("tensor_reduce", pmx[a2], sqs[a2].rearrange("p h t -> p (h t)"), AX.X, ALU.max, R=["sqs%d" % a2], W=["pmx%d" % a2], N=["pmx%d" % a2])

            def F_rest(n):
                a2 = n % 3
                PE("matmul", psum[:, 1, 0:1], onesb, pmx[a2], start=True, stop=True, R=["pmx%d" % a2, "onesb"], W=["psl0"], N=["psl0"])
                ACT("activation", nbp, psum[:, 1, 0:1], AF.Ln, scale=128.0 * 1.03, R=["psl0"], W=["nbp"], N=["nbp"])
                ACT("activation", nbp, nbp, AF.Exp, scale=0.5, R=["nbp"], W=["nbp"], N=["nbp"])
                POOL("tensor_scalar", negB[a2], nbp, -1.0, None, ALU.mult, R=["nbp"], W=["negB%d" % a2], N=["negB%d" % a2])

            def Y_dve_gen(n):
                N = (n + 1) * 128
                a2 = n % 3
                In = "I_sb%d" % a2
                hs = [12.0] + [4.0 * (0.5 ** i) for i in range(13)]
                DVE("memset", mid[0], -RB + hs[0], W=["mid0"], N=["mid0"])
                for it in range(len(hs)):
                    a, b = it % 2, (it + 1) % 2
                    h_k = hs[it]
                    h_n = hs[it + 1] if it + 1 < len(hs) else 0.0
                    DVE("tensor_scalar", mask[n % 2][:, 0:N], I_sb[a2][:, 0:N], mid[a][:, 0:1], 0.0, ALU.is_ge, ALU.add,
                        accum_out=cnt[:, 0:1], R=[In, "mid%d" % a], W=["mask%d" % (n % 2), "cnt"], N=["mask%d" % (n % 2), "cnt"])
                    DVE("tensor_scalar", mm_, cnt, 255.5, h_k, ALU.is_ge, ALU.mult, R=["cnt"], W=["mm"], N=["mm"])
                    DVE("scalar_tensor_tensor", mid[b], mm_, h_n - h_k, mid[a], ALU.add, ALU.add,
                        R=["mm", "mid%d" % a], W=["mid%d" % b], N=["mid%d" % b])
                    if it + 1 < len(hs):
                        yield
                fin = len(hs) % 2
                DVE("tensor_scalar", mask[n % 2][:, 0:N], I_sb[a2][:, 0:N], mid[fin][:, 0:1], None, ALU.is_ge,
                    R=[In, "mid%d" % fin], W=["mask%d" % (n % 2)], N=["mask%d" % (n % 2)])

            def Y_dve(n):
                for _ in Y_dve_gen(n):
                    pass

            def Y_pe(n):
                a2 = n % 2
                Mn = "maskT%d" % a2
                Sd.buf(Mn).new()
                for g4 in range((n + 4) // 4):
                    nb = min(4, n + 1 - 4 * g4)
                    tpm = PSB(5).bitcast(BF16).rearrange("p (a b) -> p a b", a=8)
                    Sd.buf("ps5").new()
                    for u_ in range(nb):
                        kb = 4 * g4 + u_
                        PE("transpose", tpm[:, u_, :], mask[n % 2][:, kb * 128:(kb + 1) * 128], identb,
                           R=["mask%d" % (n % 2), "identb"], W=["ps5"])
                    ACT("activation", maskT[a2][:, 4 * g4:4 * g4 + nb, :], tpm[:, 0:nb, :], AF.Copy, R=["ps5"], W=[Mn])

            def Z(n, gen=None, RZ=14):
                a2 = n % 3
                am = n % 2
                Mn = "maskT%d" % am
                SB = (2, 3, 4)
                SN = ("psL2", "psL3", "psI")
                nsteps = 2 * (n + 1)
                step = 0
                emitted = 0
                for hh in range(2):
                    rhs_q = qlat[a2][:, 4 * hh:4 * hh + 4, :].rearrange("p h t -> p (h t)")
                    on_, ln_ = "pso%d" % hh, "psl%d" % hh
                    Sd.buf(on_).new()
                    Sd.buf(ln_).new()

                    def score(kb_):
                        i_ = kb_ % 3
                        PE("matmul", PSB(SB[i_]), chatT[:, kb_ * 128:(kb_ + 1) * 128], rhs_q, start=True, stop=True,
                           R=["chatT", "qlat%d" % a2], W=[SN[i_]], N=[SN[i_]])
                    score(0)
                    if n >= 1:
                        score(1)
                    for kb in range(n + 1):
                        i3 = kb % 3
                        if kb + 2 <= n:
                            score(kb + 2)
                        ACT("activation", esb[i3], PSB(SB[i3]), AF.Exp, bias=negB[a2][:, 0:1], R=[SN[i3], "negB%d" % a2],
                            W=["esb%d" % i3], N=["esb%d" % i3])
                        eng = POOL
                        eng("tensor_tensor", psb_[i3].rearrange("p (h t) -> p h t", h=4),
                            esb[i3].rearrange("p (h t) -> p h t", h=4),
                            maskT[am][:, kb, :].unsqueeze(1).to_broadcast([128, 4, 128]), ALU.mult,
                            R=["esb%d" % i3, Mn], W=["psb%d" % i3], N=["psb%d" % i3])
                        PE("matmul", PSB(OB[hh]), chtok[:, kb, :], psb_[i3], start=(kb == 0), stop=(kb == n),
                           R=["chtok", "psb%d" % i3], W=[on_])
                        PE("matmul", PSB(LB[hh]), onesb, psb_[i3], start=(kb == 0), stop=(kb == n),
                           R=["onesb", "psb%d" % i3], W=[ln_])
                        step += 1
                        if gen is not None:
                            want = (RZ * step) // nsteps
                            while emitted < want:
                                try:
                                    next(gen)
                                except StopIteration:
                                    gen = None
                                    break
                                emitted += 1

            def E(n):
                t0 = s0 + n * 128
                a3 = n % 4
                for hh in range(2):
                    ACT("activation", rl[hh], PSB(LB[hh]), AF.Ln, R=["psl%d" % hh], W=["rl%d" % hh], N=["rl%d" % hh])
                    ACT("activation", rl[hh], rl[hh], AF.Exp, scale=-1.0, R=["rl%d" % hh], W=["rl%d" % hh], N=["rl%d" % hh])
                    ACT("activation", o_sb[hh], PSB(OB[hh]), AF.Copy, R=["pso%d" % hh], W=["o_sb%d" % hh], N=["o_sb%d" % hh])
                    POOL("tensor_tensor", on[hh], o_sb[hh], rl[hh], ALU.mult, R=["o_sb%d" % hh, "rl%d" % hh],
                         W=["on%d" % hh], N=["on%d" % hh])
                attps = psum[:, 5, :].rearrange("p (j t) -> p j t", j=4)
                Sd.buf("ps5").new()
                for h in range(8):
                    hh, hl = h // 4, h % 4
                    j, e = h // 2, h % 2
                    PE("matmul", attps[64 * e:64 * e + 64, j, :], wuvp[:, h, :], on[hh][:, hl * 128:(hl + 1) * 128],
                       start=True, stop=True, R=["on%d" % hh, "wuvp"], W=["ps5"])
                ACT("activation", att_sb, attps, AF.Copy, R=["ps5"], W=["att_sb"], N=["att_sb"])
                POOL("tensor_tensor", mixo, att_sb, gab[a3], ALU.mult, R=["att_sb", "gab%d" % a3], W=["mixo"], N=["mixo"])
                DMA(mix_d[0:4, :, t0:t0 + 128].rearrange("j p t -> p j t"), mixo, R=["mixo"])

            X(0)
            X(1)
            X(2)
            for k_ in range(2):
                F_dve(k_)
                F_rest(k_)
            Y_dve(0)
            Y_pe(0)
            for n in range(NQ):
                g_ = Y_dve_gen(n + 1) if n + 1 < NQ else None
                zt = 1.8 * (n + 1)
                xt = 1.2 * (n + 4) if n + 3 < NQ else 0.0
                RZ = int(round(14 * zt / (zt + xt)))
                Z(n, g_, RZ)
                if n + 3 < NQ:
                    X(n + 3, g_, 14 - RZ)
                if g_ is not None:
                    for _ in g_:
                        pass
                if n + 2 < NQ:
                    F_dve(n + 2)
                if n + 1 < NQ:
                    Y_pe(n + 1)
                E(n)
                if n + 2 < NQ:
                    F_rest(n + 2)
            Sd.barrier()
            AR.release(m0)

        def phase_C(seq):
            m0 = AR.mark()
            s0 = seq * S
            wglu = AR.alloc([4, 1024], BF16)
            stg = AR.alloc([4, 1024], F32)
            DMA(stg, C.wglu_d, W=["stgg"], N=["stgg"])
            POOL("tensor_copy", wglu, stg, R=["stgg"], W=["wglu"], N=["wglu"])
            uT = [AR.alloc([S], BF16) for _ in range(2)]
            Ub = [AR.alloc([512], BF16) for _ in range(2)]
            angt = AR.alloc([512], F32)
            C1 = AR.alloc([512], F32)
            S1 = AR.alloc([512], F32)
            tf_ = AR.alloc([512], F32)
            tg_ = AR.alloc([512], F32)
            ti_ = AR.alloc([512], I32)
            ta = AR.alloc([512], F32)
            tb = AR.alloc([512], F32)
            Stre = AR.alloc([512], F32)
            Stim = AR.alloc([512], F32)
            Zre = AR.alloc([512], F32)
            Zim = AR.alloc([512], F32)
            Xre = [AR.alloc([512], BF16) for _ in range(2)]
            Xim = [AR.alloc([512], BF16) for _ in range(2)]
            Yg = AR.alloc([8, 512], BF16)
            ygT = AR.alloc([4, S], BF16)
            sig = AR.alloc([512], F32)
            ssm = AR.alloc([512], F32)
            gsb = AR.alloc([512], BF16)
            mxo = AR.alloc([512], BF16)
            for e in range(2):
                POOL("memset", Xre[e], 0.0, W=["Xre%d" % e], N=["Xre%d" % e])
                POOL("memset", Xim[e], 0.0, W=["Xim%d" % e], N=["Xim%d" % e])
            for q in range(4):
                ub = uT[q % 2]
                un = "uT%d" % (q % 2)
                DMA(ub, u_d[q][:, s0:s0 + S], W=[un], N=[un])
                Sd.buf("Yg").new()
                for ip in range(4):
                    i = 4 * q + ip
                    for e in range(2):
                        gl = 2 * ip + e
                        pn = "psU%d" % e
                        Sd.buf(pn).new()
                        for j in range(8):
                            x0 = 112 + 16 * (gl - j)
                            PE("matmul", PSB(e), dgb[:, gl, x0:x0 + 128], ub[:, j:S:8], start=(j == 0), stop=(j == 7),
                               R=[un, "dgb"], W=[pn])
                        ACT("activation", Ub[e], PSB(e), AF.Copy, R=[pn], W=["Ub%d" % e], N=["Ub%d" % e])
                    Sd.buf("psSre").new()
                    Sd.buf("psSim").new()
                    for e in range(2):
                        sl = slice(64 * e, 64 * e + 64)
                        PE("matmul", psum[sl, 2, :], V_re[:, i, sl], Ub[e], start=True, stop=True,
                           R=["V_re", "Ub%d" % e], W=["psSre"])
                        PE("matmul", psum[sl, 3, :], V_im[:, i, sl], Ub[e], start=True, stop=True,
                           R=["V_im", "Ub%d" % e], W=["psSim"])
                    DVE("tensor_scalar", angt, kkf, Thr[:, i:i + 1], None, ALU.mult, R=["kkf", "Thr"], W=["Tang"], N=["Tang"])
                    sincos(angt, 512, C1, S1, tf_, ti_, tg_, "T")
                    DVE("tensor_tensor", ta, PSB(2), C1, ALU.mult, R=["psSre", "Tcos"], W=["ta"], N=["ta"])
                    DVE("tensor_tensor", tb, PSB(3), S1, ALU.mult, R=["psSim", "Tsin"], W=["tb"], N=["tb"])
                    DVE("tensor_tensor", Stre, ta, tb, ALU.add, R=["ta", "tb"], W=["Stre"], N=["Stre"])
                    DVE("tensor_tensor", ta, PSB(3), C1, ALU.mult, R=["psSim", "Tcos"], W=["ta"], N=["ta"])
                    DVE("tensor_tensor", tb, PSB(2), S1, ALU.mult, R=["psSre", "Tsin"], W=["tb"], N=["tb"])
                    DVE("tensor_tensor", Stim, ta, tb, ALU.subtract, R=["ta", "tb"], W=["Stim"], N=["Stim"])
                    Rbc = Rsc[:, i:i + 1].to_broadcast([128, 512])
                    DVE("tensor_tensor_scan", Zre, Rbc, Stre, 0.0, ALU.mult, ALU.add, R=["Stre", "Rsc"], W=["Zre"], N=["Zre"])
                    DVE("tensor_tensor_scan", Zim, Rbc, Stim, 0.0, ALU.mult, ALU.add, R=["Stim", "Rsc"], W=["Zim"], N=["Zim"])
                    DVE("tensor_tensor", ta, Zre, C1, ALU.mult, R=["Zre", "Tcos"], W=["ta"], N=["ta"])
                    DVE("tensor_tensor", tb, Zim, S1, ALU.mult, R=["Zim", "Tsin"], W=["tb"], N=["tb"])
                    for e in range(2):
                        sl = slice(64 * e, 64 * e + 64)
                        DVE("tensor_tensor", Xre[e][sl, 1:512], ta[sl, 0:511], tb[sl, 0:511], ALU.subtract,
                            R=["ta", "tb"], W=["Xre%d" % e], N=["Xre%d" % e])
                    DVE("tensor_tensor", ta, Zim, C1, ALU.mult, R=["Zim", "Tcos"], W=["ta"], N=["ta"])
                    DVE("tensor_tensor", tb, Zre, S1, ALU.mult, R=["Zre", "Tsin"], W=["tb"], N=["tb"])
                    for e in range(2):
                        sl = slice(64 * e, 64 * e + 64)
                        DVE("tensor_tensor", Xim[e][sl, 1:512], ta[sl, 0:511], tb[sl, 0:511], ALU.add,
                            R=["ta", "tb"], W=["Xim%d" % e], N=["Xim%d" % e])
                    for e in range(2):
                        g = 2 * i + e
                        gl = 2 * ip + e
                        pb = 4 + e
                        pn = "psY%d" % e
                        PE("matmul", PSB(pb), W_sb[:, g, :], Ub[e], start=True, stop=False,
                           R=["W_sb", "Ub%d" % e], W=[pn], N=[pn])
                        PE("matmul", PSB(pb), Wc_re[:, i, :], Xre[e], start=False, stop=False,
                           R=["Wc_re", "Xre%d" % e], W=[pn])
                        PE("matmul", PSB(pb), Wc_im[:, i, :], Xim[e], start=False, stop=True,
                           R=["Wc_im", "Xim%d" % e], W=[pn])
                        ACT("activation", Yg[:, gl, :], PSB(pb), AF.Gelu_apprx_tanh, R=[pn], W=["Yg"])
                Sd.buf("ygT").new() if q == 0 else None
                ygv = ygT[:, q, :].rearrange("p (k t) -> p k t", t=8)
                for tau in range(8):
                    pb = 6 + (tau % 2)
                    pn = "psC%d" % pb
                    Sd.buf(pn).new()
                    for gl in range(8):
                        x0 = 112 + 16 * (tau - gl)
                        PE("matmul", PSB(pb), dgb[:, tau, x0:x0 + 128], Yg[:, gl, :], start=(gl == 0), stop=(gl == 7),
                           R=["Yg", "dgb"], W=[pn])
                    if tau % 2 == 0:
                        DVE("tensor_copy", ygv[:, :, tau], PSB(pb), R=[pn], W=["ygT"])
                    else:
                        ACT("activation", ygv[:, :, tau], PSB(pb), AF.Copy, R=[pn], W=["ygT"])
            if dbgC and seq == 0:
                DMA(ygT_dbg.rearrange("q p t -> p q t"), ygT, R=["ygT"])
            for tc in range(8):
                c0 = tc * 512
                for v in range(4):
                    Sd.buf("psU0").new()
                    Sd.buf("psU1").new()
                    for q in range(4):
                        PE("matmul", PSB(0), wglu[:, q, v * 128:(v + 1) * 128], ygT[:, q, c0:c0 + 512],
                           start=(q == 0), stop=(q == 3), R=["wglu", "ygT"], W=["psU0"])
                    for q in range(4):
                        PE("matmul", PSB(1), wglu[:, q, 512 + v * 128:512 + (v + 1) * 128], ygT[:, q, c0:c0 + 512],
                           start=(q == 0), stop=(q == 3), R=["wglu", "ygT"], W=["psU1"])
                    DMA(gsb, gs_d[v][:, s0 + c0:s0 + c0 + 512], W=["gsb"], N=["gsb"])
                    ACT("activation", sig, PSB(1), AF.Sigmoid, bias=bglu[:, 4 + v:5 + v], R=["psU1", "bglu"], W=["sig"], N=["sig"])
                    DVE("scalar_tensor_tensor", ssm, PSB(0), bglu[:, v:v + 1], sig, ALU.add, ALU.mult,
                        R=["psU0", "sig", "bglu"], W=["ssm"], N=["ssm"])
                    DVE("tensor_tensor", mxo, ssm, gsb, ALU.mult, R=["ssm", "gsb"], W=["mxo"], N=["mxo"])
                    DMA(mix_d[4 + v][:, s0 + c0:s0 + c0 + 512], mxo, R=["mxo"])
            Sd.barrier()
            AR.release(m0)

        def phase_D(seq):
            m0 = AR.mark()
            s0 = seq * S
            wout = AR.alloc([8, 1024], BF16)
            stg = [AR.alloc([1024], F32) for _ in range(2)]
            for k in range(8):
                b = k % 2
                DMA(stg[b], C.wout_d[:, k, :], W=["stgo%d" % b], N=["stgo%d" % b])
                POOL("tensor_copy", wout[:, k, :], stg[b], R=["stgo%d" % b], W=["wout"])
            lng = AR.alloc([1024], F32)
            lnb = AR.alloc([1024], F32)
            DMA(lng, C.lng_d, W=["lng"], N=["lng"])
            DMA(lnb, C.lnb_d, W=["lnb"], N=["lnb"])
            mixT = [AR.alloc([8, 128], BF16) for _ in range(2)]
            xt = [AR.alloc([1024], F32) for _ in range(2)]
            rr = [AR.alloc([1024], F32) for _ in range(2)]
            st = AR.alloc([2, 6], F32)
            mvv = AR.alloc([2], F32)
            rs_ = AR.alloc([1], F32)
            for tt in range(32):
                t0 = s0 + tt * 128
                j = tt % 2
                DMA(mixT[j], mix_d[:, :, t0:t0 + 128].rearrange("j p t -> p j t"), W=["mixT%d" % j], N=["mixT%d" % j])
                DMA(xt[j], C.x_d[t0:t0 + 128, :], W=["xt%d" % j], N=["xt%d" % j])
                for half in range(2):
                    pb = 2 * j + half
                    pn = "psD%d" % pb
                    Sd.buf(pn).new()
                    for e in range(8):
                        PE("matmul", PSB(pb), mixT[j][:, e, :], wout[:, e, half * 512:(half + 1) * 512],
                           start=(e == 0), stop=(e == 7), R=["mixT%d" % j, "wout"], W=[pn])
                rn = "rr%d" % j
                Sd.buf(rn).new()
                for half in range(2):
                    pb = 2 * j + half
                    DVE("scalar_tensor_tensor", rr[j][:, half * 512:(half + 1) * 512], xt[j][:, half * 512:(half + 1) * 512],
                        ALPHA, PSB(pb), ALU.mult, ALU.add, R=["xt%d" % j, "psD%d" % pb], W=[rn])
                Sd.buf("st").new()
                for half in range(2):
                    DVE("bn_stats", st[:, half, :], rr[j][:, half * 512:(half + 1) * 512], R=[rn], W=["st"])
                DVE("bn_aggr", mvv, st.rearrange("p a b -> p (a b)"), R=["st"], W=["mvv"], N=["mvv"])
                DVE("tensor_scalar", rs_, mvv[:, 1:2], LN_EPS, None, ALU.add, R=["mvv"], W=["rs"], N=["rs"])
                ACT("activation", rs_, rs_, AF.Sqrt, R=["rs"], W=["rs"], N=["rs"])
                DVE("reciprocal", rs_, rs_, R=["rs"], W=["rs"], N=["rs"])
                DVE("tensor_scalar", rr[j], rr[j], mvv[:, 0:1], rs_[:, 0:1], ALU.subtract, ALU.mult,
                    R=["mvv", "rs", rn], W=[rn], N=[rn])
                POOL("tensor_tensor", rr[j], rr[j], lng, ALU.mult, R=[rn, "lng"], W=[rn], N=[rn])
                POOL("tensor_tensor", rr[j], rr[j], lnb, ALU.add, R=[rn, "lnb"], W=[rn], N=[rn])
                DMA(C.out_d[t0:t0 + 128, :], rr[j], R=[rn])
            Sd.barrier()
            AR.release(m0)

        for l in range(NL):
            set_layer(l)
            layer_consts()
            if "C" in phases:
                s5_setup()
            for seq in range(NSEQ):
                if "A" in phases:
                    phase_A(seq)
                if "B" in phases:
                    phase_B(seq)
                if "C" in phases:
                    phase_C(seq)
                if "D" in phases:
                    phase_D(seq)
        Sd.barrier()
        Sd.cnt["sp"] += 1

        def fin(e):
            return e.nop()
        Sd._emit("sp", fin, {}, (Sd.sem["sp"], 1))

        with nc.Block() as block:
            @block.tensor
            def _(e):
                for th in Sd.thunks["pe"]:
                    th(e)

            @block.scalar
            def _(e):
                for th in Sd.thunks["act"]:
                    th(e)

            @block.vector
            def _(e):
                for th in Sd.thunks["dve"]:
                    th(e)

            @block.gpsimd
            def _(e):
                for th in Sd.thunks["pool"]:
                    th(e)

            @block.sync
            def _(e):
                for th in Sd.thunks["sp"]:
                    th(e)
    return nc


def _consts():
    ident = np.eye(128, dtype=np.float32)
    t = np.arange(128)
    caus = np.where(t[None, :] <= t[:, None], 0.0, NEG).astype(np.float32)
    jj = t // 16
    tri = (jj[None, :] >= jj[:, None]).astype(np.float32)
    dg = np.zeros((128, 8, 352), np.float32)
    for g in range(8):
        for r in range(16 * g, 16 * g + 16):
            dg[r, g, 112 + r] = 1.0
    powers = [-(j + 1) for j in range(8)] + [tau + 1 for tau in range(8)] + [7 - j for j in range(8)]
    mvals = np.tile(np.asarray(powers, np.float32)[None, :], (128, 1))
    kk = np.tile(np.arange(1, 513, dtype=np.float32)[None, :], (128, 1))
    return dict(ident=ident, caus=caus, tri=tri, dg=dg, mvals=mvals, kk=kk)


def _ptile(a):
    rest = a.shape[2:]
    a = a.reshape((16, 2, 64) + rest)
    perm = (1, 2, 0) + tuple(range(3, 3 + len(rest)))
    return np.ascontiguousarray(a.transpose(perm).reshape((128, 16) + rest))


def prep_layer(l, inp):
    f = np.float32
    w_in = np.asarray(inp["w_in"][l], f)
    zeros32 = np.zeros((D, 32), f)
    q = w_in[:, 0:512]
    ckv = w_in[:, 512:640]
    qidx = w_in[:, 640:896]
    kidx = w_in[:, 896:928]
    widx = w_in[:, 928:936]
    ga = w_in[:, 936:1448]
    u = w_in[:, 1448:1960]
    gs = w_in[:, 1960:2472]
    qh = [qidx[:, 32 * h:32 * h + 32] for h in range(8)]
    idxA = np.concatenate([qh[0], qh[1], qh[2], zeros32], 1)
    idxB = np.concatenate([qh[3], qh[4], qh[5], zeros32], 1)
    idxC = np.concatenate([qh[6], qh[7], zeros32, zeros32], 1)
    k0_ = np.concatenate([kidx, zeros32, zeros32, zeros32], 1)
    k1_ = np.concatenate([zeros32, kidx, zeros32, zeros32], 1)
    k2_ = np.concatenate([zeros32, zeros32, kidx, zeros32], 1)
    wfm = np.concatenate([q, idxA, idxB, idxC, k0_, k1_, k2_, ga, u, gs], 1)
    wtm = np.concatenate([ckv, widx, qidx], 1)
    tile_k = lambda m: np.ascontiguousarray(m.reshape(8, 128, m.shape[1]).transpose(1, 0, 2))
    d = {}
    d["wf"] = tile_k(wfm)
    d["wt"] = tile_k(wtm)
    w_uk = np.asarray(inp["w_uk"][l], f)
    w_uv = np.asarray(inp["w_uv"][l], f)
    wukz = np.zeros((128, 8, 128), f)
    for h_ in range(8):
        e_ = h_ % 2
        wukz[64 * e_:64 * e_ + 64, h_, :] = w_uk[h_].T
    d["wuk"] = wukz
    d["wuv"] = np.ascontiguousarray(w_uv.transpose(1, 0, 2))
    g = np.asarray(inp["kv_norm_g"][l], f)
    d["kvg_bc"] = np.ascontiguousarray(np.tile(g[None, :], (128, 1)))
    d["kvg_col"] = np.ascontiguousarray(g[:, None])
    d["ldt"] = _ptile(np.tile(np.asarray(inp["log_dt"][l], f)[:, None], (1, 64)))
    d["are"] = _ptile(np.asarray(inp["a_re"][l], f))
    d["aim"] = _ptile(np.asarray(inp["a_im"][l], f))
    d["bre"] = _ptile(np.asarray(inp["b_re"][l], f))
    d["bim"] = _ptile(np.asarray(inp["b_im"][l], f))
    d["cre"] = _ptile(np.asarray(inp["c_re"][l], f).transpose(0, 2, 1))
    d["cim"] = _ptile(np.asarray(inp["c_im"][l], f).transpose(0, 2, 1))
    dsk = np.asarray(inp["d_skip"][l], f)
    d["dsk"] = np.ascontiguousarray(np.tile(dsk.T, (8, 1)))
    d["wglu"] = np.ascontiguousarray(np.asarray(inp["w_glu"][l], f).reshape(4, 128, 1024).transpose(1, 0, 2))
    d["bglu"] = np.ascontiguousarray(np.asarray(inp["b_glu"][l], f).reshape(8, 128).T)
    d["wout"] = np.ascontiguousarray(np.asarray(inp["w_out"][l], f).reshape(8, 128, 1024).transpose(1, 0, 2))
    d["lng"] = np.ascontiguousarray(np.tile(np.asarray(inp["ln_g"][l], f)[None, :], (128, 1)))
    d["lnb"] = np.ascontiguousarray(np.tile(np.asarray(inp["ln_b"][l], f)[None, :], (128, 1)))
    d.update(_consts())
    return d


def kernel(**inputs):
    x = np.ascontiguousarray(np.asarray(inputs["x"], np.float32))
    B = x.shape[0]
    h = x.reshape(NCORES, T, D)
    per = [prep_layer(l, inputs) for l in range(DEPTH)]
    cn = _consts()
    w = {k: np.ascontiguousarray(np.stack([p[k] for p in per])) for k in per[0] if k not in cn}
    w.update(cn)
    in_maps = [dict(w, x=np.ascontiguousarray(h[c])) for c in range(NCORES)]
    nc = build_model()
    res = run_bass_kernel_spmd(nc, in_maps, core_ids=list(range(NCORES)))
    out = np.stack([np.asarray(r["out"], np.float32) for r in res.results])
    return out.reshape(B, S, D).astype(np.float32)
```
